# Optimizing a Trainium2 kernel written in Bass

```python
import jax, jax.numpy as jnp
from jax import lax
import numpy as np

D_MODEL = 1024
BATCH = 8
SEQ = 4096
DEPTH = 2
DEC_BATCH = 8
DEC_SEQ = 16
PAST_LEN = 2048

CHUNK = 64
Q_BLOCK = 128
HEAD_DIM = 64
FOX_HEADS = 4
SB_HEADS = 4
MLA_HEADS = 8
FOX_WIDTH = FOX_HEADS * HEAD_DIM
SB_WIDTH = SB_HEADS * HEAD_DIM
MLA_NOPE = 64
MLA_ROPE = 32
MLA_V = 64
MLA_WIDTH = MLA_HEADS * MLA_V
Q_LORA = 256
KV_LORA = 128
MIX_WIDTH = FOX_WIDTH + SB_WIDTH + MLA_WIDTH
IN_SPLITS = (FOX_WIDTH, FOX_WIDTH, FOX_WIDTH, FOX_HEADS, FOX_WIDTH,
             SB_WIDTH, SB_WIDTH, SB_WIDTH, SB_WIDTH,
             Q_LORA, KV_LORA, MLA_ROPE, MLA_WIDTH)
IN_WIDTH = sum(IN_SPLITS)
ROPE_THETA = 10000.0
EPS = 1e-6

kernel_name = 'hybrid_fox_sb_mla_stream_step'


def rmsnorm(x, g):
    xf = x.astype(jnp.float32)
    y = xf * lax.rsqrt(jnp.mean(xf * xf, axis=-1, keepdims=True) + EPS)
    return (y * g.astype(jnp.float32)).astype(x.dtype)


def rope(x, pos):
    half = MLA_ROPE // 2
    inv = ROPE_THETA ** (-jnp.arange(half, dtype=jnp.float32) / half)
    ang = pos.astype(jnp.float32)[:, None] * inv[None, :]
    cos = jnp.cos(ang)[None, :, None, :]
    sin = jnp.sin(ang)[None, :, None, :]
    xf = x.astype(jnp.float32)
    x1, x2 = xf[..., :half], xf[..., half:]
    return jnp.concatenate([x1 * cos - x2 * sin, x2 * cos + x1 * sin], axis=-1).astype(x.dtype)


def fox_attend(q, fq, k, v, fk, q_pos, k_pos):
    s = jnp.einsum('bqhd,bkhd->bhqk', q, k).astype(jnp.float32) * HEAD_DIM ** -0.5
    s = s + jnp.transpose(fq, (0, 2, 1))[..., None] - jnp.transpose(fk, (0, 2, 1))[:, :, None, :]
    mask = k_pos[None, :] <= q_pos[:, None]
    p = jax.nn.softmax(jnp.where(mask, s, -jnp.inf), axis=-1)
    return jnp.einsum('bhqk,bkhd->bqhd', p.astype(v.dtype), v)


def sb_attend(q, k, v, q_pos, k_pos):
    z = jnp.einsum('bqhd,bkhd->bhqk', q, k).astype(jnp.float32) * HEAD_DIM ** -0.5
    mask = k_pos[None, :] < q_pos[:, None]
    log_1m = jnp.where(mask, jax.nn.log_sigmoid(-z), 0.0)
    between = lax.cumsum(log_1m, axis=3, reverse=True) - log_1m
    a = jnp.where(mask, jnp.exp(jax.nn.log_sigmoid(z) + between), 0.0)
    return jnp.einsum('bhqk,bkhd->bqhd', a.astype(v.dtype), v)


def mla_attend(q_lat, q_rope, ckv, kpe, q_pos, k_pos):
    s = (jnp.einsum('bqhc,bkc->bhqk', q_lat, ckv)
         + jnp.einsum('bqhr,bkr->bhqk', q_rope, kpe)).astype(jnp.float32) * (MLA_NOPE + MLA_ROPE) ** -0.5
    mask = (k_pos[None, :] // CHUNK) <= (q_pos[:, None] // CHUNK)
    p = jax.nn.softmax(jnp.where(mask, s, -jnp.inf), axis=-1)
    return jnp.einsum('bhqk,bkc->bqhc', p.astype(ckv.dtype), ckv)


def sweep(attend, q_args, k_args, q_pos, k_pos):
    n_q = q_pos.shape[0]
    past = k_pos.shape[0] - n_q
    outs = []
    for start in range(0, n_q, Q_BLOCK):
        stop = min(start + Q_BLOCK, n_q)
        kend = past + stop
        outs.append(attend(*[a[:, start:stop] for a in q_args], *[a[:, :kend] for a in k_args],
                           q_pos[start:stop], k_pos[:kend]))
    return jnp.concatenate(outs, axis=1)


def layer(x, c, past, g_pre, g_post, w_ada, b_ada, w_in, b_f, g_q_a, w_uq, g_kv_a, w_uk, w_uv, w_out):
    b, s, _ = x.shape
    n_past = 0 if past is None else past[0].shape[1]
    q_pos = n_past + jnp.arange(s, dtype=jnp.int32)
    k_pos = jnp.arange(n_past + s, dtype=jnp.int32)
    mod = jax.nn.silu(c) @ w_ada + b_ada
    shift, scale, gate = jnp.split(mod, 3, axis=-1)
    h = rmsnorm(x, g_pre) * (1 + scale[:, None]) + shift[:, None]
    parts = jnp.split(h @ w_in, np.cumsum(IN_SPLITS)[:-1].tolist(), axis=-1)
    fq, fk, fv, ff, fg, sq, sk, sv, sg, cq, ckv, kpe, mg = parts
    fq = fq.reshape(b, s, FOX_HEADS, HEAD_DIM)
    fk = fk.reshape(b, s, FOX_HEADS, HEAD_DIM)
    fv = fv.reshape(b, s, FOX_HEADS, HEAD_DIM)
    logf = jax.nn.log_sigmoid((ff + b_f).astype(jnp.float32))
    sq = sq.reshape(b, s, SB_HEADS, HEAD_DIM)
    sk = sk.reshape(b, s, SB_HEADS, HEAD_DIM)
    sv = sv.reshape(b, s, SB_HEADS, HEAD_DIM)
    q = (rmsnorm(cq, g_q_a) @ w_uq).reshape(b, s, MLA_HEADS, MLA_NOPE + MLA_ROPE)
    q_rope = rope(q[..., MLA_NOPE:], q_pos)
    q_lat = jnp.einsum('bshn,chn->bshc', q[..., :MLA_NOPE], w_uk)
    ckv = rmsnorm(ckv, g_kv_a)
    kpe = rope(kpe[:, :, None, :], q_pos)[:, :, 0]
    new = (fk, fv, logf, sk, sv, ckv, kpe)
    if past is None:
        full = new
    else:
        full = tuple(jnp.concatenate([p_, n_], axis=1) for p_, n_ in zip(past, new))
    k_fk, k_fv, k_logf, k_sk, k_sv, k_ckv, k_kpe = full
    F = jnp.cumsum(k_logf.astype(jnp.float32), axis=1)
    o_fox = sweep(fox_attend, (fq, F[:, n_past:]), (k_fk, k_fv, F), q_pos, k_pos)
    o_sb = sweep(sb_attend, (sq,), (k_sk, k_sv), q_pos, k_pos)
    o_lat = sweep(mla_attend, (q_lat, q_rope), (k_ckv, k_kpe), q_pos, k_pos)
    o_mla = jnp.einsum('bshc,chv->bshv', o_lat, w_uv)
    y = jnp.concatenate([o_fox.reshape(b, s, FOX_WIDTH) * jax.nn.silu(fg),
                         o_sb.reshape(b, s, SB_WIDTH) * jax.nn.silu(sg),
                         o_mla.reshape(b, s, MLA_WIDTH) * jax.nn.silu(mg)], axis=-1) @ w_out
    x = x + gate[:, None] * rmsnorm(y, g_post)
    return x, new


def setup_inputs(seed: int = 0) -> dict:
    key = jax.random.key(seed)
    ks = jax.random.split(key, 32)
    nrm = jax.random.normal
    f32 = jnp.float32
    d = D_MODEL
    return {
        'x_prompt': nrm(ks[0], (BATCH, SEQ, d), f32),
        'x_sample': nrm(ks[1], (DEC_BATCH, DEC_SEQ, d), f32),
        'c_prompt': nrm(ks[2], (BATCH, d), f32),
        'c_sample': nrm(ks[3], (DEC_BATCH, d), f32),
        'cache_fox_k': nrm(ks[4], (DEPTH, DEC_BATCH, PAST_LEN, FOX_HEADS, HEAD_DIM), f32),
        'cache_fox_v': nrm(ks[5], (DEPTH, DEC_BATCH, PAST_LEN, FOX_HEADS, HEAD_DIM), f32),
        'cache_fox_logf': jax.nn.log_sigmoid(nrm(ks[6], (DEPTH, DEC_BATCH, PAST_LEN, FOX_HEADS), f32)),
        'cache_sb_k': nrm(ks[7], (DEPTH, DEC_BATCH, PAST_LEN, SB_HEADS, HEAD_DIM), f32),
        'cache_sb_v': nrm(ks[8], (DEPTH, DEC_BATCH, PAST_LEN, SB_HEADS, HEAD_DIM), f32),
        'cache_mla_ckv': nrm(ks[9], (DEPTH, DEC_BATCH, PAST_LEN, KV_LORA), f32),
        'cache_mla_kpe': nrm(ks[10], (DEPTH, DEC_BATCH, PAST_LEN, MLA_ROPE), f32),
        'g_pre': 1.0 + 0.05 * nrm(ks[11], (DEPTH, d), f32),
        'g_post': 1.0 + 0.05 * nrm(ks[12], (DEPTH, d), f32),
        'w_ada': 0.5 * nrm(ks[13], (DEPTH, d, 3 * d), f32) * d ** -0.5,
        'b_ada': 0.01 * nrm(ks[14], (DEPTH, 3 * d), f32),
        'w_in': nrm(ks[15], (DEPTH, d, IN_WIDTH), f32) * d ** -0.5,
        'b_f': 0.5 * nrm(ks[16], (DEPTH, FOX_HEADS), f32),
        'g_q_a': 1.0 + 0.05 * nrm(ks[17], (DEPTH, Q_LORA), f32),
        'w_uq': nrm(ks[18], (DEPTH, Q_LORA, MLA_HEADS * (MLA_NOPE + MLA_ROPE)), f32) * Q_LORA ** -0.5,
        'g_kv_a': 1.0 + 0.05 * nrm(ks[19], (DEPTH, KV_LORA), f32),
        'w_uk': nrm(ks[20], (DEPTH, KV_LORA, MLA_HEADS, MLA_NOPE), f32) * KV_LORA ** -0.5,
        'w_uv': nrm(ks[21], (DEPTH, KV_LORA, MLA_HEADS, MLA_V), f32) * KV_LORA ** -0.5,
        'w_out': nrm(ks[22], (DEPTH, MIX_WIDTH, d), f32) * MIX_WIDTH ** -0.5,
    }


def reference(x_prompt, x_sample, c_prompt, c_sample, cache_fox_k, cache_fox_v, cache_fox_logf,
              cache_sb_k, cache_sb_v, cache_mla_ckv, cache_mla_kpe,
              g_pre, g_post, w_ada, b_ada, w_in, b_f, g_q_a, w_uq, g_kv_a, w_uk, w_uv, w_out):
    y_prompt = x_prompt
    y_sample = x_sample
    rows_p = []
    rows_s = []
    for l in range(DEPTH):
        w = (g_pre[l], g_post[l], w_ada[l], b_ada[l], w_in[l], b_f[l], g_q_a[l], w_uq[l],
             g_kv_a[l], w_uk[l], w_uv[l], w_out[l])
        y_prompt, new_p = layer(y_prompt, c_prompt, None, *w)
        past = (cache_fox_k[l], cache_fox_v[l], cache_fox_logf[l], cache_sb_k[l], cache_sb_v[l],
                cache_mla_ckv[l], cache_mla_kpe[l])
        y_sample, new_s = layer(y_sample, c_sample, past, *w)
        rows_p.append(new_p)
        rows_s.append(new_s)
    p_fox_k, p_fox_v, p_fox_logf, p_sb_k, p_sb_v, p_mla_ckv, p_mla_kpe = [jnp.stack(t) for t in zip(*rows_p)]
    s_fox_k, s_fox_v, s_fox_logf, s_sb_k, s_sb_v, s_mla_ckv, s_mla_kpe = [jnp.stack(t) for t in zip(*rows_s)]
    return (y_prompt, y_sample,
            p_fox_k, p_fox_v, p_fox_logf, p_sb_k, p_sb_v, p_mla_ckv, p_mla_kpe,
            s_fox_k, s_fox_v, s_fox_logf, s_sb_k, s_sb_v, s_mla_ckv, s_mla_kpe)
```

```python
from contextlib import ExitStack
import numpy as np
import concourse.bass as bass
import concourse.mybir as mybir
from concourse.bass_utils import run_bass_kernel_spmd

F32 = mybir.dt.float32
BF16 = mybir.dt.bfloat16
ALU = mybir.AluOpType
AF = mybir.ActivationFunctionType
AX = mybir.AxisListType

N_DMA_SEMS = 56
SEM_EPOCH = 30000

S = 4096
D = 1024
SS = 16
PAST = 2048
TA = S + SS
DEPTH = 2
C_FQ, C_FK, C_FV, C_FF, C_FG = 0, 256, 512, 768, 772
C_SQ, C_SK, C_SV, C_SG = 1028, 1284, 1540, 1796
C_CQ, C_CKV, C_KPE, C_MG = 2052, 2308, 2436, 2468
INW = 2980
NEG = -240000.0
EPS = 1e-6
MLA_SCALE = 96.0 ** -0.5


class Prog:
    ENGS = ("pe", "act", "dve", "pool", "sp")

    def __init__(self, nc, stack):
        self.nc = nc
        self.stack = stack
        self.streams = {e: [] for e in self.ENGS}
        self.sems = {}
        self.cur_sem = {}
        self.count = {}
        self.epoch = {e: 0 for e in self.ENGS}
        for e in self.ENGS:
            self._new_epoch(e)
        self.waited = {e: {} for e in self.ENGS}
        self.res = {}
        self.dma_slots = []
        for i in range(N_DMA_SEMS):
            k = ("dma", i)
            self.sems[k] = stack.enter_context(nc.semaphore(f"dma{i}"))
            self.count[k] = 0
            self.dma_slots.append(k)
        self.dma_rr = 0
        self.n_ops = 0

    def _new_epoch(self, e):
        k = (e, self.epoch[e])
        self.sems[k] = self.stack.enter_context(self.nc.semaphore(f"s_{e}_{self.epoch[e]}"))
        self.count[k] = 0
        self.cur_sem[e] = k
        self.epoch[e] += 1

    def _deps(self, reads, writes):
        ev = []
        for r in reads:
            st = self.res.get(r)
            if st and st[0] is not None:
                ev.append(st[0])
        for w in writes:
            st = self.res.get(w)
            if st:
                if st[0] is not None:
                    ev.append(st[0])
                ev.extend(st[1])
        return ev

    def _record(self, reads, writes, event):
        for r in reads:
            st = self.res.setdefault(r, [None, []])
            st[1].append(event)
            if len(st[1]) > 48:
                best = {}
                for (k, v) in st[1]:
                    if best.get(k, -1) < v:
                        best[k] = v
                st[1] = list(best.items())
        for w in writes:
            self.res[w] = [event, []]

    def _waits_for(self, eng, events, skip_self):
        need = {}
        for (k, v) in events:
            if skip_self and k[0] == eng:
                continue
            if need.get(k, 0) < v:
                need[k] = v
        out = []
        wd = self.waited[eng]
        for k, v in need.items():
            if wd.get(k, 0) >= v:
                continue
            wd[k] = v
            out.append((self.sems[k], v))
        return out

    def op(self, eng, fn, reads=(), writes=(), signal=True):
        self.n_ops += 1
        pr = [r for r in reads if r.startswith("ps")]
        if pr:
            reads = [r for r in reads if not r.startswith("ps")]
            writes = list(writes) + [p for p in pr if p not in writes]
        events = self._deps(reads, writes)
        waits = self._waits_for(eng, events, skip_self=(eng == "pe"))
        k = self.cur_sem[eng]
        if signal:
            self.count[k] += 1
            ev = (k, self.count[k])
            if self.count[k] >= SEM_EPOCH:
                self._new_epoch(eng)
        else:
            ev = (k, self.count[k] + 1)
        sem = self.sems[k]

        def run(e, waits=waits, fn=fn, sem=sem, signal=signal):
            for (s, v) in waits:
                e.wait_ge(s, v)
            ins = fn(e)
            if signal:
                ins.then_inc(sem, 1)
        self.streams[eng].append(run)
        self._record(reads, writes, ev)
        return ev

    def dma(self, out, in_, reads=(), writes=(), queue="sp", **kw):
        self.n_ops += 1
        slot = self.dma_slots[self.dma_rr % N_DMA_SEMS]
        self.dma_rr += 1
        events = self._deps(reads, writes)
        if self.count[slot] > 0:
            events.append((slot, self.count[slot]))
        waits = self._waits_for(queue, events, skip_self=False)
        self.count[slot] += 16
        ev = (slot, self.count[slot])
        sem = self.sems[slot]

        def run(e, waits=waits, sem=sem, out=out, in_=in_, kw=kw):
            for (s, v) in waits:
                e.wait_ge(s, v)
            e.dma_start(out=out, in_=in_, **kw).then_inc(sem, 16)
        self.streams[queue].append(run)
        self._record(reads, writes, ev)
        return ev

    def barrier(self):
        events = [(k, c) for k, c in self.count.items() if c > 0]
        for e in self.ENGS:
            waits = self._waits_for(e, events, skip_self=True)

            def run(eo, waits=waits):
                for (s, v) in waits:
                    eo.wait_ge(s, v)
            self.streams[e].append(run)
        self.res = {}

    def finish(self):
        events = [(k, c) for k, c in self.count.items() if c > 0]
        waits = self._waits_for("sp", events, skip_self=True)

        def run(eo, waits=waits):
            for (s, v) in waits:
                eo.wait_ge(s, v)
        self.streams["sp"].append(run)

    def emit(self):
        nc = self.nc
        streams = self.streams
        self.streams = {e: [] for e in self.ENGS}
        with nc.Block() as block:
            @block.tensor
            def _(e):
                for f in streams["pe"]:
                    f(e)

            @block.scalar
            def _(e):
                for f in streams["act"]:
                    f(e)

            @block.vector
            def _(e):
                for f in streams["dve"]:
                    f(e)

            @block.gpsimd
            def _(e):
                for f in streams["pool"]:
                    f(e)

            @block.sync
            def _(e):
                for f in streams["sp"]:
                    f(e)


class StopBuild(Exception):
    pass


STOP = [None]
PLIM = [9]


def ckpt(name):
    if STOP[0] == name:
        raise StopBuild()


class Rot:
    def __init__(self, items):
        self.items = items
        self.i = 0

    def next(self):
        it = self.items[self.i % len(self.items)]
        self.i += 1
        return it


def build_program(depth=DEPTH, do_b=True, do_c=True):
    nc = bass.Bass("TRN2", target_bir_lowering=False)

    def din(name, shape, dt=F32):
        return nc.dram_tensor(name, shape, dt, kind="ExternalInput").ap()

    def dout(name, shape):
        return nc.dram_tensor(name, shape, F32, kind="ExternalOutput").ap()

    def dscr(name, shape, dt=BF16):
        return nc.dram_tensor(name, shape, dt, kind="Internal").ap()

    x_p = din("x_p", [S, D]); x_s = din("x_s", [SS, D]); c2 = din("c2", [2, D])
    cfk = din("cfk", [DEPTH, PAST, 256]); cfv = din("cfv", [DEPTH, PAST, 256]); cfl = din("cfl", [DEPTH, PAST, 4])
    csk = din("csk", [DEPTH, PAST, 256]); csv = din("csv", [DEPTH, PAST, 256])
    cck = din("cck", [DEPTH, PAST, 128]); ckp = din("ckp", [DEPTH, PAST, 32])
    g_pre = din("g_pre", [DEPTH, D]); g_post = din("g_post", [DEPTH, D])
    w_ada = din("w_ada", [DEPTH, D, 3 * D]); b_ada = din("b_ada", [DEPTH, 3 * D])
    w_in = din("w_in", [DEPTH, D, INW]); b_f = din("b_f", [DEPTH, 4]); g_q_a = din("g_q_a", [DEPTH, 256])
    w_uq = din("w_uq", [DEPTH, 256, 768]); w_uqs = din("w_uqs", [DEPTH, 256, 768])
    g_kv_a = din("g_kv_a", [DEPTH, 128]); w_uk = din("w_uk", [DEPTH, 128, 512]); w_uv = din("w_uv", [DEPTH, 128, 512])
    w_out = din("w_out", [DEPTH, D, D])
    consts = din("consts", [128, 512 + 3 * 896]); selc = din("selc", [2, 256])
    cs_tok = din("cs_tok", [S + 128, 32]); ropeT = din("ropeT", [2, 32, S])

    y_p = dout("y_p", [S, D]); y_s = dout("y_s", [SS, D])
    o_fk = [dout("p_fk", [DEPTH, S, 256]), dout("s_fk", [DEPTH, SS, 256])]
    o_fv = [dout("p_fv", [DEPTH, S, 256]), dout("s_fv", [DEPTH, SS, 256])]
    o_fl = [dout("p_fl", [DEPTH, S, 4]), dout("s_fl", [DEPTH, SS, 4])]
    o_sk = [dout("p_sk", [DEPTH, S, 256]), dout("s_sk", [DEPTH, SS, 256])]
    o_sv = [dout("p_sv", [DEPTH, S, 256]), dout("s_sv", [DEPTH, SS, 256])]
    o_ck = [dout("p_ck", [DEPTH, S, 128]), dout("s_ck", [DEPTH, SS, 128])]
    o_kp = [dout("p_kp", [DEPTH, S, 32]), dout("s_kp", [DEPTH, SS, 32])]

    QF = dscr("QF", [256, TA]); KF = dscr("KF", [256, TA]); GF = dscr("GF", [256, TA]); FQ = dscr("FQ", [4, TA])
    VF = dscr("VF", [TA + 112, 4, 128])
    QB = dscr("QB", [256, TA]); KB = dscr("KB", [256, TA]); GB = dscr("GB", [256, TA]); VB = dscr("VB", [TA + 112, 256])
    QM = dscr("QM", [8, 96, TA]); KM = dscr("KM", [512, TA]); KP = dscr("KP", [32, TA]); GM = dscr("GM", [512, TA])
    VM = dscr("VM", [TA + 112, 8, 128])
    YT = dscr("YT", [D, TA])
    x1 = dscr("x1", [S, D], F32)

    with ExitStack() as gst:
        P = Prog(nc, gst)

        uid = [0]

        def sbt(st, name, shape, dt):
            uid[0] += 1
            return st.enter_context(nc.sbuf_tensor(f"{name}_u{uid[0]}", shape, dt))

        ps = [gst.enter_context(nc.psum_tensor(f"ps{i}", [128, 512], F32)) for i in range(8)]
        psk = [f"ps{i}" for i in range(8)]
        jrot = Rot([0, 1, 2, 3])

        def MM(out, lhsT, rhs, start, stop, r, w, sig=None):
            if sig is None:
                sig = stop
            P.op("pe", lambda e: e.matmul(out, lhsT=lhsT, rhs=rhs, start=start, stop=stop), reads=r, writes=w, signal=sig)

        def TR(out, in_, ident, r, w):
            P.op("pe", lambda e: e.transpose(out, in_, ident), reads=r, writes=w)

        def ACT(out, in_, func, r, w, bias=None, scale=None, accum=None):
            kw = {}
            if bias is not None:
                kw["bias"] = bias
            if scale is not None:
                kw["scale"] = scale
            if accum is not None:
                kw["accum_out"] = accum
            P.op("act", lambda e: e.activation(out=out, in_=in_, func=func, **kw), reads=r, writes=w)

        def TT(eng, out, in0, in1, op, r, w):
            P.op(eng, lambda e: e.tensor_tensor(out=out, in0=in0, in1=in1, op=op), reads=r, writes=w)

        def TS(eng, out, in0, s1, s2, op0, op1, r, w):
            if op1 is None:
                P.op(eng, lambda e: e.tensor_scalar(out=out, in0=in0, scalar1=s1, scalar2=None, op0=op0), reads=r, writes=w)
            else:
                P.op(eng, lambda e: e.tensor_scalar(out=out, in0=in0, scalar1=s1, scalar2=s2, op0=op0, op1=op1), reads=r, writes=w)

        def STT(out, in0, scalar, in1, op0, op1, r, w):
            P.op("dve", lambda e: e.scalar_tensor_tensor(out=out, in0=in0, scalar=scalar, in1=in1, op0=op0, op1=op1), reads=r, writes=w)

        def CP(eng, out, in_, r, w):
            if eng == "act":
                P.op("act", lambda e: e.copy(out=out, in_=in_), reads=r, writes=w)
            else:
                P.op(eng, lambda e: e.tensor_copy(out=out, in_=in_), reads=r, writes=w)

        def MEMSET(eng, ap, val, w):
            P.op(eng, lambda e: e.memset(ap, val), writes=w)

        evrot = Rot(["act", "dve"])

        cst = sbt(gst, "cst", [128, 512], F32)
        cstb = sbt(gst, "cstb", [128, 384], BF16)
        mk = sbt(gst, "mk", [128, 3 * 896], BF16)
        selt = sbt(gst, "selt", [2, 256], F32)
        zt = sbt(gst, "zt", [128, 128], F32)
        zb = sbt(gst, "zb", [128, 128], BF16)
        epst = sbt(gst, "epst", [128, 1], F32)
        P.dma(cst[:], consts[:, 0:512], writes=["cst"])
        P.dma(selt[:], selc, writes=["selt"])
        with ExitStack() as st0:
            mkf = sbt(st0, "mkf", [128, 3 * 896], F32)
            P.dma(mkf[:], consts[:, 512:512 + 3 * 896], writes=["mkf"])
            CP("dve", mk[:], mkf[:], ["mkf"], ["mk"])
            CP("pool", cstb[:, 0:128], cst[:, 0:128], ["cst"], ["cstb"])
            CP("pool", cstb[:, 128:256], cst[:, 384:512], ["cst"], ["cstb"])
            TS("dve", cstb[:, 256:384], cst[:, 256:384], -1.0, None, ALU.mult, None, ["cst"], ["cstb"])
            MEMSET("pool", zt[:], 0.0, ["zt"])
            MEMSET("pool", zb[:], 0.0, ["zb"])
            MEMSET("pool", epst[:], EPS, ["epst"])
            P.barrier()
            P.emit()
        ident = cst[:, 0:128]
        triF = cst[:, 128:256]
        onesF = cst[:, 256:384]
        identb = cstb[:, 0:128]
        tinclneg = cstb[:, 128:256]
        onesneg = cstb[:, 256:384]

        xs = sbt(gst, "xs", [SS, D], F32)
        P.dma(xs[:], x_s, writes=["xs"])
        negF = sbt(gst, "negF", [128, 34, 4], F32)
        negFp = sbt(gst, "negFp", [128, 16, 4], F32)
        prmT = sbt(gst, "prmT", [128, 42], F32)
        modS = sbt(gst, "modS", [128, 2, 16], F32)
        aPS = sbt(gst, "aPS", [128, 2, 8], F32)
        ggP = sbt(gst, "ggP", [128, D], F32)
        ggS = sbt(gst, "ggS", [SS, D], F32)
        bfb = sbt(gst, "bfb", [128, 4, 4], F32)
        gkvb = sbt(gst, "gkvb", [128, 128], F32)
        wuk = sbt(gst, "wuk", [128, 512], BF16)
        wuv = sbt(gst, "wuv", [128, 512], BF16)

        def rstd_from_ssq(out, ssq, n, rows, rk, wk, tmp):
            ACT(tmp, ssq, AF.Ln, rk, [wk + "_t"], bias=epst[0:rows, :], scale=1.0 / n)
            ACT(out, tmp, AF.Exp, [wk + "_t"], [wk], scale=-0.5)

        def alloc_weights(wst):
            return dict(win=sbt(wst, "win", [128, 8, INW], BF16), wuq=sbt(wst, "wuq", [128, 2, 768], BF16),
                        wuqs=sbt(wst, "wuqs", [128, 2, 768], BF16), stg=[sbt(wst, f"stg{i}", [128, 2048], F32) for i in range(2)])

        def load_weights(l, wd, engines):
            win, wuq, wuqs, stg = wd["win"], wd["wuq"], wd["wuqs"], wd["stg"]
            srot = Rot([0, 1])
            castrot = Rot(list(engines))
            wv = w_in[l].rearrange("(k p) c -> p k c", p=128)
            for c0 in range(0, INW, 256):
                cw = min(256, INW - c0)
                si = srot.next()
                sv = stg[si][:, 0:8 * cw].rearrange("p (k c) -> p k c", k=8)
                P.dma(sv, wv[:, :, c0:c0 + cw], writes=[f"stg{si}"], queue="pool")
                CP(castrot.next(), win[:, :, c0:c0 + cw], sv, [f"stg{si}"], ["win"])
                yield
            for (dst, src_, nm) in ((wuq, w_uq, "wuq"), (wuqs, w_uqs, "wuqs")):
                si = srot.next()
                sv = stg[si][:, 0:1536].rearrange("p (k c) -> p k c", k=2)
                P.dma(sv, src_[l].rearrange("(k p) c -> p k c", p=128), writes=[f"stg{si}"], queue="pool")
                CP(castrot.next(), dst[:], sv, [f"stg{si}"], [nm])
                yield
            for (dst, src_, nm) in ((wuk, w_uk, "wuk"), (wuv, w_uv, "wuv")):
                si = srot.next()
                P.dma(stg[si][:, 0:512], src_[l], writes=[f"stg{si}"], queue="pool")
                CP(castrot.next(), dst[:], stg[si][:, 0:512], [f"stg{si}"], [nm])
                yield

        wst_cur = [ExitStack()]
        cur_w = [alloc_weights(wst_cur[0])]
        for _ in load_weights(0, cur_w[0], ("pool", "dve")):
            pass
        wgen = [None]

        for l in range(depth):
          try:
            xin = x_p if l == 0 else x1
            xout = y_p if l == depth - 1 else x1

            ckpt("setup")
            with ExitStack() as st:
                prm = sbt(st, "prm", [42, 128], F32)
                t16 = sbt(st, "t16", [16, 128], F32)
                t16b = sbt(st, "t16b", [16, 128], F32)
                wa = [sbt(st, f"wa{i}", [128, 3 * D], F32) for i in range(2)]
                grow = sbt(st, "grow", [2, D], F32)
                badag = sbt(st, "badag", [2, D], F32)
                gpb = sbt(st, "gpb", [128, D], F32)
                P.dma(prm[0:16, :], c2.rearrange("j (k p) -> (j k) p", p=128), writes=["prm"])
                P.dma(prm[16:24, :], g_pre[l].rearrange("(k p) -> k p", p=128), writes=["prm"])
                P.dma(prm[24:40, :], b_ada[l, 0:2048].rearrange("(k p) -> k p", p=128), writes=["prm"])
                P.dma(prm[40:42, :], g_q_a[l].rearrange("(k p) -> k p", p=128), writes=["prm"])
                P.dma(badag[:], b_ada[l, 2048:3072].partition_broadcast(2), writes=["badag"])
                P.dma(gpb[:], g_post[l].partition_broadcast(128), writes=["gpb"])
                for j4 in range(4):
                    P.dma(bfb[:, j4, :], b_f[l].partition_broadcast(128), writes=["bfb"])
                P.dma(gkvb[:], g_kv_a[l].partition_broadcast(128), writes=["gkvb"])
                ACT(t16[:], prm[0:16, :], AF.Exp, ["prm"], ["t16"], scale=-1.0)
                ACT(t16b[:], t16[:], AF.Ln, ["t16"], ["t16b"], bias=1.0)
                ACT(t16[:], t16b[:], AF.Exp, ["t16b"], ["t16"], scale=-1.0)
                TT("dve", prm[0:16, :], t16[:], prm[0:16, :], ALU.mult, ["t16", "prm"], ["prm"])
                TR(ps[6][:, 0:42], prm[0:42, :], cst[0:42, 0:42], ["prm", "cst"], ["ps6"])
                CP("dve", prmT[:], ps[6][:, 0:42], ["ps6"], ["prmT"])
                MM(ps[7][:, 0:32], zt[:], zt[:, 0:32], True, False, ["zt"], ["ps7"], sig=False)
                for k in range(8):
                    w_t = wa[k % 2]
                    wk = f"wa{k % 2}"
                    P.dma(w_t[:], w_ada[l, k * 128:(k + 1) * 128, :], writes=[wk])
                    sck = prmT[:, k:16:8]
                    for fc in range(16):
                        MM(ps[7][:, 2 * fc:2 * fc + 2], w_t[:, fc * 128:(fc + 1) * 128], sck, False, (k == 7 and fc == 15),
                           [wk, "prmT"], ["ps7"], sig=(fc == 15))
                    MM(ps[4][0:2, :], sck, w_t[:, 2048:2560], k == 0, k == 7, [wk, "prmT"], ["ps4"], sig=True)
                    MM(ps[5][0:2, :], sck, w_t[:, 2560:3072], k == 0, k == 7, [wk, "prmT"], ["ps5"], sig=True)
                for j in range(2):
                    TT("dve", modS[:, j, :], ps[7][:, j:32:2], prmT[:, 24:40], ALU.add, ["ps7", "prmT"], ["modS"])
                    STT(aPS[:, j, :], modS[:, j, 8:16], 1.0, prmT[:, 16:24], ALU.add, ALU.mult, ["modS", "prmT"], ["aPS"])
                TT("dve", grow[:, 0:512], ps[4][0:2, :], badag[:, 0:512], ALU.add, ["ps4", "badag"], ["grow"])
                TT("dve", grow[:, 512:1024], ps[5][0:2, :], badag[:, 512:1024], ALU.add, ["ps5", "badag"], ["grow"])
                for hf in range(2):
                    MM(ps[0 + hf][:, :], selt[0:2, 0:128], grow[0:2, hf * 512:(hf + 1) * 512], True, True, ["selt", "grow"], [psk[hf]])
                    TT("dve", ggP[:, hf * 512:(hf + 1) * 512], ps[hf][:, :], gpb[:, hf * 512:(hf + 1) * 512], ALU.mult,
                       [psk[hf], "gpb"], ["ggP"])
                    MM(ps[2 + hf][0:SS, :], selt[0:2, 128:128 + SS], grow[0:2, hf * 512:(hf + 1) * 512], True, True,
                       ["selt", "grow"], [psk[2 + hf]])
                    TT("dve", ggS[:, hf * 512:(hf + 1) * 512], ps[2 + hf][0:SS, :], gpb[0:SS, hf * 512:(hf + 1) * 512], ALU.mult,
                       [psk[2 + hf], "gpb"], ["ggS"])
                P.barrier()
                P.emit()

            ckpt("adaln")
            with ExitStack() as st:
                win, wuq, wuqs = cur_w[0]["win"], cur_w[0]["wuq"], cur_w[0]["wuqs"]
                cstok = sbt(st, "cstok", [128, 33, 32], F32)
                for q4 in range(4):
                    P.dma(cstok[:, q4 * 8:(q4 + 1) * 8, :],
                          cs_tok[q4 * 1024:(q4 + 1) * 1024, :].rearrange("(t p) c -> p t c", p=128), writes=["cstok"])
                P.dma(cstok[:, 32, :], cs_tok[PAST:PAST + 128, :], writes=["cstok"])

                xts = [sbt(st, f"xt{i}", [128, D], F32) for i in range(2)]
                xns = [sbt(st, f"xn{i}", [128, D], BF16) for i in range(4)]
                sqj = sbt(st, "sqj", [128, D], BF16)
                stat = sbt(st, "stat", [128, 16], F32)
                HTg = [sbt(st, f"HTg{i}", [128, 8, 512], BF16) for i in range(2)]
                stb = [sbt(st, f"stb{i}", [128, 512], BF16) for i in range(6)]
                stbrot = Rot([0, 1, 2, 3, 4, 5])
                kvst = [sbt(st, f"kvst{i}", [128, 512], F32) for i in range(2)]
                kvrot = Rot([0, 1])
                vst = [sbt(st, f"vst{i}", [128, 8, 128], BF16) for i in range(2)]
                vrot = Rot([0, 1])
                for i in range(2):
                    MEMSET("pool", vst[i][:, :, 64:128], 2.0, [f"vst{i}"])
                ckst = [sbt(st, f"ckst{i}", [128, 160], F32) for i in range(4)]
                ckrot = Rot([0, 1])
                rt = [sbt(st, f"rt{i}", [128, 32], F32) for i in range(4)]
                lf = sbt(st, "lf", [128, 8, 4], F32)
                Rrun = sbt(st, "Rrun", [128, 4], F32)
                f8 = sbt(st, "f8", [128, 4, 4], F32)
                fqs = sbt(st, "fqs", [4, 512], BF16)
                ckTg = sbt(st, "ckTg", [128, 512], BF16)
                kpTg = sbt(st, "kpTg", [32, 512], BF16)
                cqT = sbt(st, "cqT", [128, 2, 512], F32)
                sqb = sbt(st, "sqb", [128, 2, 512], BF16)
                cqg = sbt(st, "cqg", [128, 2, 512], BF16)
                rsb = sbt(st, "rsb", [128, 512], F32)
                rsbt = sbt(st, "rsbt", [128, 512], F32)
                ropet = sbt(st, "ropet", [128, 2, 512], F32)
                tq1 = [sbt(st, f"tq1_{i}", [128, 512], F32) for i in range(2)]
                tq2 = [sbt(st, f"tq2_{i}", [128, 512], F32) for i in range(2)]
                tqrot = Rot([0, 1])
                qhrot = Rot([0, 1, 2, 3, 4, 5, 6])
                g1s = [sbt(st, f"g1_{i}", [128, 512], F32) for i in range(2)]
                g2s = [sbt(st, f"g2_{i}", [128, 512], F32) for i in range(2)]
                grot = Rot([0, 1])
                lfp = sbt(st, "lfp", [128, 16, 4], F32)
                Rp = sbt(st, "Rp", [128, 17, 4], F32)
                qst = [sbt(st, f"qst{i}", [96, 512], BF16) for i in range(3)]
                qrot = Rot([0, 1, 2])
                xrot = Rot([0, 1])

                class PG:
                    def __init__(self, gi, W, tiles, sample):
                        self.gi, self.W, self.tiles, self.sample = gi, W, tiles, sample
                        self.si = 1 if sample else 0
                        self.c0 = S if sample else gi * 512
                        self.pos0 = PAST if sample else gi * 512
                        self.HT = HTg[gi % 2]
                        self.hk = f"HTg{gi % 2}"
                        self.aP = aPS[:, self.si, :]
                        self.bP = modS[:, self.si, 0:8]
                        self.xt = []

                def put(dst_ap, bf_ap, key):
                    P.dma(dst_ap, bf_ap, reads=[key], writes=[])

                def P_A1(g):
                    for i, (r0, nr) in enumerate(g.tiles):
                        if g.sample:
                            xt, xk = xs, "xs"
                        else:
                            xi = xrot.next()
                            xt, xk = xts[xi], f"xt{xi}"
                            P.dma(xt[:], xin[r0:r0 + nr, :], reads=[f"x1_{r0 // 128}"] if l > 0 else [], writes=[xk])
                        ACT(sqj[0:nr, :], xt[0:nr, :], AF.Square, [xk], ["sqj"], accum=stat[0:nr, i:i + 1])
                        rstd_from_ssq(stat[0:nr, 8 + i:9 + i], stat[0:nr, i:i + 1], D, nr, ["sqj"], f"stat{i}", stat[0:nr, 4 + i:5 + i])
                        TS("dve", xns[i][0:nr, :], xt[0:nr, :], stat[0:nr, 8 + i:9 + i], None, ALU.mult, None, [xk, f"stat{i}"], [f"xn{i}"])

                def P_A2(g):
                    W, HT, hk = g.W, g.HT, g.hk
                    for k in range(8):
                        b = jrot.next()
                        psb = ps[b][:].bitcast(BF16)
                        for i, (r0, nr) in enumerate(g.tiles):
                            TR(psb[:, i * 128:i * 128 + nr], xns[i][0:nr, k * 128:(k + 1) * 128], cstb[0:nr, 0:nr],
                               [f"xn{i}", "cstb"], [psk[b]])
                        if k % 2 == 0:
                            ACT(HT[:, k, 0:W], psb[:, 0:W], AF.Identity, [psk[b], "aPS", "modS"], [hk],
                                bias=g.bP[:, k:k + 1], scale=g.aP[:, k:k + 1])
                        else:
                            TS("dve", HT[:, k, 0:W], psb[:, 0:W], g.aP[:, k:k + 1], g.bP[:, k:k + 1], ALU.mult, ALU.add,
                               [psk[b], "aPS", "modS"], [hk])

                jrotB = Rot([0, 1, 2, 3, 4, 5, 6])

                def P_B(g):
                    W, HT, hk, c0, si_, sample = g.W, g.HT, g.hk, g.c0, g.si, g.sample
                    jrot = jrotB

                    def projT(col0, M, b):
                        for k in range(8):
                            MM(ps[b][0:M, 0:W], win[:, k, col0:col0 + M], HT[:, k, 0:W], k == 0, k == 7, ["win", hk], [psk[b]])

                    P.dma(ropet[64:96, 0, 0:W], ropeT[0, :, g.pos0:g.pos0 + W], writes=["ropet"])
                    P.dma(ropet[64:96, 1, 0:W], ropeT[1, :, g.pos0:g.pos0 + W], writes=["ropet"])
                    for (col0, dst, nmt) in ((C_FG, GF, 2), (C_SG, GB, 2), (C_MG, GM, 4)):
                        for mt in range(nmt):
                            b = jrot.next()
                            projT(col0 + mt * 128, 128, b)
                            gi_ = grot.next()
                            g1, g2 = g1s[gi_], g2s[gi_]
                            ACT(g1[:, 0:W], ps[b][:, 0:W], AF.Exp, [psk[b]], [f"g1_{gi_}"], scale=-1.0)
                            ACT(g2[:, 0:W], g1[:, 0:W], AF.Ln, [f"g1_{gi_}"], [f"g2_{gi_}"], bias=1.0)
                            ACT(g1[:, 0:W], g2[:, 0:W], AF.Exp, [f"g2_{gi_}"], [f"g1_{gi_}"], scale=-1.0)
                            sb_i = stbrot.next()
                            TT("dve", stb[sb_i][:, 0:W], ps[b][:, 0:W], g1[:, 0:W], ALU.mult, [psk[b], f"g1_{gi_}"], [f"stb{sb_i}"])
                            put(dst[mt * 128:(mt + 1) * 128, c0:c0 + W], stb[sb_i][:, 0:W], f"stb{sb_i}")
                    for (col0, dst, scl) in ((C_FQ, QF, None), (C_FK, KF, None), (C_SQ, QB, 0.125), (C_SK, KB, None)):
                        for mt in range(2):
                            b = jrot.next()
                            projT(col0 + mt * 128, 128, b)
                            sb_i = stbrot.next()
                            eng = evrot.next()
                            if scl is not None:
                                TS("dve", stb[sb_i][:, 0:W], ps[b][:, 0:W], scl, None, ALU.mult, None, [psk[b]], [f"stb{sb_i}"])
                            else:
                                CP(eng, stb[sb_i][:, 0:W], ps[b][:, 0:W], [psk[b]], [f"stb{sb_i}"])
                            put(dst[mt * 128:(mt + 1) * 128, c0:c0 + W], stb[sb_i][:, 0:W], f"stb{sb_i}")
                    for c in range(2):
                        b = jrot.next()
                        projT(C_CQ + c * 128, 128, b)
                        CP("act", cqT[:, c, 0:W], ps[b][:, 0:W], [psk[b]], ["cqT"])
                        ACT(sqb[:, c, 0:W], ps[b][:, 0:W], AF.Square, [psk[b]], ["sqb"])
                    bL = 7
                    nt = len(g.tiles)
                    nr0 = g.tiles[0][1]
                    for i, (r0, nr) in enumerate(g.tiles):
                        for k in range(8):
                            MM(ps[bL][0:nr, i * 4:i * 4 + 4], HT[:, k, i * 128:i * 128 + nr], win[:, k, C_FF:C_FF + 4], k == 0, k == 7,
                               ["win", hk], [psk[bL]])
                    TT("dve", lf[0:nr0, 0:nt, :], ps[bL][0:nr0, 0:nt * 4].rearrange("p (t h) -> p t h", t=nt), bfb[0:nr0, 0:nt, :], ALU.add,
                       [psk[bL], "bfb"], ["lf"])
                    ACT(lf[0:nr0, 4:4 + nt, :], lf[0:nr0, 0:nt, :], AF.Exp, ["lf"], ["lfe"], scale=-1.0)
                    ACT(lf[0:nr0, 0:nt, :], lf[0:nr0, 4:4 + nt, :], AF.Ln, ["lfe"], ["lf"], bias=1.0)
                    TS("dve", lf[0:nr0, 0:nt, :], lf[0:nr0, 0:nt, :], -1.0, None, ALU.mult, None, ["lf"], ["lf"])
                    for i, (r0, nr) in enumerate(g.tiles):
                        P.dma(o_fl[si_][l, r0:r0 + nr, :], lf[0:nr, i, :], reads=["lf"])
                    for i, (r0, nr) in enumerate(g.tiles):
                        orow = slice(r0, r0 + nr)
                        srow = slice(c0 + i * 128, c0 + i * 128 + nr)
                        lhs = lambda k: HT[:, k, i * 128:i * 128 + nr]
                        for (colk, okk, ovv, isfox) in ((C_FK, o_fk, o_fv, True), (C_SK, o_sk, o_sv, False)):
                            b = jrot.next()
                            for k in range(8):
                                MM(ps[b][0:nr, :], lhs(k), win[:, k, colk:colk + 512], k == 0, k == 7, ["win", hk], [psk[b]])
                            ki = kvrot.next()
                            CP("act", kvst[ki][0:nr, :], ps[b][0:nr, :], [psk[b]], [f"kvst{ki}"])
                            P.dma(okk[si_][l, orow, :], kvst[ki][0:nr, 0:256], reads=[f"kvst{ki}"])
                            P.dma(ovv[si_][l, orow, :], kvst[ki][0:nr, 256:512], reads=[f"kvst{ki}"])
                            if isfox:
                                vi = vrot.next()
                                CP("dve", vst[vi][0:nr, 0:4, 0:64], ps[b][0:nr, 256:512].rearrange("p (h d) -> p h d", h=4),
                                   [psk[b]], [f"vst{vi}"])
                                P.dma(VF[srow, :, :], vst[vi][0:nr, 0:4, :], reads=[f"vst{vi}"])
                            else:
                                sb_i = stbrot.next()
                                CP("dve", stb[sb_i][0:nr, 0:256], ps[b][0:nr, 256:512], [psk[b]], [f"stb{sb_i}"])
                                P.dma(VB[srow, :], stb[sb_i][0:nr, 0:256], reads=[f"stb{sb_i}"])
                        b = jrot.next()
                        for k in range(8):
                            MM(ps[b][0:nr, 0:160], lhs(k), win[:, k, C_CKV:C_CKV + 160], k == 0, k == 7, ["win", hk], [psk[b]])
                        r_ = rt[i]
                        rk = f"rt{i}"
                        ACT(sqj[0:nr, 0:128], ps[b][0:nr, 0:128], AF.Square, [psk[b]], ["sqj"], accum=r_[0:nr, 0:1])
                        rstd_from_ssq(r_[0:nr, 2:3], r_[0:nr, 0:1], 128, nr, ["sqj"], rk, r_[0:nr, 1:2])
                        ck = ckst[i]
                        ckk = f"ckst{i}"
                        STT(ck[0:nr, 0:128], ps[b][0:nr, 0:128], r_[0:nr, 2:3], gkvb[0:nr, :], ALU.mult, ALU.mult, [psk[b], rk, "gkvb"], [ckk])
                        cosv = cstok[0:nr, (32 if sample else r0 // 128), 0:16]
                        sinv = cstok[0:nr, (32 if sample else r0 // 128), 16:32]
                        x1p = ps[b][0:nr, 128:144]
                        x2p = ps[b][0:nr, 144:160]
                        TT("dve", ck[0:nr, 128:144], x1p, cosv, ALU.mult, [psk[b], "cstok"], [ckk])
                        TT("dve", r_[0:nr, 16:32], x2p, sinv, ALU.mult, [psk[b], "cstok"], [rk + "b"])
                        TT("dve", ck[0:nr, 128:144], ck[0:nr, 128:144], r_[0:nr, 16:32], ALU.subtract, [ckk, rk + "b"], [ckk])
                        TT("dve", ck[0:nr, 144:160], x2p, cosv, ALU.mult, [psk[b], "cstok"], [ckk])
                        TT("dve", r_[0:nr, 16:32], x1p, sinv, ALU.mult, [psk[b], "cstok", ckk], [rk + "b"])
                        TT("dve", ck[0:nr, 144:160], ck[0:nr, 144:160], r_[0:nr, 16:32], ALU.add, [ckk, rk + "b"], [ckk])
                        P.dma(o_ck[si_][l, orow, :], ck[0:nr, 0:128], reads=[ckk])
                        P.dma(o_kp[si_][l, orow, :], ck[0:nr, 128:160], reads=[ckk])

                def P_C(g):
                    W, c0, sample = g.W, g.c0, g.sample
                    bT1, bT2, bT3, bL = 4, 5, 6, 7
                    if sample:
                        P.dma(lfp[:, 0:8, :], cfl[l, 0:1024, :].rearrange("(t p) h -> p t h", p=128), writes=["lfp"])
                        P.dma(lfp[:, 8:16, :], cfl[l, 1024:2048, :].rearrange("(t p) h -> p t h", p=128), writes=["lfp"])
                        MEMSET("pool", Rp[:, 0, :], 0.0, ["Rp"])
                        for t in range(16):
                            TT("pool", Rp[:, t + 1, :], Rp[:, t, :], lfp[:, t, :], ALU.add, ["Rp", "lfp"], ["Rp"])
                        bF = jrot.next()
                        for t in range(16):
                            MM(ps[bF][:, t * 4:t * 4 + 4], triF, lfp[:, t, :], True, False, ["cst", "lfp"], [psk[bF]], sig=False)
                            MM(ps[bF][:, t * 4:t * 4 + 4], onesF, Rp[:, t, :], False, True, ["cst", "Rp"], [psk[bF]], sig=True)
                        TS("dve", negFp[:].rearrange("p t h -> p (t h)"), ps[bF][:, 0:64], -1.0, None, ALU.mult, None, [psk[bF]], ["negFp"])
                        CP("dve", Rrun[:], Rp[:, 16, :], ["Rp"], ["Rrun"])
                    elif g.gi == 0:
                        MEMSET("pool", Rrun[:], 0.0, ["Rrun"])
                    nt = len(g.tiles)
                    nr0 = g.tiles[0][1]
                    tg0 = 33 if sample else (g.tiles[0][0] // 128)
                    for i, (r0, nr) in enumerate(g.tiles):
                        ck = ckst[i]
                        ckk = f"ckst{i}"
                        TR(ps[bT1][:, i * 128:i * 128 + nr], ck[0:nr, 0:128], cst[0:nr, 0:nr], [ckk, "cst"], [psk[bT1]])
                        TR(ps[bT2][0:32, i * 128:i * 128 + nr], ck[0:nr, 128:160], cst[0:nr, 0:nr], [ckk, "cst"], [psk[bT2]])
                    for i, (r0, nr) in enumerate(g.tiles):
                        fo = ps[bL][0:nr, 32 + i * 4:36 + i * 4]
                        MM(fo, cst[0:nr, 128:128 + nr], lf[0:nr, i, :], True, False, ["cst", "lf"], [psk[bL]], sig=False)
                        MM(fo, cst[:, 256:256 + nr], Rrun[:], False, i == 0, ["cst", "Rrun"], [psk[bL]], sig=(i == 0))
                        for j in range(i):
                            MM(fo, cst[:, 256:256 + nr], lf[:, j, :], False, j == i - 1, ["cst", "lf"], [psk[bL]], sig=(j == i - 1))
                    TS("dve", negF[0:nr0, tg0:tg0 + nt, :], ps[bL][0:nr0, 32:32 + 4 * nt].rearrange("p (t h) -> p t h", t=nt), -1.0, None,
                       ALU.mult, None, [psk[bL]], ["negF"])
                    TS("dve", f8[0:nr0, 0:nt, :], ps[bL][0:nr0, 32:32 + 4 * nt].rearrange("p (t h) -> p t h", t=nt), 8.0, None,
                       ALU.mult, None, [psk[bL]], ["f8"])
                    if not sample:
                        for i in range(nt):
                            TT("dve", Rrun[:], Rrun[:], lf[:, i, :], ALU.add, ["Rrun", "lf"], ["Rrun"])
                    for i, (r0, nr) in enumerate(g.tiles):
                        TR(ps[bT3][0:4, i * 128:i * 128 + nr], f8[0:nr, i, :], cst[0:nr, 0:nr], ["f8", "cst"], [psk[bT3]])
                    CP("act", ckTg[:, 0:W], ps[bT1][:, 0:W], [psk[bT1]], ["ckTg"])
                    CP("dve", kpTg[:, 0:W], ps[bT2][0:32, 0:W], [psk[bT2]], ["kpTg"])
                    put(KP[:, c0:c0 + W], kpTg[:, 0:W], "kpTg")
                    CP("dve", fqs[:, 0:W], ps[bT3][0:4, 0:W], [psk[bT3]], ["fqs"])
                    put(FQ[:, c0:c0 + W], fqs[:, 0:W], "fqs")
                    b = jrot.next()
                    for c in range(2):
                        MM(ps[b][:, 0:W], cstb[:, 256:384], sqb[:, c, 0:W], c == 0, c == 1, ["cstb", "sqb"], [psk[b]])
                    ACT(rsbt[:, 0:W], ps[b][:, 0:W], AF.Ln, [psk[b]], ["rsbt"], bias=epst[:, :], scale=-1.0 / 256)
                    ACT(rsb[:, 0:W], rsbt[:, 0:W], AF.Exp, ["rsbt"], ["rsb"], scale=-0.5)
                    for c in range(2):
                        STT(cqg[:, c, 0:W], cqT[:, c, 0:W], prmT[:, 40 + c:41 + c], rsb[:, 0:W], ALU.mult, ALU.mult, ["cqT", "prmT", "rsb"], ["cqg"])
                    for hp in range(4):
                        b = jrot.next()
                        MM(ps[b][:, 0:W], wuk[:, hp * 128:(hp + 1) * 128], ckTg[:, 0:W], True, True, ["wuk", "ckTg"], [psk[b]])
                        sb_i = stbrot.next()
                        CP(evrot.next(), stb[sb_i][:, 0:W], ps[b][:, 0:W], [psk[b]], [f"stb{sb_i}"])
                        put(KM[hp * 128:(hp + 1) * 128, c0:c0 + W], stb[sb_i][:, 0:W], f"stb{sb_i}")
                    for i, (r0, nr) in enumerate(g.tiles):
                        b = jrot.next()
                        MM(ps[b][0:nr, :], ckTg[:, i * 128:i * 128 + nr], wuv[:], True, True, ["wuv", "ckTg"], [psk[b]])
                        vi = vrot.next()
                        CP(evrot.next(), vst[vi][0:nr, :, 0:64], ps[b][0:nr, :].rearrange("p (h d) -> p h d", h=8), [psk[b]], [f"vst{vi}"])
                        P.dma(VM[c0 + i * 128:c0 + i * 128 + nr, :, :], vst[vi][0:nr, :, :], reads=[f"vst{vi}"])
                    for h in range(8):
                        bA = qhrot.next()
                        bB = qhrot.next()
                        for c in range(2):
                            MM(ps[bA][0:96, 0:W], wuq[:, c, h * 96:(h + 1) * 96], cqg[:, c, 0:W], c == 0, c == 1, ["wuq", "cqg"], [psk[bA]])
                        for c in range(2):
                            MM(ps[bB][0:96, 0:W], wuqs[:, c, h * 96:(h + 1) * 96], cqg[:, c, 0:W], c == 0, c == 1, ["wuqs", "cqg"], [psk[bB]])
                        qi = qrot.next()
                        q_ = qst[qi]
                        qk = f"qst{qi}"
                        ti = tqrot.next()
                        CP("act", q_[0:64, 0:W], ps[bA][0:64, 0:W], [psk[bA]], [qk])
                        TT("dve", tq1[ti][64:96, 0:W], ps[bA][64:96, 0:W], ropet[64:96, 0, 0:W], ALU.mult, [psk[bA], "ropet"], [f"tq1_{ti}"])
                        TT("dve", tq2[ti][64:96, 0:W], ps[bB][64:96, 0:W], ropet[64:96, 1, 0:W], ALU.mult, [psk[bB], "ropet"], [f"tq2_{ti}"])
                        TT("pool", q_[64:96, 0:W], tq1[ti][64:96, 0:W], tq2[ti][64:96, 0:W], ALU.add, [f"tq1_{ti}", f"tq2_{ti}"], [qk + "b"])
                        P.dma(QM[h, :, c0:c0 + W], q_[0:96, 0:W], reads=[qk, qk + "b"], writes=[])

                groups = [PG(gi, 512, [(gi * 512 + i * 128, 128) for i in range(4)], False) for gi in range(8)]
                groups.append(PG(8, SS, [(0, SS)], True))
                P_A1(groups[0])
                P_A2(groups[0])
                for gi, g in enumerate(groups):
                    nx = groups[gi + 1] if gi + 1 < len(groups) else None
                    if nx is not None:
                        P_A1(nx)
                    P_B(g)
                    if nx is not None:
                        P_A2(nx)
                    P_C(g)
                P.barrier()
                P.emit()

            wst_cur[0].close()
            if do_b:
                with ExitStack() as st:
                    KTb = [sbt(st, f"KT{i}", [128, S], BF16) for i in range(2)]
                    KTs = [sbt(st, f"KTs{i}", [128, PAST + SS], BF16) for i in range(4)]
                    Vbig = [sbt(st, f"Vbig{i}", [128, 32, 4, 128], BF16) for i in range(2)]
                    Vs = [sbt(st, f"Vs{i}", [128, 17, 128], BF16) for i in range(4)]
                    Qt = [sbt(st, f"Qt{i}", [128, 512], BF16) for i in range(5)]
                    SGt = [sbt(st, f"SGt{i}", [64, 512], BF16) for i in range(5)]
                    Qs = sbt(st, "Qs", [128, 4, SS], BF16)
                    SGs = sbt(st, "SGs", [64, 4, SS], BF16)
                    Pb = [sbt(st, f"Pb{i}", [128, 512], BF16) for i in range(5)]
                    Eb = [sbt(st, f"Eb{i}", [128, 512], F32) for i in range(2)]
                    SPb = [sbt(st, f"SPb{i}", [128, 512], BF16) for i in range(4)]
                    SPs = [sbt(st, f"SPs{i}", [128, 512], BF16) for i in range(5)]
                    Pbs = sbt(st, "Pbs", [128, 4, 6, SS], BF16)
                    Ebs = sbt(st, "Ebs", [128, 4, 2, SS], F32)
                    SPbs = sbt(st, "SPbs", [128, 4, 4, SS], BF16)
                    SPss = sbt(st, "SPss", [128, 4, 5, SS], BF16)
                    rl = sbt(st, "rl", [64, 512], F32)
                    tmpb = sbt(st, "tmpb", [64, 512], F32)
                    yb = [sbt(st, f"yb{i}", [64, 512], BF16) for i in range(2)]
                    cstg = [sbt(st, f"cstg{i}", [128, 16, 128], F32) for i in range(1)]
                    kst = sbt(st, "kst", [128, 4, 16, 64], BF16)
                    ckTs = sbt(st, "ckTs", [128, PAST], BF16)
                    qrot2 = Rot([0, 1, 2, 3, 4])
                    yrot = Rot([0, 1])
                    orot = Rot([4, 5, 6])
                    jr2 = Rot([7, 2, 3])
                    orot_sb = Rot([4, 5])
                    ktrot = Rot([0, 1])
                    cgrot = Rot([0, 1])
                    for i in range(4):
                        MEMSET("pool", Vs[i][:, :, 64:128], 2.0, [f"Vs{i}"])

                    class Bufs:
                        def __init__(self, sample_slot=None):
                            self.ss = sample_slot
                            self.c = {}

                        def _n(self, name, n):
                            i = self.c.get(name, 0)
                            self.c[name] = i + 1
                            return i % n

                        def S(self):
                            i = self._n("S", 4) if self.ss is None else self.ss
                            return ps[i], psk[i]

                        def Z(self):
                            if self.ss is not None:
                                return ps[6][:, self.ss * SS:(self.ss + 1) * SS], psk[6]
                            i = self._n("Z", 2)
                            return ps[i], psk[i]

                        def W(self):
                            if self.ss is not None:
                                return ps[6][:, 64 + self.ss * SS:64 + (self.ss + 1) * SS], psk[6]
                            i = 2 + self._n("W", 2)
                            return ps[i], psk[i]

                        def P(self):
                            if self.ss is None:
                                i = self._n("P", 5)
                                return Pb[i], f"Pb{i}"
                            i = self._n("P", 6)
                            return Pbs[:, self.ss, i, :], f"Pbs{self.ss}_{i}"

                        def E(self):
                            if self.ss is None:
                                i = self._n("E", 2)
                                return Eb[i], f"Eb{i}"
                            i = self._n("E", 2)
                            return Ebs[:, self.ss, i, :], f"Ebs{self.ss}_{i}"

                        def SP(self):
                            if self.ss is None:
                                i = self._n("SP", 4)
                                return SPb[i], f"SPb{i}"
                            i = self._n("SP", 4)
                            return SPbs[:, self.ss, i, :], f"SPbs{self.ss}_{i}"

                        def SS_(self):
                            if self.ss is None:
                                i = self._n("SS", 5)
                                return SPs[i], f"SPs{i}"
                            i = self._n("SS", 5)
                            return SPss[:, self.ss, i, :], f"SPss{self.ss}_{i}"

                    def run_softmax(problems, LA=3):
                        N = max(len(p[0]) for p in problems)
                        for s in range(N + LA):
                            for (items, bf) in problems:
                                if s < len(items):
                                    it = items[s]
                                    if it.get("pre"):
                                        it["pre"]()
                                    kw, qw, cl, mw = it["kw"], it["qw"], it.get("cl", 0), it.get("mw", 128)
                                    sb_, sk_ = bf.S()
                                    last = it["mask"] is None
                                    MM(sb_[0:kw, cl:qw], it["kt"], it["q"][:, cl:qw], True, last, it["rk"] + it["qk"], [sk_], sig=last)
                                    if not last:
                                        MM(sb_[0:kw, cl:cl + mw], identb[0:kw, 0:kw], it["mask"], False, True, ["cstb", "mk"], [sk_], sig=True)
                                    pt, pk = bf.P()
                                    ACT(pt[0:kw, cl:qw], sb_[0:kw, cl:qw], AF.Exp, [sk_] + it["bk"], [pk], bias=it["bias"], scale=it["scale"])
                                    it["p"] = (pt, pk)
                            for (items, bf) in problems:
                                j = s - LA
                                if 0 <= j < len(items):
                                    it = items[j]
                                    kw, qw, cl = it["kw"], it["qw"], it.get("cl", 0)
                                    pt, pk = it["p"]
                                    ob = it["ob"]
                                    oc = it.get("oc", 0)
                                    MM(ps[ob][0:it["M"], oc + cl:oc + qw], it["v"], pt[0:kw, cl:qw], it["first"] and not it.get("nostart"), it["last"],
                                       it["rk"] + [pk], [psk[ob]], sig=it["last"])
                                    if it["last"]:
                                        it["fin"]()

                    def run_sb(problems):
                        N = max(len(p[0]) for p in problems)
                        for s in range(N + 5):
                            for (items, bf) in problems:
                                if s < len(items):
                                    it = items[s]
                                    if it.get("pre"):
                                        it["pre"]()
                                    kw, qw, cl, mw = it["kw"], it["qw"], it.get("cl", 0), it.get("mw", 128)
                                    zb, zk = bf.Z()
                                    last = it["mask"] is None
                                    MM(zb[0:kw, cl:qw], it["kt"], it["q"][:, cl:qw], True, last, it["rk"] + it["qk"], [zk], sig=last)
                                    if not last:
                                        MM(zb[0:kw, cl:cl + mw], identb[0:kw, 0:kw], it["mask"], False, True, ["cstb", "mk"], [zk], sig=True)
                                    if bf.ss is None:
                                        et, ek = zb, zk
                                    else:
                                        et, ek = bf.E()
                                    ACT(et[0:kw, cl:qw], zb[0:kw, cl:qw], AF.Exp, [zk], [ek])
                                    it["e"] = (et, ek)
                            for (items, bf) in problems:
                                s_ = s
                                s = s_ - 1
                                if 0 <= s < len(items):
                                    it = items[s]
                                    kw, qw, cl, mw = it["kw"], it["qw"], it.get("cl", 0), it.get("mw", 128)
                                    et, ek = it["e"]
                                    spt, spk = bf.SP()
                                    ACT(spt[0:kw, cl:qw], et[0:kw, cl:qw], AF.Ln, [ek], [spk], bias=1.0)
                                    sst, ssk = bf.SS_()
                                    if it["first"]:
                                        if kw < 128:
                                            MEMSET("pool", sst[:, 0:qw], 0.0, [ssk])
                                        CP("pool", sst[0:kw, cl:qw], spt[0:kw, cl:qw], [spk], [ssk])
                                    else:
                                        pst, psk_ = items[s - 1]["ss"]
                                        pcl = items[s - 1].get("cl", 0)
                                        TT("dve", sst[:, pcl:qw], pst[:, pcl:qw], spt[:, pcl:qw], ALU.add, [psk_, psk_ + "x", spk], [ssk])
                                        if pcl > cl:
                                            CP("pool", sst[:, cl:pcl], spt[:, cl:pcl], [spk], [ssk + "x"])
                                    it["sp"] = (spt, spk)
                                    it["ss"] = (sst, ssk)
                                s = s_
                            for (items, bf) in problems:
                                j = s - 3
                                if 0 <= j < len(items):
                                    it = items[j]
                                    kw, qw, cl, mw = it["kw"], it["qw"], it.get("cl", 0), it.get("mw", 128)
                                    wb, wk_ = bf.W()
                                    MM(wb[0:kw, cl:qw], it["kt"], it["q"][:, cl:qw], True, False, it["rk"] + it["qk"], [wk_], sig=False)
                                    if it["mask"] is not None:
                                        MM(wb[0:kw, cl:cl + mw], identb[0:kw, 0:kw], it["mask"], False, False, ["cstb", "mk"], [wk_], sig=False)
                                    spt, spk = it["sp"]
                                    MM(wb[0:kw, cl:qw], tinclneg[0:kw, 0:kw], spt[0:kw, cl:qw], False, it["first"], ["cstb", spk], [wk_], sig=it["first"])
                                    if not it["first"]:
                                        pst, psk_ = items[j - 1]["ss"]
                                        pcl = items[j - 1].get("cl", 0)
                                        MM(wb[0:kw, pcl:qw], onesneg[:, 0:kw], pst[:, pcl:qw], False, True, ["cstb", psk_, psk_ + "x"], [wk_], sig=True)
                                    pt, pk = bf.P()
                                    ACT(pt[0:kw, cl:qw], wb[0:kw, cl:qw], AF.Exp, [wk_], [pk])
                                    it["p"] = (pt, pk)
                            for (items, bf) in problems:
                                j = s - 5
                                if 0 <= j < len(items):
                                    it = items[j]
                                    kw, qw, cl = it["kw"], it["qw"], it.get("cl", 0)
                                    pt, pk = it["p"]
                                    ob = it["ob"]
                                    oc = it.get("oc", 0)
                                    MM(ps[ob][0:it["M"], oc + cl:oc + qw], it["v"], pt[0:kw, cl:qw], it["first"] and not it.get("nostart"), it["last"],
                                       it["rk"] + [pk], [psk[ob]], sig=it["last"])
                                    if it["last"]:
                                        it["fin"]()

                    def finalize(kind, ob, sg_ap, sgk, qw, yrow, c0, ro=0, oc=0):
                        yi = yrot.next()
                        if kind == "sb":
                            STT(yb[yi][:, 0:qw], ps[ob][ro:ro + 64, oc:oc + qw], 1.0, sg_ap, ALU.mult, ALU.mult, [psk[ob], sgk], [f"yb{yi}"])
                        else:
                            P.op("dve", lambda e: e.reciprocal(out=rl[:, 0:qw], in_=ps[ob][64:128, oc:oc + qw]), reads=[psk[ob]], writes=["rl"])
                            TT("pool", tmpb[:, 0:qw], sg_ap, rl[:, 0:qw], ALU.mult, [sgk, "rl"], ["tmpb"])
                            STT(yb[yi][:, 0:qw], ps[ob][0:64, oc:oc + qw], 2.0, tmpb[:, 0:qw], ALU.mult, ALU.mult, [psk[ob], "tmpb"], [f"yb{yi}"])
                        P.dma(YT[yrow:yrow + 64, c0:c0 + qw], yb[yi][:, 0:qw], reads=[f"yb{yi}"])

                    def run_group(kind, hook):
                        nheads = 8 if kind == "mla" else 4
                        Qsrc = {"fox": QF, "sb": QB}.get(kind)
                        Ksrc = {"fox": KF, "sb": KB, "mla": KM}[kind]
                        Gsrc = {"fox": GF, "sb": GB, "mla": GM}[kind]
                        ybase = {"fox": 0, "sb": 256, "mla": 512}[kind]
                        Kd = {"fox": 65, "sb": 128, "mla": 96}[kind]
                        M = 128
                        mkind = {"fox": 0, "sb": 1, "mla": 2}[kind]
                        scale = {"fox": 0.125, "sb": 1.0, "mla": MLA_SCALE}[kind]
                        runner = run_sb if kind == "sb" else run_softmax

                        def sample_loads(hs, sl0):
                            ctxs = []
                            for sl, h in enumerate(hs, start=sl0):
                                Ks, ksk = KTs[sl], f"KTs{sl}"
                                Vh, vsk = Vs[sl], f"Vs{sl}"
                                if kind in ("fox", "sb"):
                                    ksrc_c = cfk if kind == "fox" else csk
                                    vsrc_c = cfv if kind == "fox" else csv
                                    P.dma(kst[:, sl, :, :], ksrc_c[l, :, h * 64:(h + 1) * 64].rearrange("(t p) c -> p t c", p=128),
                                          writes=[f"kst{sl}"], queue="pool")
                                    P.dma(Vh[:, 0:16, 0:64], vsrc_c[l, :, h * 64:(h + 1) * 64].rearrange("(t p) c -> p t c", p=128),
                                          writes=[vsk], queue="pool")
                                    if kind == "fox":
                                        MEMSET("pool", Ks[64:65, :], 1.0, [ksk + "a"])
                                        P.dma(Vh[0:SS, 16, :], VF[S:S + SS, h, :], writes=[vsk + "n"])
                                        P.dma(Qs[64:65, sl, :], FQ[h:h + 1, S:S + SS], writes=[f"Qs{sl}a"])
                                    else:
                                        MEMSET("pool", Ks[64:128, :], 0.0, [ksk + "a"])
                                        MEMSET("pool", Qs[64:128, sl, :], 0.0, [f"Qs{sl}a"])
                                        P.dma(Vh[0:SS, 16, 0:64], VB[S:S + SS, h * 64:(h + 1) * 64], writes=[vsk + "n"])
                                    P.dma(Qs[0:64, sl, :], Qsrc[h * 64:(h + 1) * 64, S:S + SS], writes=[f"Qs{sl}"])
                                else:
                                    P.dma(Vh[0:SS, 16, :], VM[S:S + SS, h, :], writes=[vsk + "n"])
                                    P.dma(Qs[0:96, sl, :], QM[h, :, S:S + SS], writes=[f"Qs{sl}"])
                                P.dma(Ks[0:64, PAST:PAST + SS], Ksrc[h * 64:(h + 1) * 64, S:S + SS], writes=[ksk + "n"])
                                P.dma(SGs[:, sl, :], Gsrc[h * 64:(h + 1) * 64, S:S + SS], writes=[f"SGs{sl}"])
                                ctxs.append(dict(h=h, sl=sl, Ks=Ks, ksk=ksk, Vh=Vh, vsk=vsk))
                            return ctxs

                        def sample_prep(ctxs):
                            for c_ in ctxs:
                                h, sl, Ks, ksk, Vh, vsk = c_["h"], c_["sl"], c_["Ks"], c_["ksk"], c_["Vh"], c_["vsk"]
                                if kind in ("fox", "sb"):
                                    for g2 in range(2):
                                        b = jr2.next()
                                        psb = ps[b][:].bitcast(BF16)
                                        for j in range(8):
                                            TR(psb[0:64, j * 128:(j + 1) * 128], kst[:, sl, g2 * 8 + j, :], identb, [f"kst{sl}", "cstb"], [psk[b]])
                                        CP(evrot.next(), Ks[0:64, g2 * 1024:(g2 + 1) * 1024], psb[0:64, 0:1024], [psk[b]], [ksk])
                                else:
                                    for g4 in range(4):
                                        b = jr2.next()
                                        MM(ps[b][0:64, :], wuk[:, h * 64:(h + 1) * 64], ckTs[:, g4 * 512:(g4 + 1) * 512], True, True, ["wuk", "ckTs"], [psk[b]])
                                        CP(evrot.next(), Ks[0:64, g4 * 512:(g4 + 1) * 512], ps[b][0:64, :], [psk[b]], [ksk])
                                    for g4 in range(4):
                                        b = jr2.next()
                                        for i in range(4):
                                            t = g4 * 4 + i
                                            MM(ps[b][:, i * 64:(i + 1) * 64], ckTs[:, t * 128:(t + 1) * 128], wuv[:, h * 64:(h + 1) * 64], True, True,
                                               ["wuv", "ckTs"], [psk[b]])
                                        CP(evrot.next(), Vh[:, g4 * 4:(g4 + 1) * 4, 0:64], ps[b][:, 0:256].rearrange("p (t d) -> p t d", t=4), [psk[b]], [vsk])

                        def mla_sample_once():
                            P.dma(kst[:, 0, :, 0:32], ckp[l].rearrange("(t p) c -> p t c", p=128), writes=["kst0"], queue="pool")
                            for g2 in range(2):
                                b = jr2.next()
                                psb = ps[b][:].bitcast(BF16)
                                for j in range(8):
                                    TR(psb[0:32, j * 128:(j + 1) * 128], kst[:, 0, g2 * 8 + j, 0:32], identb, ["kst0", "cstb"], [psk[b]])
                                for sl in range(4):
                                    CP(evrot.next(), KTs[sl][64:96, g2 * 1024:(g2 + 1) * 1024], psb[0:32, 0:1024], [psk[b]], [f"KTs{sl}a"])
                            for sl in range(4):
                                P.dma(KTs[sl][64:96, PAST:PAST + SS], KP[:, S:S + SS], writes=[f"KTs{sl}b"])

                        def sample_attn(ctxs):
                            problems = []
                            for c_ in ctxs:
                                h, sl, Ks, ksk, Vh, vsk = c_["h"], c_["sl"], c_["Ks"], c_["ksk"], c_["Vh"], c_["vsk"]
                                rks = [ksk, ksk + "a", ksk + "b", ksk + "n", vsk, vsk + "n"]
                                ob = sl
                                qap = Qs[0:Kd, sl, :]
                                qk = [f"Qs{sl}", f"Qs{sl}a"]
                                items = []
                                for kb in range(16):
                                    items.append(dict(kt=Ks[0:Kd, kb * 128:(kb + 1) * 128], v=Vh[:, kb, 0:M], kw=128, qw=SS, mask=None,
                                                      bias=(negFp[:, kb, h:h + 1] if kind == "fox" else None),
                                                      bk=(["negFp"] if kind == "fox" else []), rk=rks))
                                smask = None
                                if kind != "mla":
                                    ms = mkind * 896 + 384
                                    smask = mk[0:SS, ms:ms + SS]
                                items.append(dict(kt=Ks[0:Kd, PAST:PAST + SS], v=Vh[0:SS, 16, 0:M], kw=SS, qw=SS, mask=smask, cl=0, mw=SS,
                                                  bias=(negF[0:SS, 33, h:h + 1] if kind == "fox" else None),
                                                  bk=(["negF"] if kind == "fox" else []), rk=rks))
                                if kind == "sb":
                                    items = items[::-1]
                                obank = 4 + (sl % 2) if False else None
                                for ii, it in enumerate(items):
                                    it.update(q=qap, qk=qk, scale=scale, M=M, first=(ii == 0), last=(ii == len(items) - 1), ob=None, fin=None)
                                problems.append((items, Bufs(sample_slot=sl), c_))
                            MM(ps[7][:, 0:4 * SS], zb[:], zb[:, 0:4 * SS], True, True, ["zb"], ["ps7"])
                            pr = []
                            for (items, bf, c_) in problems:
                                h, sl = c_["h"], c_["sl"]
                                for it in items:
                                    it.update(ob=7, oc=sl * SS, nostart=True)
                                items[-1]["fin"] = (lambda sl=sl, h=h: finalize(kind, 7, SGs[:, sl, :], f"SGs{sl}", SS, ybase + h * 64, S, oc=sl * SS))
                                pr.append((items, bf))
                            return pr

                        def vidx(h):
                            return ((0 if kind != "sb" else 1) if h < 4 else 1)

                        def prep_head(h):
                            ctx = dict(h=h)
                            if h % 4 == 0:
                                vi = vidx(h)
                                vk = f"Vbig{vi}"
                                if kind == "fox":
                                    for q4 in range(4):
                                        P.dma(Vbig[vi][:, q4 * 8:(q4 + 1) * 8, :, :],
                                              VF[q4 * 1024:(q4 + 1) * 1024, :, :].rearrange("(t p) h c -> p t h c", p=128), writes=[vk])
                                elif kind == "sb":
                                    vv = Vbig[vi][:].rearrange("p t h c -> p (t h c)")[:, 0:32 * 256].rearrange("p (t c) -> p t c", t=32)
                                    for q4 in range(4):
                                        P.dma(vv[:, q4 * 8:(q4 + 1) * 8, :],
                                              VB[q4 * 1024:(q4 + 1) * 1024, :].rearrange("(t p) c -> p t c", p=128), writes=[vk])
                                else:
                                    for q4 in range(4):
                                        P.dma(Vbig[vi][:, q4 * 8:(q4 + 1) * 8, :, :],
                                              VM[q4 * 1024:(q4 + 1) * 1024, h:h + 4, :].rearrange("(t p) h c -> p t h c", p=128), writes=[vk])
                            ki = h % 2
                            KT = KTb[ki]
                            kk = f"KT{ki}"
                            P.dma(KT[0:64, :], Ksrc[h * 64:(h + 1) * 64, 0:S], writes=[kk])
                            if kind == "fox":
                                MEMSET("pool", KT[64:65, :], 1.0, [kk + "a"])
                            elif kind == "mla":
                                P.dma(KT[64:96, :], KP[:, 0:S], writes=[kk + "a"])
                            else:
                                MEMSET("pool", KT[64:128, :], 0.0, [kk + "a"])
                            ctx.update(KT=KT, kk=kk)
                            return ctx

                        def load_q(h, qt):
                            qi = qrot2.next()
                            qk = f"Qt{qi}"
                            c0 = qt * 512
                            if kind == "mla":
                                P.dma(Qt[qi][0:96, :], QM[h, :, c0:c0 + 512], writes=[qk])
                            else:
                                P.dma(Qt[qi][0:64, :], Qsrc[h * 64:(h + 1) * 64, c0:c0 + 512], writes=[qk])
                                if kind == "fox":
                                    P.dma(Qt[qi][64:65, :], FQ[h:h + 1, c0:c0 + 512], writes=[qk + "a"])
                                else:
                                    MEMSET("pool", Qt[qi][64:128, :], 0.0, [qk + "a"])
                            P.dma(SGt[qi][:, :], Gsrc[h * 64:(h + 1) * 64, c0:c0 + 512], writes=[f"SGt{qi}"])
                            return qi

                        if kind == "mla":
                            P.dma(cstg[0][:], cck[l].rearrange("(t p) c -> p t c", p=128), writes=["cstg0", "cstg0v"])
                            for g4 in range(4):
                                b = jr2.next()
                                for i in range(4):
                                    TR(ps[b][:, i * 128:(i + 1) * 128], cstg[0][:, g4 * 4 + i, :], ident, ["cstg0", "cst"], [psk[b]])
                                CP(evrot.next(), ckTs[:, g4 * 512:(g4 + 1) * 512], ps[b][:, :], [psk[b]], ["ckTs"])

                        if kind == "mla":
                            mla_sample_once()
                        c03 = sample_loads([0, 1, 2, 3], 0)
                        sample_prep(c03)
                        yield
                        spr0 = sample_attn(c03)
                        pbufs = Bufs()
                        late = []
                        prep_head(0)
                        all_items = []
                        qpre = {}
                        qseq = [(h_, q_) for h_ in range(nheads) for q_ in range(8)]
                        qpre[(0, 0)] = load_q(0, 0)
                        qpre[(0, 1)] = load_q(0, 1)
                        for h in range(nheads):
                            vc = vidx(h)
                            vk = f"Vbig{vc}"
                            KT, kk = KTb[h % 2], f"KT{h % 2}"
                            if kind == "sb":
                                vv = Vbig[vc][:].rearrange("p t h c -> p (t h c)")[:, 0:32 * 256].rearrange("p (t c) -> p t c", t=32)
                                Vof = lambda t, h=h, vv=vv: vv[:, t, (h // 2) * 128:(h // 2) * 128 + 128]
                            else:
                                Vof = lambda t, h=h, vc=vc: Vbig[vc][:, t, h % 4, :]
                            items = []
                            for qt in range(8):
                                c0 = qt * 512
                                nkb = 4 * qt + 4
                                ob = (orot_sb if kind == "sb" else orot).next()
                                blk = []
                                for kb in range(nkb):
                                    r = kb - 4 * qt
                                    mask = None
                                    cl = 0
                                    if r >= 0:
                                        ms = mkind * 896 + 384
                                        mask = mk[:, ms:ms + 128]
                                        cl = 128 * r
                                    blk.append(dict(kt=KT[0:Kd, kb * 128:(kb + 1) * 128], v=Vof(kb), kw=128, qw=512, mask=mask, cl=cl, mw=128,
                                                    bias=(negF[:, kb, h:h + 1] if kind == "fox" else None),
                                                    bk=(["negF"] if kind == "fox" else []), rk=[kk, kk + "a", vk],
                                                    scale=scale, M=M, ob=ob, first=False, last=False, fin=None, qt=qt))
                                if kind == "sb":
                                    blk = blk[::-1]
                                blk[0]["first"] = True
                                blk[-1]["last"] = True
                                items.extend(blk)
                            def mk_pre(h, qt, blk_items):
                                def pre():
                                    if qt == 1 and h + 1 < nheads:
                                        prep_head(h + 1)
                                    if h == 2 and qt == 2 and hook is not None:
                                        hook()
                                    if kind == "mla" and h == 5 and qt == 2:
                                        c47 = sample_loads([4, 5, 6, 7], 0)
                                        sample_prep(c47)
                                        late.append(c47)
                                    qi = qpre.pop((h, qt))
                                    for it in blk_items:
                                        it["q"] = Qt[qi][0:Kd, :]
                                        it["qk"] = [f"Qt{qi}", f"Qt{qi}a"]
                                    blk_items[-1]["fin"] = (lambda qi=qi, ob=blk_items[0]["ob"], qt=qt, h=h:
                                                            finalize(kind, ob, SGt[qi][:, :], f"SGt{qi}", 512, ybase + h * 64, qt * 512,
                                                                     ro=((h % 2) * 64 if kind == "sb" else 0)))
                                    idx = h * 8 + qt + 2
                                    if idx < len(qseq):
                                        qpre[qseq[idx]] = load_q(*qseq[idx])
                                return pre
                            pos = 0
                            for qt in range(8):
                                nkb = 4 * qt + 4
                                seg = items[pos:pos + nkb]
                                seg[0]["pre"] = mk_pre(h, qt, seg)
                                pos += nkb
                            all_items.extend(items)
                        runner([(all_items, pbufs)] + spr0)

                        for c47 in late:
                            runner(sample_attn(c47))

                    g_mla = run_group("mla", None)
                    g_sb = run_group("sb", lambda: next(g_mla))
                    g_fox = run_group("fox", lambda: next(g_sb))
                    next(g_fox)
                    for g_ in (g_fox, g_sb, g_mla):
                        next(g_, None)
                    P.barrier()
                    P.emit()

            if l + 1 < depth:
                wst_cur[0] = ExitStack()
                cur_w[0] = alloc_weights(wst_cur[0])
                wgen[0] = load_weights(l + 1, cur_w[0], ("pool",))
            if do_c:
                with ExitStack() as st:
                    wob = sbt(st, "wob", [128, 8, D], BF16)
                    stg = [sbt(st, f"stgc{i}", [128, 2048], F32) for i in range(2)]
                    ytl = [sbt(st, f"ytl{i}", [128, 8, 512], BF16) for i in range(2)]
                    xr = [sbt(st, f"xr{i}", [128, D], F32) for i in range(2)]
                    tt_ = [sbt(st, f"tt{i}", [128, D], F32) for i in range(2)]
                    xo = [sbt(st, f"xo{i}", [128, D], F32) for i in range(2)]
                    sqc = sbt(st, "sqc", [128, 512], BF16)
                    stc = sbt(st, "stc", [128, 8], F32)
                    wv = w_out[l].rearrange("(k p) c -> p k c", p=128)
                    for c4 in range(4):
                        sv = stg[c4 % 2][:, 0:2048].rearrange("p (k c) -> p k c", k=8)
                        P.dma(sv, wv[:, :, c4 * 256:(c4 + 1) * 256], writes=[f"stgc{c4 % 2}"])
                        CP(["pool", "dve"][c4 % 2], wob[:, :, c4 * 256:(c4 + 1) * 256], sv, [f"stgc{c4 % 2}"], ["wob"])
                    yv = YT.rearrange("(k p) t -> p k t", p=128)
                    crot = Rot([0, 1])
                    brot = Rot([0, 2, 4, 6])

                    def out_tile(ytile, ykey, coff, nr, xsrc, xkey, gg, dst, dkey):
                        b0 = brot.next()
                        for hf in range(2):
                            for k in range(8):
                                MM(ps[b0 + hf][0:nr, :], ytile[:, k, coff:coff + nr], wob[:, k, hf * 512:(hf + 1) * 512], k == 0, k == 7,
                                   [ykey, "wob"], [psk[b0 + hf]])
                        ci = crot.next()
                        for hf in range(2):
                            ACT(sqc[0:nr, :], ps[b0 + hf][0:nr, :], AF.Square, [psk[b0 + hf]], ["sqc"], accum=stc[0:nr, hf:hf + 1])
                        TT("dve", stc[0:nr, 2:3], stc[0:nr, 0:1], stc[0:nr, 1:2], ALU.add, ["sqc"], ["stc2"])
                        rstd_from_ssq(stc[0:nr, 4:5], stc[0:nr, 2:3], D, nr, ["stc2"], "stc4", stc[0:nr, 3:4])
                        for hf in range(2):
                            TT("dve", tt_[ci][0:nr, hf * 512:(hf + 1) * 512], ps[b0 + hf][0:nr, :], gg[0:nr, hf * 512:(hf + 1) * 512], ALU.mult,
                               [psk[b0 + hf], "ggP", "ggS"], [f"tt{ci}"])
                        STT(dst[0:nr, :], tt_[ci][0:nr, :], stc[0:nr, 4:5], xsrc[0:nr, :], ALU.mult, ALU.add, [f"tt{ci}", "stc4", xkey], [dkey])

                    for g in range(8):
                        yi = g % 2
                        P.dma(ytl[yi][:], yv[:, :, g * 512:(g + 1) * 512], reads=["YT"], writes=[f"ytl{yi}"])
                        for i in range(4):
                            t = g * 4 + i
                            xi = crot.i % 2
                            P.dma(xr[xi][:], xin[t * 128:(t + 1) * 128, :], reads=[f"x1_{t}"] if l > 0 else [], writes=[f"xr{xi}"])
                            out_tile(ytl[yi], f"ytl{yi}", i * 128, 128, xr[xi], f"xr{xi}", ggP, xo[xi], f"xo{xi}")
                            P.dma(xout[t * 128:(t + 1) * 128, :], xo[xi][:], reads=[f"xo{xi}"], writes=[f"x1_{t}"], queue="pool")
                            if wgen[0] is not None and t % 2 == 1:
                                next(wgen[0], None)
                    if wgen[0] is not None:
                        for _ in wgen[0]:
                            pass
                        wgen[0] = None
                    P.dma(ytl[0][:, :, 0:SS], yv[:, :, S:S + SS], reads=["YT"], writes=["ytl0"])
                    out_tile(ytl[0], "ytl0", 0, SS, xs, "xs", ggS, xs, "xs")
                    if l == depth - 1:
                        P.dma(y_s, xs[:], reads=["xs"])
                    P.barrier()
                    P.emit()

          except StopBuild:
            P.barrier()
            break
        P.finish()
        P.emit()
    return nc


_CACHE = {}


def _host_consts():
    c = np.zeros((128, 512 + 3 * 896), np.float32)
    idx = np.arange(128)
    c[:, 0:128] = np.eye(128, dtype=np.float32)
    c[:, 128:256] = (idx[:, None] <= idx[None, :]).astype(np.float32)
    c[:, 256:384] = 1.0
    c[:, 384:512] = -(idx[:, None] >= idx[None, :]).astype(np.float32)
    k = idx[:, None]
    q = idx[None, :]
    tri = [np.where(k > q, NEG, 0.0), np.where(k >= q, NEG, 0.0), np.where((k // 64) > (q // 64), NEG, 0.0)]
    for m in range(3):
        base = 512 + m * 896
        c[:, base:base + 384] = NEG
        c[:, base + 384:base + 512] = tri[m]
        c[:, base + 512:base + 896] = 0.0
    sel = np.zeros((2, 256), np.float32)
    sel[0, 0:128] = 1.0
    sel[1, 128:256] = 1.0
    half = 16
    inv = (10000.0 ** (-np.arange(half, dtype=np.float32) / half)).astype(np.float32)
    pos = np.arange(S + 128, dtype=np.float32)
    ang = pos[:, None] * inv[None, :]
    cs_tok = np.concatenate([np.cos(ang), np.sin(ang)], axis=1).astype(np.float32)
    cosT = np.cos(ang[:S]).T.astype(np.float32)
    sinT = np.sin(ang[:S]).T.astype(np.float32)
    ropeT = np.stack([np.concatenate([cosT, cosT], 0), np.concatenate([-sinT, sinT], 0)]).astype(np.float32)
    return c, sel, cs_tok, ropeT


def kernel(x_prompt, x_sample, c_prompt, c_sample, cache_fox_k, cache_fox_v, cache_fox_logf,
           cache_sb_k, cache_sb_v, cache_mla_ckv, cache_mla_kpe,
           g_pre, g_post, w_ada, b_ada, w_in, b_f, g_q_a, w_uq, g_kv_a, w_uk, w_uv, w_out):
    f = lambda a: np.ascontiguousarray(np.asarray(a, dtype=np.float32))
    if "nc" not in _CACHE:
        _CACHE["nc"] = build_program()
    nc = _CACHE["nc"]
    consts, sel, cs_tok, ropeT = _host_consts()
    perm = np.arange(768)
    for h in range(8):
        b0 = h * 96 + 64
        perm[b0:b0 + 16] = np.arange(b0 + 16, b0 + 32)
        perm[b0 + 16:b0 + 32] = np.arange(b0, b0 + 16)
    w_uq_f = f(w_uq)
    shared = {
        "g_pre": f(g_pre), "g_post": f(g_post), "w_ada": f(w_ada), "b_ada": f(b_ada), "w_in": f(w_in), "b_f": f(b_f),
        "g_q_a": f(g_q_a), "w_uq": w_uq_f, "w_uqs": np.ascontiguousarray(w_uq_f[:, :, perm]), "g_kv_a": f(g_kv_a),
        "w_uk": f(w_uk).reshape(DEPTH, 128, 512), "w_uv": f(w_uv).reshape(DEPTH, 128, 512), "w_out": f(w_out),
        "consts": consts, "selc": sel, "cs_tok": cs_tok, "ropeT": ropeT,
    }
    xp = f(x_prompt); xsm = f(x_sample); cp = f(c_prompt); csm = f(c_sample)
    cf = [f(a) for a in (cache_fox_k, cache_fox_v, cache_fox_logf, cache_sb_k, cache_sb_v, cache_mla_ckv, cache_mla_kpe)]
    in_maps = []
    for b in range(8):
        m = dict(shared)
        m["x_p"] = xp[b]
        m["x_s"] = xsm[b]
        m["c2"] = np.ascontiguousarray(np.stack([cp[b], csm[b]]))
        m["cfk"] = np.ascontiguousarray(cf[0][:, b].reshape(DEPTH, PAST, 256))
        m["cfv"] = np.ascontiguousarray(cf[1][:, b].reshape(DEPTH, PAST, 256))
        m["cfl"] = np.ascontiguousarray(cf[2][:, b])
        m["csk"] = np.ascontiguousarray(cf[3][:, b].reshape(DEPTH, PAST, 256))
        m["csv"] = np.ascontiguousarray(cf[4][:, b].reshape(DEPTH, PAST, 256))
        m["cck"] = np.ascontiguousarray(cf[5][:, b])
        m["ckp"] = np.ascontiguousarray(cf[6][:, b])
        in_maps.append(m)
    res = run_bass_kernel_spmd(nc, in_maps, core_ids=list(range(8)))
    R = res.results

    def gat(name, shape):
        return np.stack([np.asarray(R[b][name], dtype=np.float32) for b in range(8)], axis=1).reshape(shape)

    y_prompt = np.stack([np.asarray(R[b]["y_p"], dtype=np.float32) for b in range(8)])
    y_sample = np.stack([np.asarray(R[b]["y_s"], dtype=np.float32) for b in range(8)])
    outs = [y_prompt, y_sample]
    for pre, T in (("p", S), ("s", SS)):
        outs.append(gat(f"{pre}_fk", (DEPTH, 8, T, 4, 64)))
        outs.append(gat(f"{pre}_fv", (DEPTH, 8, T, 4, 64)))
        outs.append(gat(f"{pre}_fl", (DEPTH, 8, T, 4)))
        outs.append(gat(f"{pre}_sk", (DEPTH, 8, T, 4, 64)))
        outs.append(gat(f"{pre}_sv", (DEPTH, 8, T, 4, 64)))
        outs.append(gat(f"{pre}_ck", (DEPTH, 8, T, 128)))
        outs.append(gat(f"{pre}_kp", (DEPTH, 8, T, 32)))
    return tuple(outs)
```

```python
from contextlib import ExitStack
import numpy as np
import concourse.bass as bass
import concourse.mybir as mybir
from concourse.bass_utils import run_bass_kernel_spmd

F32 = mybir.dt.float32
BF16 = mybir.dt.bfloat16
ALU = mybir.AluOpType
AF = mybir.ActivationFunctionType
AX = mybir.AxisListType

N_DMA_SEMS = 56
SEM_EPOCH = 30000

S = 4096
D = 1024
SS = 16
PAST = 2048
TA = S + SS
DEPTH = 2
C_FQ, C_FK, C_FV, C_FF, C_FG = 0, 256, 512, 768, 772
C_SQ, C_SK, C_SV, C_SG = 1028, 1284, 1540, 1796
C_CQ, C_CKV, C_KPE, C_MG = 2052, 2308, 2436, 2468
INW = 2980
NEG = -240000.0
EPS = 1e-6
MLA_SCALE = 96.0 ** -0.5


class Prog:
    ENGS = ("pe", "act", "dve", "pool", "sp")

    def __init__(self, nc, stack):
        self.nc = nc
        self.stack = stack
        self.streams = {e: [] for e in self.ENGS}
        self.sems = {}
        self.cur_sem = {}
        self.count = {}
        self.epoch = {e: 0 for e in self.ENGS}
        for e in self.ENGS:
            self._new_epoch(e)
        self.waited = {e: {} for e in self.ENGS}
        self.res = {}
        self.dma_slots = []
        for i in range(N_DMA_SEMS):
            k = ("dma", i)
            self.sems[k] = stack.enter_context(nc.semaphore(f"dma{i}"))
            self.count[k] = 0
            self.dma_slots.append(k)
        self.dma_rr = 0
        self.n_ops = 0

    def _new_epoch(self, e):
        k = (e, self.epoch[e])
        self.sems[k] = self.stack.enter_context(self.nc.semaphore(f"s_{e}_{self.epoch[e]}"))
        self.count[k] = 0
        self.cur_sem[e] = k
        self.epoch[e] += 1

    def _deps(self, reads, writes):
        ev = []
        for r in reads:
            st = self.res.get(r)
            if st and st[0] is not None:
                ev.append(st[0])
        for w in writes:
            st = self.res.get(w)
            if st:
                if st[0] is not None:
                    ev.append(st[0])
                ev.extend(st[1])
        return ev

    def _record(self, reads, writes, event):
        for r in reads:
            st = self.res.setdefault(r, [None, []])
            st[1].append(event)
            if len(st[1]) > 48:
                best = {}
                for (k, v) in st[1]:
                    if best.get(k, -1) < v:
                        best[k] = v
                st[1] = list(best.items())
        for w in writes:
            self.res[w] = [event, []]

    def _waits_for(self, eng, events, skip_self):
        need = {}
        for (k, v) in events:
            if skip_self and k[0] == eng:
                continue
            if need.get(k, 0) < v:
                need[k] = v
        out = []
        wd = self.waited[eng]
        for k, v in need.items():
            if wd.get(k, 0) >= v:
                continue
            wd[k] = v
            out.append((self.sems[k], v))
        return out

    def op(self, eng, fn, reads=(), writes=(), signal=True):
        self.n_ops += 1
        pr = [r for r in reads if r.startswith("ps")]
        if pr:
            reads = [r for r in reads if not r.startswith("ps")]
            writes = list(writes) + [p for p in pr if p not in writes]
        events = self._deps(reads, writes)
        waits = self._waits_for(eng, events, skip_self=(eng == "pe"))
        k = self.cur_sem[eng]
        if signal:
            self.count[k] += 1
            ev = (k, self.count[k])
            if self.count[k] >= SEM_EPOCH:
                self._new_epoch(eng)
        else:
            ev = (k, self.count[k] + 1)
        sem = self.sems[k]

        def run(e, waits=waits, fn=fn, sem=sem, signal=signal):
            for (s, v) in waits:
                e.wait_ge(s, v)
            ins = fn(e)
            if signal:
                ins.then_inc(sem, 1)
        self.streams[eng].append(run)
        self._record(reads, writes, ev)
        return ev

    def dma(self, out, in_, reads=(), writes=(), queue="sp", **kw):
        self.n_ops += 1
        slot = self.dma_slots[self.dma_rr % N_DMA_SEMS]
        self.dma_rr += 1
        events = self._deps(reads, writes)
        if self.count[slot] > 0:
            events.append((slot, self.count[slot]))
        waits = self._waits_for(queue, events, skip_self=False)
        self.count[slot] += 16
        ev = (slot, self.count[slot])
        sem = self.sems[slot]

        def run(e, waits=waits, sem=sem, out=out, in_=in_, kw=kw):
            for (s, v) in waits:
                e.wait_ge(s, v)
            e.dma_start(out=out, in_=in_, **kw).then_inc(sem, 16)
        self.streams[queue].append(run)
        self._record(reads, writes, ev)
        return ev

    def barrier(self):
        events = [(k, c) for k, c in self.count.items() if c > 0]
        for e in self.ENGS:
            waits = self._waits_for(e, events, skip_self=True)

            def run(eo, waits=waits):
                for (s, v) in waits:
                    eo.wait_ge(s, v)
            self.streams[e].append(run)
        self.res = {}

    def finish(self):
        events = [(k, c) for k, c in self.count.items() if c > 0]
        waits = self._waits_for("sp", events, skip_self=True)

        def run(eo, waits=waits):
            for (s, v) in waits:
                eo.wait_ge(s, v)
        self.streams["sp"].append(run)

    def emit(self):
        nc = self.nc
        streams = self.streams
        self.streams = {e: [] for e in self.ENGS}
        with nc.Block() as block:
            @block.tensor
            def _(e):
                for f in streams["pe"]:
                    f(e)

            @block.scalar
            def _(e):
                for f in streams["act"]:
                    f(e)

            @block.vector
            def _(e):
                for f in streams["dve"]:
                    f(e)

            @block.gpsimd
            def _(e):
                for f in streams["pool"]:
                    f(e)

            @block.sync
            def _(e):
                for f in streams["sp"]:
                    f(e)


class StopBuild(Exception):
    pass


STOP = [None]
PLIM = [9]


def ckpt(name):
    if STOP[0] == name:
        raise StopBuild()


class Rot:
    def __init__(self, items):
        self.items = items
        self.i = 0

    def next(self):
        it = self.items[self.i % len(self.items)]
        self.i += 1
        return it


def build_program(depth=DEPTH, do_b=True, do_c=True):
    nc = bass.Bass("TRN2", target_bir_lowering=False)

    def din(name, shape, dt=F32):
        return nc.dram_tensor(name, shape, dt, kind="ExternalInput").ap()

    def dout(name, shape):
        return nc.dram_tensor(name, shape, F32, kind="ExternalOutput").ap()

    def dscr(name, shape, dt=BF16):
        return nc.dram_tensor(name, shape, dt, kind="Internal").ap()

    x_p = din("x_p", [S, D]); x_s = din("x_s", [SS, D]); c2 = din("c2", [2, D])
    cfk = din("cfk", [DEPTH, PAST, 256]); cfv = din("cfv", [DEPTH, PAST, 256]); cfl = din("cfl", [DEPTH, PAST, 4])
    csk = din("csk", [DEPTH, PAST, 256]); csv = din("csv", [DEPTH, PAST, 256])
    cck = din("cck", [DEPTH, PAST, 128]); ckp = din("ckp", [DEPTH, PAST, 32])
    g_pre = din("g_pre", [DEPTH, D]); g_post = din("g_post", [DEPTH, D])
    w_ada = din("w_ada", [DEPTH, D, 3 * D]); b_ada = din("b_ada", [DEPTH, 3 * D])
    w_in = din("w_in", [DEPTH, D, INW]); b_f = din("b_f", [DEPTH, 4]); g_q_a = din("g_q_a", [DEPTH, 256])
    w_uq = din("w_uq", [DEPTH, 256, 768]); w_uqs = din("w_uqs", [DEPTH, 256, 768])
    g_kv_a = din("g_kv_a", [DEPTH, 128]); w_uk = din("w_uk", [DEPTH, 128, 512]); w_uv = din("w_uv", [DEPTH, 128, 512])
    w_out = din("w_out", [DEPTH, D, D])
    consts = din("consts", [128, 512 + 3 * 896]); selc = din("selc", [2, 256])
    cs_tok = din("cs_tok", [S + 128, 32]); ropeT = din("ropeT", [2, 32, S])

    y_p = dout("y_p", [S, D]); y_s = dout("y_s", [SS, D])
    o_fk = [dout("p_fk", [DEPTH, S, 256]), dout("s_fk", [DEPTH, SS, 256])]
    o_fv = [dout("p_fv", [DEPTH, S, 256]), dout("s_fv", [DEPTH, SS, 256])]
    o_fl = [dout("p_fl", [DEPTH, S, 4]), dout("s_fl", [DEPTH, SS, 4])]
    o_sk = [dout("p_sk", [DEPTH, S, 256]), dout("s_sk", [DEPTH, SS, 256])]
    o_sv = [dout("p_sv", [DEPTH, S, 256]), dout("s_sv", [DEPTH, SS, 256])]
    o_ck = [dout("p_ck", [DEPTH, S, 128]), dout("s_ck", [DEPTH, SS, 128])]
    o_kp = [dout("p_kp", [DEPTH, S, 32]), dout("s_kp", [DEPTH, SS, 32])]

    QF = dscr("QF", [256, TA]); KF = dscr("KF", [256, TA]); GF = dscr("GF", [256, TA]); FQ = dscr("FQ", [4, TA])
    VF = dscr("VF", [TA + 112, 4, 128])
    QB = dscr("QB", [256, TA]); KB = dscr("KB", [256, TA]); GB = dscr("GB", [256, TA]); VB = dscr("VB", [TA + 112, 256])
    QM = dscr("QM", [8, 96, TA]); KM = dscr("KM", [512, TA]); KP = dscr("KP", [32, TA]); GM = dscr("GM", [512, TA])
    VM = dscr("VM", [TA + 112, 8, 128])
    YT = dscr("YT", [D, TA])
    x1 = dscr("x1", [S, D], F32)

    with ExitStack() as gst:
        P = Prog(nc, gst)

        uid = [0]

        def sbt(st, name, shape, dt):
            uid[0] += 1
            return st.enter_context(nc.sbuf_tensor(f"{name}_u{uid[0]}", shape, dt))

        ps = [gst.enter_context(nc.psum_tensor(f"ps{i}", [128, 512], F32)) for i in range(8)]
        psk = [f"ps{i}" for i in range(8)]
        jrot = Rot([0, 1, 2, 3])

        def MM(out, lhsT, rhs, start, stop, r, w, sig=None):
            if sig is None:
                sig = stop
            P.op("pe", lambda e: e.matmul(out, lhsT=lhsT, rhs=rhs, start=start, stop=stop), reads=r, writes=w, signal=sig)

        def TR(out, in_, ident, r, w):
            P.op("pe", lambda e: e.transpose(out, in_, ident), reads=r, writes=w)

        def ACT(out, in_, func, r, w, bias=None, scale=None, accum=None):
            kw = {}
            if bias is not None:
                kw["bias"] = bias
            if scale is not None:
                kw["scale"] = scale
            if accum is not None:
                kw["accum_out"] = accum
            P.op("act", lambda e: e.activation(out=out, in_=in_, func=func, **kw), reads=r, writes=w)

        def TT(eng, out, in0, in1, op, r, w):
            P.op(eng, lambda e: e.tensor_tensor(out=out, in0=in0, in1=in1, op=op), reads=r, writes=w)

        def TS(eng, out, in0, s1, s2, op0, op1, r, w):
            if op1 is None:
                P.op(eng, lambda e: e.tensor_scalar(out=out, in0=in0, scalar1=s1, scalar2=None, op0=op0), reads=r, writes=w)
            else:
                P.op(eng, lambda e: e.tensor_scalar(out=out, in0=in0, scalar1=s1, scalar2=s2, op0=op0, op1=op1), reads=r, writes=w)

        def STT(out, in0, scalar, in1, op0, op1, r, w):
            P.op("dve", lambda e: e.scalar_tensor_tensor(out=out, in0=in0, scalar=scalar, in1=in1, op0=op0, op1=op1), reads=r, writes=w)

        def CP(eng, out, in_, r, w):
            if eng == "act":
                P.op("act", lambda e: e.copy(out=out, in_=in_), reads=r, writes=w)
            else:
                P.op(eng, lambda e: e.tensor_copy(out=out, in_=in_), reads=r, writes=w)

        def MEMSET(eng, ap, val, w):
            P.op(eng, lambda e: e.memset(ap, val), writes=w)

        evrot = Rot(["act", "dve"])

        cst = sbt(gst, "cst", [128, 512], F32)
        cstb = sbt(gst, "cstb", [128, 384], BF16)
        mk = sbt(gst, "mk", [128, 3 * 896], BF16)
        selt = sbt(gst, "selt", [2, 256], F32)
        zt = sbt(gst, "zt", [128, 128], F32)
        zb = sbt(gst, "zb", [128, 128], BF16)
        epst = sbt(gst, "epst", [128, 1], F32)
        P.dma(cst[:], consts[:, 0:512], writes=["cst"])
        P.dma(selt[:], selc, writes=["selt"])
        with ExitStack() as st0:
            mkf = sbt(st0, "mkf", [128, 3 * 896], F32)
            P.dma(mkf[:], consts[:, 512:512 + 3 * 896], writes=["mkf"])
            CP("dve", mk[:], mkf[:], ["mkf"], ["mk"])
            CP("pool", cstb[:, 0:128], cst[:, 0:128], ["cst"], ["cstb"])
            CP("pool", cstb[:, 128:256], cst[:, 384:512], ["cst"], ["cstb"])
            TS("dve", cstb[:, 256:384], cst[:, 256:384], -1.0, None, ALU.mult, None, ["cst"], ["cstb"])
            MEMSET("pool", zt[:], 0.0, ["zt"])
            MEMSET("pool", zb[:], 0.0, ["zb"])
            MEMSET("pool", epst[:], EPS, ["epst"])
            P.barrier()
            P.emit()
        ident = cst[:, 0:128]
        triF = cst[:, 128:256]
        onesF = cst[:, 256:384]
        identb = cstb[:, 0:128]
        tinclneg = cstb[:, 128:256]
        onesneg = cstb[:, 256:384]

        xs = sbt(gst, "xs", [SS, D], F32)
        P.dma(xs[:], x_s, writes=["xs"])
        negF = sbt(gst, "negF", [128, 34, 4], F32)
        negFp = sbt(gst, "negFp", [128, 16, 4], F32)
        prmT = sbt(gst, "prmT", [128, 42], F32)
        modS = sbt(gst, "modS", [128, 2, 16], F32)
        aPS = sbt(gst, "aPS", [128, 2, 8], F32)
        ggP = sbt(gst, "ggP", [128, D], F32)
        ggS = sbt(gst, "ggS", [SS, D], F32)
        bfb = sbt(gst, "bfb", [128, 4, 4], F32)
        gkvb = sbt(gst, "gkvb", [128, 128], F32)
        wuk = sbt(gst, "wuk", [128, 512], BF16)
        wuv = sbt(gst, "wuv", [128, 512], BF16)

        def rstd_from_ssq(out, ssq, n, rows, rk, wk, tmp):
            ACT(tmp, ssq, AF.Ln, rk, [wk + "_t"], bias=epst[0:rows, :], scale=1.0 / n)
            ACT(out, tmp, AF.Exp, [wk + "_t"], [wk], scale=-0.5)

        def alloc_weights(wst):
            return dict(win=sbt(wst, "win", [128, 8, INW], BF16), wuq=sbt(wst, "wuq", [128, 2, 768], BF16),
                        wuqs=sbt(wst, "wuqs", [128, 2, 768], BF16), stg=[sbt(wst, f"stg{i}", [128, 2048], F32) for i in range(2)])

        def load_weights(l, wd, engines):
            win, wuq, wuqs, stg = wd["win"], wd["wuq"], wd["wuqs"], wd["stg"]
            srot = Rot([0, 1])
            castrot = Rot(list(engines))
            wv = w_in[l].rearrange("(k p) c -> p k c", p=128)
            for c0 in range(0, INW, 256):
                cw = min(256, INW - c0)
                si = srot.next()
                sv = stg[si][:, 0:8 * cw].rearrange("p (k c) -> p k c", k=8)
                P.dma(sv, wv[:, :, c0:c0 + cw], writes=[f"stg{si}"], queue="pool")
                CP(castrot.next(), win[:, :, c0:c0 + cw], sv, [f"stg{si}"], ["win"])
                yield
            for (dst, src_, nm) in ((wuq, w_uq, "wuq"), (wuqs, w_uqs, "wuqs")):
                si = srot.next()
                sv = stg[si][:, 0:1536].rearrange("p (k c) -> p k c", k=2)
                P.dma(sv, src_[l].rearrange("(k p) c -> p k c", p=128), writes=[f"stg{si}"], queue="pool")
                CP(castrot.next(), dst[:], sv, [f"stg{si}"], [nm])
                yield
            for (dst, src_, nm) in ((wuk, w_uk, "wuk"), (wuv, w_uv, "wuv")):
                si = srot.next()
                P.dma(stg[si][:, 0:512], src_[l], writes=[f"stg{si}"], queue="pool")
                CP(castrot.next(), dst[:], stg[si][:, 0:512], [f"stg{si}"], [nm])
                yield

        wst_cur = [ExitStack()]
        cur_w = [alloc_weights(wst_cur[0])]
        for _ in load_weights(0, cur_w[0], ("pool", "dve")):
            pass
        wgen = [None]

        for l in range(depth):
          try:
            xin = x_p if l == 0 else x1
            xout = y_p if l == depth - 1 else x1

            ckpt("setup")
            with ExitStack() as st:
                prm = sbt(st, "prm", [42, 128], F32)
                t16 = sbt(st, "t16", [16, 128], F32)
                t16b = sbt(st, "t16b", [16, 128], F32)
                wa = [sbt(st, f"wa{i}", [128, 3 * D], F32) for i in range(2)]
                grow = sbt(st, "grow", [2, D], F32)
                badag = sbt(st, "badag", [2, D], F32)
                gpb = sbt(st, "gpb", [128, D], F32)
                P.dma(prm[0:16, :], c2.rearrange("j (k p) -> (j k) p", p=128), writes=["prm"])
                P.dma(prm[16:24, :], g_pre[l].rearrange("(k p) -> k p", p=128), writes=["prm"])
                P.dma(prm[24:40, :], b_ada[l, 0:2048].rearrange("(k p) -> k p", p=128), writes=["prm"])
                P.dma(prm[40:42, :], g_q_a[l].rearrange("(k p) -> k p", p=128), writes=["prm"])
                P.dma(badag[:], b_ada[l, 2048:3072].partition_broadcast(2), writes=["badag"])
                P.dma(gpb[:], g_post[l].partition_broadcast(128), writes=["gpb"])
                for j4 in range(4):
                    P.dma(bfb[:, j4, :], b_f[l].partition_broadcast(128), writes=["bfb"])
                P.dma(gkvb[:], g_kv_a[l].partition_broadcast(128), writes=["gkvb"])
                ACT(t16[:], prm[0:16, :], AF.Exp, ["prm"], ["t16"], scale=-1.0)
                ACT(t16b[:], t16[:], AF.Ln, ["t16"], ["t16b"], bias=1.0)
                ACT(t16[:], t16b[:], AF.Exp, ["t16b"], ["t16"], scale=-1.0)
                TT("dve", prm[0:16, :], t16[:], prm[0:16, :], ALU.mult, ["t16", "prm"], ["prm"])
                TR(ps[6][:, 0:42], prm[0:42, :], cst[0:42, 0:42], ["prm", "cst"], ["ps6"])
                CP("dve", prmT[:], ps[6][:, 0:42], ["ps6"], ["prmT"])
                MM(ps[7][:, 0:32], zt[:], zt[:, 0:32], True, False, ["zt"], ["ps7"], sig=False)
                for k in range(8):
                    w_t = wa[k % 2]
                    wk = f"wa{k % 2}"
                    P.dma(w_t[:], w_ada[l, k * 128:(k + 1) * 128, :], writes=[wk])
                    sck = prmT[:, k:16:8]
                    for fc in range(16):
                        MM(ps[7][:, 2 * fc:2 * fc + 2], w_t[:, fc * 128:(fc + 1) * 128], sck, False, (k == 7 and fc == 15),
                           [wk, "prmT"], ["ps7"], sig=(fc == 15))
                    MM(ps[4][0:2, :], sck, w_t[:, 2048:2560], k == 0, k == 7, [wk, "prmT"], ["ps4"], sig=True)
                    MM(ps[5][0:2, :], sck, w_t[:, 2560:3072], k == 0, k == 7, [wk, "prmT"], ["ps5"], sig=True)
                for j in range(2):
                    TT("dve", modS[:, j, :], ps[7][:, j:32:2], prmT[:, 24:40], ALU.add, ["ps7", "prmT"], ["modS"])
                    STT(aPS[:, j, :], modS[:, j, 8:16], 1.0, prmT[:, 16:24], ALU.add, ALU.mult, ["modS", "prmT"], ["aPS"])
                TT("dve", grow[:, 0:512], ps[4][0:2, :], badag[:, 0:512], ALU.add, ["ps4", "badag"], ["grow"])
                TT("dve", grow[:, 512:1024], ps[5][0:2, :], badag[:, 512:1024], ALU.add, ["ps5", "badag"], ["grow"])
                for hf in range(2):
                    MM(ps[0 + hf][:, :], selt[0:2, 0:128], grow[0:2, hf * 512:(hf + 1) * 512], True, True, ["selt", "grow"], [psk[hf]])
                    TT("dve", ggP[:, hf * 512:(hf + 1) * 512], ps[hf][:, :], gpb[:, hf * 512:(hf + 1) * 512], ALU.mult,
                       [psk[hf], "gpb"], ["ggP"])
                    MM(ps[2 + hf][0:SS, :], selt[0:2, 128:128 + SS], grow[0:2, hf * 512:(hf + 1) * 512], True, True,
                       ["selt", "grow"], [psk[2 + hf]])
                    TT("dve", ggS[:, hf * 512:(hf + 1) * 512], ps[2 + hf][0:SS, :], gpb[0:SS, hf * 512:(hf + 1) * 512], ALU.mult,
                       [psk[2 + hf], "gpb"], ["ggS"])
                P.barrier()
                P.emit()

            ckpt("adaln")
            with ExitStack() as st:
                win, wuq, wuqs = cur_w[0]["win"], cur_w[0]["wuq"], cur_w[0]["wuqs"]
                cstok = sbt(st, "cstok", [128, 33, 32], F32)
                for q4 in range(4):
                    P.dma(cstok[:, q4 * 8:(q4 + 1) * 8, :],
                          cs_tok[q4 * 1024:(q4 + 1) * 1024, :].rearrange("(t p) c -> p t c", p=128), writes=["cstok"])
                P.dma(cstok[:, 32, :], cs_tok[PAST:PAST + 128, :], writes=["cstok"])

                xts = [sbt(st, f"xt{i}", [128, D], F32) for i in range(2)]
                xns = [sbt(st, f"xn{i}", [128, D], BF16) for i in range(4)]
                sqj = sbt(st, "sqj", [128, D], BF16)
                stat = sbt(st, "stat", [128, 16], F32)
                HTg = [sbt(st, f"HTg{i}", [128, 8, 512], BF16) for i in range(2)]
                stb = [sbt(st, f"stb{i}", [128, 512], BF16) for i in range(6)]
                stbrot = Rot([0, 1, 2, 3, 4, 5])
                kvst = [sbt(st, f"kvst{i}", [128, 512], F32) for i in range(2)]
                kvrot = Rot([0, 1])
                vst = [sbt(st, f"vst{i}", [128, 8, 128], BF16) for i in range(2)]
                vrot = Rot([0, 1])
                for i in range(2):
                    MEMSET("pool", vst[i][:, :, 64:128], 2.0, [f"vst{i}"])
                ckst = [sbt(st, f"ckst{i}", [128, 160], F32) for i in range(4)]
                ckrot = Rot([0, 1])
                rt = [sbt(st, f"rt{i}", [128, 32], F32) for i in range(4)]
                lf = sbt(st, "lf", [128, 8, 4], F32)
                Rrun = sbt(st, "Rrun", [128, 4], F32)
                f8 = sbt(st, "f8", [128, 4, 4], F32)
                fqs = sbt(st, "fqs", [4, 512], BF16)
                ckTg = sbt(st, "ckTg", [128, 512], BF16)
                kpTg = sbt(st, "kpTg", [32, 512], BF16)
                cqT = sbt(st, "cqT", [128, 2, 512], F32)
                sqb = sbt(st, "sqb", [128, 2, 512], BF16)
                cqg = sbt(st, "cqg", [128, 2, 512], BF16)
                rsb = sbt(st, "rsb", [128, 512], F32)
                rsbt = sbt(st, "rsbt", [128, 512], F32)
                ropet = sbt(st, "ropet", [128, 2, 512], F32)
                tq1 = [sbt(st, f"tq1_{i}", [128, 512], F32) for i in range(2)]
                tq2 = [sbt(st, f"tq2_{i}", [128, 512], F32) for i in range(2)]
                tqrot = Rot([0, 1])
                qhrot = Rot([0, 1, 2, 3, 4, 5, 6])
                g1s = [sbt(st, f"g1_{i}", [128, 512], F32) for i in range(2)]
                g2s = [sbt(st, f"g2_{i}", [128, 512], F32) for i in range(2)]
                grot = Rot([0, 1])
                lfp = sbt(st, "lfp", [128, 16, 4], F32)
                Rp = sbt(st, "Rp", [128, 17, 4], F32)
                qst = [sbt(st, f"qst{i}", [96, 512], BF16) for i in range(3)]
                qrot = Rot([0, 1, 2])
                xrot = Rot([0, 1])

                class PG:
                    def __init__(self, gi, W, tiles, sample):
                        self.gi, self.W, self.tiles, self.sample = gi, W, tiles, sample
                        self.si = 1 if sample else 0
                        self.c0 = S if sample else gi * 512
                        self.pos0 = PAST if sample else gi * 512
                        self.HT = HTg[gi % 2]
                        self.hk = f"HTg{gi % 2}"
                        self.aP = aPS[:, self.si, :]
                        self.bP = modS[:, self.si, 0:8]
                        self.xt = []

                def put(dst_ap, bf_ap, key):
                    P.dma(dst_ap, bf_ap, reads=[key], writes=[])

                def P_A1(g):
                    for i, (r0, nr) in enumerate(g.tiles):
                        if g.sample:
                            xt, xk = xs, "xs"
                        else:
                            xi = xrot.next()
                            xt, xk = xts[xi], f"xt{xi}"
                            P.dma(xt[:], xin[r0:r0 + nr, :], reads=[f"x1_{r0 // 128}"] if l > 0 else [], writes=[xk])
                        ACT(sqj[0:nr, :], xt[0:nr, :], AF.Square, [xk], ["sqj"], accum=stat[0:nr, i:i + 1])
                        rstd_from_ssq(stat[0:nr, 8 + i:9 + i], stat[0:nr, i:i + 1], D, nr, ["sqj"], f"stat{i}", stat[0:nr, 4 + i:5 + i])
                        TS("dve", xns[i][0:nr, :], xt[0:nr, :], stat[0:nr, 8 + i:9 + i], None, ALU.mult, None, [xk, f"stat{i}"], [f"xn{i}"])

                def P_A2(g):
                    W, HT, hk = g.W, g.HT, g.hk
                    for k in range(8):
                        b = jrot.next()
                        psb = ps[b][:].bitcast(BF16)
                        for i, (r0, nr) in enumerate(g.tiles):
                            TR(psb[:, i * 128:i * 128 + nr], xns[i][0:nr, k * 128:(k + 1) * 128], cstb[0:nr, 0:nr],
                               [f"xn{i}", "cstb"], [psk[b]])
                        if k % 2 == 0:
                            ACT(HT[:, k, 0:W], psb[:, 0:W], AF.Identity, [psk[b], "aPS", "modS"], [hk],
                                bias=g.bP[:, k:k + 1], scale=g.aP[:, k:k + 1])
                        else:
                            TS("dve", HT[:, k, 0:W], psb[:, 0:W], g.aP[:, k:k + 1], g.bP[:, k:k + 1], ALU.mult, ALU.add,
                               [psk[b], "aPS", "modS"], [hk])

                jrotB = Rot([0, 1, 2, 3, 4, 5, 6])

                def P_B(g):
                    W, HT, hk, c0, si_, sample = g.W, g.HT, g.hk, g.c0, g.si, g.sample
                    jrot = jrotB

                    def projT(col0, M, b):
                        for k in range(8):
                            MM(ps[b][0:M, 0:W], win[:, k, col0:col0 + M], HT[:, k, 0:W], k == 0, k == 7, ["win", hk], [psk[b]])

                    P.dma(ropet[64:96, 0, 0:W], ropeT[0, :, g.pos0:g.pos0 + W], writes=["ropet"])
                    P.dma(ropet[64:96, 1, 0:W], ropeT[1, :, g.pos0:g.pos0 + W], writes=["ropet"])
                    gt = [(col0 + mt * 128, dst, mt) for (col0, dst, nmt) in ((C_FG, GF, 2), (C_SG, GB, 2), (C_MG, GM, 4)) for mt in range(nmt)]
                    for p0 in range(0, len(gt), 2):
                        pr = gt[p0:p0 + 2]
                        bs = []
                        for (cc, dst, mt) in pr:
                            b = jrot.next()
                            projT(cc, 128, b)
                            bs.append(b)
                        for gi_, b in enumerate(bs):
                            ACT(g1s[gi_][:, 0:W], ps[b][:, 0:W], AF.Exp, [psk[b]], [f"g1_{gi_}"], scale=-1.0)
                        for gi_, b in enumerate(bs):
                            ACT(g2s[gi_][:, 0:W], g1s[gi_][:, 0:W], AF.Ln, [f"g1_{gi_}"], [f"g2_{gi_}"], bias=1.0)
                        for gi_, b in enumerate(bs):
                            ACT(g1s[gi_][:, 0:W], g2s[gi_][:, 0:W], AF.Exp, [f"g2_{gi_}"], [f"g1_{gi_}"], scale=-1.0)
                        for gi_, (b, (cc, dst, mt)) in enumerate(zip(bs, pr)):
                            sb_i = stbrot.next()
                            TT("dve", stb[sb_i][:, 0:W], ps[b][:, 0:W], g1s[gi_][:, 0:W], ALU.mult, [psk[b], f"g1_{gi_}"], [f"stb{sb_i}"])
                            put(dst[mt * 128:(mt + 1) * 128, c0:c0 + W], stb[sb_i][:, 0:W], f"stb{sb_i}")
                    for (col0, dst, scl) in ((C_FQ, QF, None), (C_FK, KF, None), (C_SQ, QB, 0.125), (C_SK, KB, None)):
                        for mt in range(2):
                            b = jrot.next()
                            projT(col0 + mt * 128, 128, b)
                            sb_i = stbrot.next()
                            eng = evrot.next()
                            if scl is not None:
                                TS("dve", stb[sb_i][:, 0:W], ps[b][:, 0:W], scl, None, ALU.mult, None, [psk[b]], [f"stb{sb_i}"])
                            else:
                                CP(eng, stb[sb_i][:, 0:W], ps[b][:, 0:W], [psk[b]], [f"stb{sb_i}"])
                            put(dst[mt * 128:(mt + 1) * 128, c0:c0 + W], stb[sb_i][:, 0:W], f"stb{sb_i}")
                    for c in range(2):
                        b = jrot.next()
                        projT(C_CQ + c * 128, 128, b)
                        CP("act", cqT[:, c, 0:W], ps[b][:, 0:W], [psk[b]], ["cqT"])
                        ACT(sqb[:, c, 0:W], ps[b][:, 0:W], AF.Square, [psk[b]], ["sqb"])
                    bL = 7
                    nt = len(g.tiles)
                    nr0 = g.tiles[0][1]
                    for i, (r0, nr) in enumerate(g.tiles):
                        for k in range(8):
                            MM(ps[bL][0:nr, i * 4:i * 4 + 4], HT[:, k, i * 128:i * 128 + nr], win[:, k, C_FF:C_FF + 4], k == 0, k == 7,
                               ["win", hk], [psk[bL]])
                    TT("dve", lf[0:nr0, 0:nt, :], ps[bL][0:nr0, 0:nt * 4].rearrange("p (t h) -> p t h", t=nt), bfb[0:nr0, 0:nt, :], ALU.add,
                       [psk[bL], "bfb"], ["lf"])
                    ACT(lf[0:nr0, 4:4 + nt, :], lf[0:nr0, 0:nt, :], AF.Exp, ["lf"], ["lfe"], scale=-1.0)
                    ACT(lf[0:nr0, 0:nt, :], lf[0:nr0, 4:4 + nt, :], AF.Ln, ["lfe"], ["lf"], bias=1.0)
                    TS("dve", lf[0:nr0, 0:nt, :], lf[0:nr0, 0:nt, :], -1.0, None, ALU.mult, None, ["lf"], ["lf"])
                    for i, (r0, nr) in enumerate(g.tiles):
                        P.dma(o_fl[si_][l, r0:r0 + nr, :], lf[0:nr, i, :], reads=["lf"])
                    for i, (r0, nr) in enumerate(g.tiles):
                        orow = slice(r0, r0 + nr)
                        srow = slice(c0 + i * 128, c0 + i * 128 + nr)
                        lhs = lambda k: HT[:, k, i * 128:i * 128 + nr]
                        for (colk, okk, ovv, isfox) in ((C_FK, o_fk, o_fv, True), (C_SK, o_sk, o_sv, False)):
                            b = jrot.next()
                            for k in range(8):
                                MM(ps[b][0:nr, :], lhs(k), win[:, k, colk:colk + 512], k == 0, k == 7, ["win", hk], [psk[b]])
                            ki = kvrot.next()
                            CP("act", kvst[ki][0:nr, :], ps[b][0:nr, :], [psk[b]], [f"kvst{ki}"])
                            P.dma(okk[si_][l, orow, :], kvst[ki][0:nr, 0:256], reads=[f"kvst{ki}"])
                            P.dma(ovv[si_][l, orow, :], kvst[ki][0:nr, 256:512], reads=[f"kvst{ki}"])
                            if isfox:
                                vi = vrot.next()
                                CP("dve", vst[vi][0:nr, 0:4, 0:64], ps[b][0:nr, 256:512].rearrange("p (h d) -> p h d", h=4),
                                   [psk[b]], [f"vst{vi}"])
                                P.dma(VF[srow, :, :], vst[vi][0:nr, 0:4, :], reads=[f"vst{vi}"])
                            else:
                                sb_i = stbrot.next()
                                CP("dve", stb[sb_i][0:nr, 0:256], ps[b][0:nr, 256:512], [psk[b]], [f"stb{sb_i}"])
                                P.dma(VB[srow, :], stb[sb_i][0:nr, 0:256], reads=[f"stb{sb_i}"])
                        b = jrot.next()
                        for k in range(8):
                            MM(ps[b][0:nr, 0:160], lhs(k), win[:, k, C_CKV:C_CKV + 160], k == 0, k == 7, ["win", hk], [psk[b]])
                        r_ = rt[i]
                        rk = f"rt{i}"
                        ACT(sqj[0:nr, 0:128], ps[b][0:nr, 0:128], AF.Square, [psk[b]], ["sqj"], accum=r_[0:nr, 0:1])
                        rstd_from_ssq(r_[0:nr, 2:3], r_[0:nr, 0:1], 128, nr, ["sqj"], rk, r_[0:nr, 1:2])
                        ck = ckst[i]
                        ckk = f"ckst{i}"
                        STT(ck[0:nr, 0:128], ps[b][0:nr, 0:128], r_[0:nr, 2:3], gkvb[0:nr, :], ALU.mult, ALU.mult, [psk[b], rk, "gkvb"], [ckk])
                        cosv = cstok[0:nr, (32 if sample else r0 // 128), 0:16]
                        sinv = cstok[0:nr, (32 if sample else r0 // 128), 16:32]
                        x1p = ps[b][0:nr, 128:144]
                        x2p = ps[b][0:nr, 144:160]
                        TT("dve", ck[0:nr, 128:144], x1p, cosv, ALU.mult, [psk[b], "cstok"], [ckk])
                        TT("dve", r_[0:nr, 16:32], x2p, sinv, ALU.mult, [psk[b], "cstok"], [rk + "b"])
                        TT("dve", ck[0:nr, 128:144], ck[0:nr, 128:144], r_[0:nr, 16:32], ALU.subtract, [ckk, rk + "b"], [ckk])
                        TT("dve", ck[0:nr, 144:160], x2p, cosv, ALU.mult, [psk[b], "cstok"], [ckk])
                        TT("dve", r_[0:nr, 16:32], x1p, sinv, ALU.mult, [psk[b], "cstok", ckk], [rk + "b"])
                        TT("dve", ck[0:nr, 144:160], ck[0:nr, 144:160], r_[0:nr, 16:32], ALU.add, [ckk, rk + "b"], [ckk])
                        P.dma(o_ck[si_][l, orow, :], ck[0:nr, 0:128], reads=[ckk])
                        P.dma(o_kp[si_][l, orow, :], ck[0:nr, 128:160], reads=[ckk])

                def P_C(g):
                    W, c0, sample = g.W, g.c0, g.sample
                    bT1, bT2, bT3, bL = 4, 5, 6, 7
                    if sample:
                        P.dma(lfp[:, 0:8, :], cfl[l, 0:1024, :].rearrange("(t p) h -> p t h", p=128), writes=["lfp"])
                        P.dma(lfp[:, 8:16, :], cfl[l, 1024:2048, :].rearrange("(t p) h -> p t h", p=128), writes=["lfp"])
                        MEMSET("pool", Rp[:, 0, :], 0.0, ["Rp"])
                        for t in range(16):
                            TT("pool", Rp[:, t + 1, :], Rp[:, t, :], lfp[:, t, :], ALU.add, ["Rp", "lfp"], ["Rp"])
                        bF = jrot.next()
                        for t in range(16):
                            MM(ps[bF][:, t * 4:t * 4 + 4], triF, lfp[:, t, :], True, False, ["cst", "lfp"], [psk[bF]], sig=False)
                            MM(ps[bF][:, t * 4:t * 4 + 4], onesF, Rp[:, t, :], False, True, ["cst", "Rp"], [psk[bF]], sig=True)
                        TS("dve", negFp[:].rearrange("p t h -> p (t h)"), ps[bF][:, 0:64], -1.0, None, ALU.mult, None, [psk[bF]], ["negFp"])
                        CP("dve", Rrun[:], Rp[:, 16, :], ["Rp"], ["Rrun"])
                    elif g.gi == 0:
                        MEMSET("pool", Rrun[:], 0.0, ["Rrun"])
                    nt = len(g.tiles)
                    nr0 = g.tiles[0][1]
                    tg0 = 33 if sample else (g.tiles[0][0] // 128)
                    for i, (r0, nr) in enumerate(g.tiles):
                        ck = ckst[i]
                        ckk = f"ckst{i}"
                        TR(ps[bT1][:, i * 128:i * 128 + nr], ck[0:nr, 0:128], cst[0:nr, 0:nr], [ckk, "cst"], [psk[bT1]])
                        TR(ps[bT2][0:32, i * 128:i * 128 + nr], ck[0:nr, 128:160], cst[0:nr, 0:nr], [ckk, "cst"], [psk[bT2]])
                    for i, (r0, nr) in enumerate(g.tiles):
                        fo = ps[bL][0:nr, 32 + i * 4:36 + i * 4]
                        MM(fo, cst[0:nr, 128:128 + nr], lf[0:nr, i, :], True, False, ["cst", "lf"], [psk[bL]], sig=False)
                        MM(fo, cst[:, 256:256 + nr], Rrun[:], False, i == 0, ["cst", "Rrun"], [psk[bL]], sig=(i == 0))
                        for j in range(i):
                            MM(fo, cst[:, 256:256 + nr], lf[:, j, :], False, j == i - 1, ["cst", "lf"], [psk[bL]], sig=(j == i - 1))
                    TS("dve", negF[0:nr0, tg0:tg0 + nt, :], ps[bL][0:nr0, 32:32 + 4 * nt].rearrange("p (t h) -> p t h", t=nt), -1.0, None,
                       ALU.mult, None, [psk[bL]], ["negF"])
                    TS("dve", f8[0:nr0, 0:nt, :], ps[bL][0:nr0, 32:32 + 4 * nt].rearrange("p (t h) -> p t h", t=nt), 8.0, None,
                       ALU.mult, None, [psk[bL]], ["f8"])
                    if not sample:
                        for i in range(nt):
                            TT("dve", Rrun[:], Rrun[:], lf[:, i, :], ALU.add, ["Rrun", "lf"], ["Rrun"])
                    for i, (r0, nr) in enumerate(g.tiles):
                        TR(ps[bT3][0:4, i * 128:i * 128 + nr], f8[0:nr, i, :], cst[0:nr, 0:nr], ["f8", "cst"], [psk[bT3]])
                    CP("act", ckTg[:, 0:W], ps[bT1][:, 0:W], [psk[bT1]], ["ckTg"])
                    CP("dve", kpTg[:, 0:W], ps[bT2][0:32, 0:W], [psk[bT2]], ["kpTg"])
                    put(KP[:, c0:c0 + W], kpTg[:, 0:W], "kpTg")
                    CP("dve", fqs[:, 0:W], ps[bT3][0:4, 0:W], [psk[bT3]], ["fqs"])
                    put(FQ[:, c0:c0 + W], fqs[:, 0:W], "fqs")
                    b = jrot.next()
                    for c in range(2):
                        MM(ps[b][:, 0:W], cstb[:, 256:384], sqb[:, c, 0:W], c == 0, c == 1, ["cstb", "sqb"], [psk[b]])
                    ACT(rsbt[:, 0:W], ps[b][:, 0:W], AF.Ln, [psk[b]], ["rsbt"], bias=epst[:, :], scale=-1.0 / 256)
                    ACT(rsb[:, 0:W], rsbt[:, 0:W], AF.Exp, ["rsbt"], ["rsb"], scale=-0.5)
                    for c in range(2):
                        STT(cqg[:, c, 0:W], cqT[:, c, 0:W], prmT[:, 40 + c:41 + c], rsb[:, 0:W], ALU.mult, ALU.mult, ["cqT", "prmT", "rsb"], ["cqg"])
                    for hp in range(4):
                        b = jrot.next()
                        MM(ps[b][:, 0:W], wuk[:, hp * 128:(hp + 1) * 128], ckTg[:, 0:W], True, True, ["wuk", "ckTg"], [psk[b]])
                        sb_i = stbrot.next()
                        CP(evrot.next(), stb[sb_i][:, 0:W], ps[b][:, 0:W], [psk[b]], [f"stb{sb_i}"])
                        put(KM[hp * 128:(hp + 1) * 128, c0:c0 + W], stb[sb_i][:, 0:W], f"stb{sb_i}")
                    for i, (r0, nr) in enumerate(g.tiles):
                        b = jrot.next()
                        MM(ps[b][0:nr, :], ckTg[:, i * 128:i * 128 + nr], wuv[:], True, True, ["wuv", "ckTg"], [psk[b]])
                        vi = vrot.next()
                        CP(evrot.next(), vst[vi][0:nr, :, 0:64], ps[b][0:nr, :].rearrange("p (h d) -> p h d", h=8), [psk[b]], [f"vst{vi}"])
                        P.dma(VM[c0 + i * 128:c0 + i * 128 + nr, :, :], vst[vi][0:nr, :, :], reads=[f"vst{vi}"])
                    for h in range(8):
                        bA = qhrot.next()
                        bB = qhrot.next()
                        for c in range(2):
                            MM(ps[bA][0:96, 0:W], wuq[:, c, h * 96:(h + 1) * 96], cqg[:, c, 0:W], c == 0, c == 1, ["wuq", "cqg"], [psk[bA]])
                        for c in range(2):
                            MM(ps[bB][0:96, 0:W], wuqs[:, c, h * 96:(h + 1) * 96], cqg[:, c, 0:W], c == 0, c == 1, ["wuqs", "cqg"], [psk[bB]])
                        qi = qrot.next()
                        q_ = qst[qi]
                        qk = f"qst{qi}"
                        ti = tqrot.next()
                        CP("act", q_[0:64, 0:W], ps[bA][0:64, 0:W], [psk[bA]], [qk])
                        TT("dve", tq1[ti][64:96, 0:W], ps[bA][64:96, 0:W], ropet[64:96, 0, 0:W], ALU.mult, [psk[bA], "ropet"], [f"tq1_{ti}"])
                        TT("dve", tq2[ti][64:96, 0:W], ps[bB][64:96, 0:W], ropet[64:96, 1, 0:W], ALU.mult, [psk[bB], "ropet"], [f"tq2_{ti}"])
                        TT("pool", q_[64:96, 0:W], tq1[ti][64:96, 0:W], tq2[ti][64:96, 0:W], ALU.add, [f"tq1_{ti}", f"tq2_{ti}"], [qk + "b"])
                        P.dma(QM[h, :, c0:c0 + W], q_[0:96, 0:W], reads=[qk, qk + "b"], writes=[])

                groups = [PG(gi, 512, [(gi * 512 + i * 128, 128) for i in range(4)], False) for gi in range(8)]
                groups.append(PG(8, SS, [(0, SS)], True))
                P_A1(groups[0])
                P_A2(groups[0])
                for gi, g in enumerate(groups):
                    nx = groups[gi + 1] if gi + 1 < len(groups) else None
                    if nx is not None:
                        P_A1(nx)
                    P_B(g)
                    if nx is not None:
                        P_A2(nx)
                    P_C(g)
                P.barrier()
                P.emit()

            wst_cur[0].close()
            if do_b:
                with ExitStack() as st:
                    KTb = [sbt(st, f"KT{i}", [128, S], BF16) for i in range(2)]
                    KTs = [sbt(st, f"KTs{i}", [128, PAST + SS], BF16) for i in range(4)]
                    Vbig = [sbt(st, f"Vbig{i}", [128, 32, 4, 128], BF16) for i in range(2)]
                    Vs = [sbt(st, f"Vs{i}", [128, 17, 128], BF16) for i in range(4)]
                    Qt = [sbt(st, f"Qt{i}", [128, 512], BF16) for i in range(5)]
                    SGt = [sbt(st, f"SGt{i}", [64, 512], BF16) for i in range(5)]
                    Qs = sbt(st, "Qs", [128, 4, SS], BF16)
                    SGs = sbt(st, "SGs", [64, 4, SS], BF16)
                    Pb = [sbt(st, f"Pb{i}", [128, 512], BF16) for i in range(5)]
                    Eb = [sbt(st, f"Eb{i}", [128, 512], F32) for i in range(2)]
                    SPb = [sbt(st, f"SPb{i}", [128, 512], BF16) for i in range(4)]
                    SPs = [sbt(st, f"SPs{i}", [128, 512], BF16) for i in range(5)]
                    Pbs = sbt(st, "Pbs", [128, 4, 6, SS], BF16)
                    Ebs = sbt(st, "Ebs", [128, 4, 2, SS], F32)
                    SPbs = sbt(st, "SPbs", [128, 4, 4, SS], BF16)
                    SPss = sbt(st, "SPss", [128, 4, 5, SS], BF16)
                    rl = sbt(st, "rl", [64, 512], F32)
                    tmpb = sbt(st, "tmpb", [64, 512], F32)
                    yb = [sbt(st, f"yb{i}", [64, 512], BF16) for i in range(2)]
                    cstg = [sbt(st, f"cstg{i}", [128, 16, 128], F32) for i in range(1)]
                    kst = sbt(st, "kst", [128, 4, 16, 64], BF16)
                    ckTs = sbt(st, "ckTs", [128, PAST], BF16)
                    qrot2 = Rot([0, 1, 2, 3, 4])
                    yrot = Rot([0, 1])
                    orot = Rot([4, 5, 6])
                    jr2 = Rot([7, 0, 1, 2, 3])
                    ktrot = Rot([0, 1])
                    cgrot = Rot([0, 1])
                    for i in range(4):
                        MEMSET("pool", Vs[i][:, :, 64:128], 2.0, [f"Vs{i}"])

                    class Bufs:
                        def __init__(self, sample_slot=None):
                            self.ss = sample_slot
                            self.c = {}

                        def _n(self, name, n):
                            i = self.c.get(name, 0)
                            self.c[name] = i + 1
                            return i % n

                        def S(self):
                            i = self._n("S", 4) if self.ss is None else self.ss
                            return ps[i], psk[i]

                        def Z(self):
                            i = self._n("Z", 2) if self.ss is None else (self.ss % 2)
                            return ps[i], psk[i]

                        def W(self):
                            i = 2 + (self._n("W", 2) if self.ss is None else (self.ss % 2))
                            return ps[i], psk[i]

                        def P(self):
                            if self.ss is None:
                                i = self._n("P", 5)
                                return Pb[i], f"Pb{i}"
                            i = self._n("P", 6)
                            return Pbs[:, self.ss, i, :], f"Pbs{self.ss}_{i}"

                        def E(self):
                            if self.ss is None:
                                i = self._n("E", 2)
                                return Eb[i], f"Eb{i}"
                            i = self._n("E", 2)
                            return Ebs[:, self.ss, i, :], f"Ebs{self.ss}_{i}"

                        def SP(self):
                            if self.ss is None:
                                i = self._n("SP", 4)
                                return SPb[i], f"SPb{i}"
                            i = self._n("SP", 4)
                            return SPbs[:, self.ss, i, :], f"SPbs{self.ss}_{i}"

                        def SS_(self):
                            if self.ss is None:
                                i = self._n("SS", 5)
                                return SPs[i], f"SPs{i}"
                            i = self._n("SS", 5)
                            return SPss[:, self.ss, i, :], f"SPss{self.ss}_{i}"

                    def run_softmax(problems, LA=3):
                        N = max(len(p[0]) for p in problems)
                        for s in range(N + LA):
                            for (items, bf) in problems:
                                if s < len(items):
                                    it = items[s]
                                    if it.get("pre"):
                                        it["pre"]()
                                    kw, qw, cl, mw = it["kw"], it["qw"], it.get("cl", 0), it.get("mw", 128)
                                    sb_, sk_ = bf.S()
                                    last = it["mask"] is None
                                    MM(sb_[0:kw, cl:qw], it["kt"], it["q"][:, cl:qw], True, last, it["rk"] + it["qk"], [sk_], sig=last)
                                    if not last:
                                        MM(sb_[0:kw, cl:cl + mw], identb[0:kw, 0:kw], it["mask"], False, True, ["cstb", "mk"], [sk_], sig=True)
                                    pt, pk = bf.P()
                                    ACT(pt[0:kw, cl:qw], sb_[0:kw, cl:qw], AF.Exp, [sk_] + it["bk"], [pk], bias=it["bias"], scale=it["scale"])
                                    it["p"] = (pt, pk)
                            for (items, bf) in problems:
                                j = s - LA
                                if 0 <= j < len(items):
                                    it = items[j]
                                    kw, qw, cl = it["kw"], it["qw"], it.get("cl", 0)
                                    pt, pk = it["p"]
                                    ob = it["ob"]
                                    oc = it.get("oc", 0)
                                    MM(ps[ob][0:it["M"], oc + cl:oc + qw], it["v"], pt[0:kw, cl:qw], it["first"] and not it.get("nostart"), it["last"],
                                       it["rk"] + [pk], [psk[ob]], sig=it["last"])
                                    if it["last"]:
                                        it["fin"]()

                    def run_sb(problems):
                        N = max(len(p[0]) for p in problems)
                        for s in range(N + 5):
                            for (items, bf) in problems:
                                if s < len(items):
                                    it = items[s]
                                    if it.get("pre"):
                                        it["pre"]()
                                    kw, qw, cl, mw = it["kw"], it["qw"], it.get("cl", 0), it.get("mw", 128)
                                    zb, zk = bf.Z()
                                    last = it["mask"] is None
                                    MM(zb[0:kw, cl:qw], it["kt"], it["q"][:, cl:qw], True, last, it["rk"] + it["qk"], [zk], sig=last)
                                    if not last:
                                        MM(zb[0:kw, cl:cl + mw], identb[0:kw, 0:kw], it["mask"], False, True, ["cstb", "mk"], [zk], sig=True)
                                    et, ek = bf.E()
                                    ACT(et[0:kw, cl:qw], zb[0:kw, cl:qw], AF.Exp, [zk], [ek])
                                    it["e"] = (et, ek)
                            for (items, bf) in problems:
                                s_ = s
                                s = s_ - 1
                                if 0 <= s < len(items):
                                    it = items[s]
                                    kw, qw, cl, mw = it["kw"], it["qw"], it.get("cl", 0), it.get("mw", 128)
                                    et, ek = it["e"]
                                    spt, spk = bf.SP()
                                    ACT(spt[0:kw, cl:qw], et[0:kw, cl:qw], AF.Ln, [ek], [spk], bias=1.0)
                                    sst, ssk = bf.SS_()
                                    if it["first"]:
                                        if kw < 128:
                                            MEMSET("pool", sst[:, 0:qw], 0.0, [ssk])
                                        CP("pool", sst[0:kw, cl:qw], spt[0:kw, cl:qw], [spk], [ssk])
                                    else:
                                        pst, psk_ = items[s - 1]["ss"]
                                        pcl = items[s - 1].get("cl", 0)
                                        TT("dve", sst[:, pcl:qw], pst[:, pcl:qw], spt[:, pcl:qw], ALU.add, [psk_, psk_ + "x", spk], [ssk])
                                        if pcl > cl:
                                            CP("pool", sst[:, cl:pcl], spt[:, cl:pcl], [spk], [ssk + "x"])
                                    it["sp"] = (spt, spk)
                                    it["ss"] = (sst, ssk)
                                s = s_
                            for (items, bf) in problems:
                                j = s - 3
                                if 0 <= j < len(items):
                                    it = items[j]
                                    kw, qw, cl, mw = it["kw"], it["qw"], it.get("cl", 0), it.get("mw", 128)
                                    wb, wk_ = bf.W()
                                    MM(wb[0:kw, cl:qw], it["kt"], it["q"][:, cl:qw], True, False, it["rk"] + it["qk"], [wk_], sig=False)
                                    if it["mask"] is not None:
                                        MM(wb[0:kw, cl:cl + mw], identb[0:kw, 0:kw], it["mask"], False, False, ["cstb", "mk"], [wk_], sig=False)
                                    spt, spk = it["sp"]
                                    MM(wb[0:kw, cl:qw], tinclneg[0:kw, 0:kw], spt[0:kw, cl:qw], False, it["first"], ["cstb", spk], [wk_], sig=it["first"])
                                    if not it["first"]:
                                        pst, psk_ = items[j - 1]["ss"]
                                        pcl = items[j - 1].get("cl", 0)
                                        MM(wb[0:kw, pcl:qw], onesneg[:, 0:kw], pst[:, pcl:qw], False, True, ["cstb", psk_, psk_ + "x"], [wk_], sig=True)
                                    pt, pk = bf.P()
                                    ACT(pt[0:kw, cl:qw], wb[0:kw, cl:qw], AF.Exp, [wk_], [pk])
                                    it["p"] = (pt, pk)
                            for (items, bf) in problems:
                                j = s - 5
                                if 0 <= j < len(items):
                                    it = items[j]
                                    kw, qw, cl = it["kw"], it["qw"], it.get("cl", 0)
                                    pt, pk = it["p"]
                                    ob = it["ob"]
                                    oc = it.get("oc", 0)
                                    MM(ps[ob][0:it["M"], oc + cl:oc + qw], it["v"], pt[0:kw, cl:qw], it["first"] and not it.get("nostart"), it["last"],
                                       it["rk"] + [pk], [psk[ob]], sig=it["last"])
                                    if it["last"]:
                                        it["fin"]()

                    def finalize(kind, ob, sg_ap, sgk, qw, yrow, c0, ro=0, oc=0):
                        yi = yrot.next()
                        if kind == "sb":
                            STT(yb[yi][:, 0:qw], ps[ob][ro:ro + 64, oc:oc + qw], 1.0, sg_ap, ALU.mult, ALU.mult, [psk[ob], sgk], [f"yb{yi}"])
                        else:
                            P.op("dve", lambda e: e.reciprocal(out=rl[:, 0:qw], in_=ps[ob][64:128, oc:oc + qw]), reads=[psk[ob]], writes=["rl"])
                            TT("pool", tmpb[:, 0:qw], sg_ap, rl[:, 0:qw], ALU.mult, [sgk, "rl"], ["tmpb"])
                            STT(yb[yi][:, 0:qw], ps[ob][0:64, oc:oc + qw], 2.0, tmpb[:, 0:qw], ALU.mult, ALU.mult, [psk[ob], "tmpb"], [f"yb{yi}"])
                        P.dma(YT[yrow:yrow + 64, c0:c0 + qw], yb[yi][:, 0:qw], reads=[f"yb{yi}"])

                    def run_group(kind, hook):
                        nheads = 8 if kind == "mla" else 4
                        Qsrc = {"fox": QF, "sb": QB}.get(kind)
                        Ksrc = {"fox": KF, "sb": KB, "mla": KM}[kind]
                        Gsrc = {"fox": GF, "sb": GB, "mla": GM}[kind]
                        ybase = {"fox": 0, "sb": 256, "mla": 512}[kind]
                        Kd = {"fox": 65, "sb": 128, "mla": 96}[kind]
                        M = 128
                        mkind = {"fox": 0, "sb": 1, "mla": 2}[kind]
                        scale = {"fox": 0.125, "sb": 1.0, "mla": MLA_SCALE}[kind]
                        runner = run_sb if kind == "sb" else run_softmax

                        def sample_loads(hs, sl0):
                            ctxs = []
                            for sl, h in enumerate(hs, start=sl0):
                                Ks, ksk = KTs[sl], f"KTs{sl}"
                                Vh, vsk = Vs[sl], f"Vs{sl}"
                                if kind in ("fox", "sb"):
                                    ksrc_c = cfk if kind == "fox" else csk
                                    vsrc_c = cfv if kind == "fox" else csv
                                    P.dma(kst[:, sl, :, :], ksrc_c[l, :, h * 64:(h + 1) * 64].rearrange("(t p) c -> p t c", p=128),
                                          writes=[f"kst{sl}"], queue="pool")
                                    P.dma(Vh[:, 0:16, 0:64], vsrc_c[l, :, h * 64:(h + 1) * 64].rearrange("(t p) c -> p t c", p=128),
                                          writes=[vsk], queue="pool")
                                    if kind == "fox":
                                        MEMSET("pool", Ks[64:65, :], 1.0, [ksk + "a"])
                                        P.dma(Vh[0:SS, 16, :], VF[S:S + SS, h, :], writes=[vsk + "n"])
                                        P.dma(Qs[64:65, sl, :], FQ[h:h + 1, S:S + SS], writes=[f"Qs{sl}a"])
                                    else:
                                        MEMSET("pool", Ks[64:128, :], 0.0, [ksk + "a"])
                                        MEMSET("pool", Qs[64:128, sl, :], 0.0, [f"Qs{sl}a"])
                                        P.dma(Vh[0:SS, 16, 0:64], VB[S:S + SS, h * 64:(h + 1) * 64], writes=[vsk + "n"])
                                    P.dma(Qs[0:64, sl, :], Qsrc[h * 64:(h + 1) * 64, S:S + SS], writes=[f"Qs{sl}"])
                                else:
                                    P.dma(Vh[0:SS, 16, :], VM[S:S + SS, h, :], writes=[vsk + "n"])
                                    P.dma(Qs[0:96, sl, :], QM[h, :, S:S + SS], writes=[f"Qs{sl}"])
                                P.dma(Ks[0:64, PAST:PAST + SS], Ksrc[h * 64:(h + 1) * 64, S:S + SS], writes=[ksk + "n"])
                                P.dma(SGs[:, sl, :], Gsrc[h * 64:(h + 1) * 64, S:S + SS], writes=[f"SGs{sl}"])
                                ctxs.append(dict(h=h, sl=sl, Ks=Ks, ksk=ksk, Vh=Vh, vsk=vsk))
                            return ctxs

                        def sample_prep(ctxs):
                            for c_ in ctxs:
                                h, sl, Ks, ksk, Vh, vsk = c_["h"], c_["sl"], c_["Ks"], c_["ksk"], c_["Vh"], c_["vsk"]
                                if kind in ("fox", "sb"):
                                    for g2 in range(2):
                                        b = jr2.next()
                                        psb = ps[b][:].bitcast(BF16)
                                        for j in range(8):
                                            TR(psb[0:64, j * 128:(j + 1) * 128], kst[:, sl, g2 * 8 + j, :], identb, [f"kst{sl}", "cstb"], [psk[b]])
                                        CP(evrot.next(), Ks[0:64, g2 * 1024:(g2 + 1) * 1024], psb[0:64, 0:1024], [psk[b]], [ksk])
                                else:
                                    for g4 in range(4):
                                        b = jr2.next()
                                        MM(ps[b][0:64, :], wuk[:, h * 64:(h + 1) * 64], ckTs[:, g4 * 512:(g4 + 1) * 512], True, True, ["wuk", "ckTs"], [psk[b]])
                                        CP(evrot.next(), Ks[0:64, g4 * 512:(g4 + 1) * 512], ps[b][0:64, :], [psk[b]], [ksk])
                                    for g4 in range(4):
                                        b = jr2.next()
                                        for i in range(4):
                                            t = g4 * 4 + i
                                            MM(ps[b][:, i * 64:(i + 1) * 64], ckTs[:, t * 128:(t + 1) * 128], wuv[:, h * 64:(h + 1) * 64], True, True,
                                               ["wuv", "ckTs"], [psk[b]])
                                        CP(evrot.next(), Vh[:, g4 * 4:(g4 + 1) * 4, 0:64], ps[b][:, 0:256].rearrange("p (t d) -> p t d", t=4), [psk[b]], [vsk])

                        def mla_sample_once():
                            P.dma(kst[:, 0, :, 0:32], ckp[l].rearrange("(t p) c -> p t c", p=128), writes=["kst0"], queue="pool")
                            for g2 in range(2):
                                b = jr2.next()
                                psb = ps[b][:].bitcast(BF16)
                                for j in range(8):
                                    TR(psb[0:32, j * 128:(j + 1) * 128], kst[:, 0, g2 * 8 + j, 0:32], identb, ["kst0", "cstb"], [psk[b]])
                                for sl in range(4):
                                    CP(evrot.next(), KTs[sl][64:96, g2 * 1024:(g2 + 1) * 1024], psb[0:32, 0:1024], [psk[b]], [f"KTs{sl}a"])
                            for sl in range(4):
                                P.dma(KTs[sl][64:96, PAST:PAST + SS], KP[:, S:S + SS], writes=[f"KTs{sl}b"])

                        def sample_attn(ctxs):
                            problems = []
                            for c_ in ctxs:
                                h, sl, Ks, ksk, Vh, vsk = c_["h"], c_["sl"], c_["Ks"], c_["ksk"], c_["Vh"], c_["vsk"]
                                rks = [ksk, ksk + "a", ksk + "b", ksk + "n", vsk, vsk + "n"]
                                ob = sl
                                qap = Qs[0:Kd, sl, :]
                                qk = [f"Qs{sl}", f"Qs{sl}a"]
                                items = []
                                for kb in range(16):
                                    items.append(dict(kt=Ks[0:Kd, kb * 128:(kb + 1) * 128], v=Vh[:, kb, 0:M], kw=128, qw=SS, mask=None,
                                                      bias=(negFp[:, kb, h:h + 1] if kind == "fox" else None),
                                                      bk=(["negFp"] if kind == "fox" else []), rk=rks))
                                smask = None
                                if kind != "mla":
                                    ms = mkind * 896 + 384
                                    smask = mk[0:SS, ms:ms + SS]
                                items.append(dict(kt=Ks[0:Kd, PAST:PAST + SS], v=Vh[0:SS, 16, 0:M], kw=SS, qw=SS, mask=smask, cl=0, mw=SS,
                                                  bias=(negF[0:SS, 33, h:h + 1] if kind == "fox" else None),
                                                  bk=(["negF"] if kind == "fox" else []), rk=rks))
                                if kind == "sb":
                                    items = items[::-1]
                                obank = 4 + (sl % 2) if False else None
                                for ii, it in enumerate(items):
                                    it.update(q=qap, qk=qk, scale=scale, M=M, first=(ii == 0), last=(ii == len(items) - 1), ob=None, fin=None)
                                problems.append((items, Bufs(sample_slot=sl), c_))
                            MM(ps[7][:, 0:4 * SS], zb[:], zb[:, 0:4 * SS], True, True, ["zb"], ["ps7"])
                            pr = []
                            for (items, bf, c_) in problems:
                                h, sl = c_["h"], c_["sl"]
                                for it in items:
                                    it.update(ob=7, oc=sl * SS, nostart=True)
                                items[-1]["fin"] = (lambda sl=sl, h=h: finalize(kind, 7, SGs[:, sl, :], f"SGs{sl}", SS, ybase + h * 64, S, oc=sl * SS))
                                pr.append((items, bf))
                            return pr

                        def vidx(h):
                            return ((0 if kind != "sb" else 1) if h < 4 else 1)

                        def prep_head(h):
                            ctx = dict(h=h)
                            if h % 4 == 0:
                                vi = vidx(h)
                                vk = f"Vbig{vi}"
                                if kind == "fox":
                                    for q4 in range(4):
                                        P.dma(Vbig[vi][:, q4 * 8:(q4 + 1) * 8, :, :],
                                              VF[q4 * 1024:(q4 + 1) * 1024, :, :].rearrange("(t p) h c -> p t h c", p=128), writes=[vk])
                                elif kind == "sb":
                                    vv = Vbig[vi][:].rearrange("p t h c -> p (t h c)")[:, 0:32 * 256].rearrange("p (t c) -> p t c", t=32)
                                    for q4 in range(4):
                                        P.dma(vv[:, q4 * 8:(q4 + 1) * 8, :],
                                              VB[q4 * 1024:(q4 + 1) * 1024, :].rearrange("(t p) c -> p t c", p=128), writes=[vk])
                                else:
                                    for q4 in range(4):
                                        P.dma(Vbig[vi][:, q4 * 8:(q4 + 1) * 8, :, :],
                                              VM[q4 * 1024:(q4 + 1) * 1024, h:h + 4, :].rearrange("(t p) h c -> p t h c", p=128), writes=[vk])
                            ki = h % 2
                            KT = KTb[ki]
                            kk = f"KT{ki}"
                            P.dma(KT[0:64, :], Ksrc[h * 64:(h + 1) * 64, 0:S], writes=[kk])
                            if kind == "fox":
                                MEMSET("pool", KT[64:65, :], 1.0, [kk + "a"])
                            elif kind == "mla":
                                P.dma(KT[64:96, :], KP[:, 0:S], writes=[kk + "a"])
                            else:
                                MEMSET("pool", KT[64:128, :], 0.0, [kk + "a"])
                            ctx.update(KT=KT, kk=kk)
                            return ctx

                        def load_q(h, qt):
                            qi = qrot2.next()
                            qk = f"Qt{qi}"
                            c0 = qt * 512
                            if kind == "mla":
                                P.dma(Qt[qi][0:96, :], QM[h, :, c0:c0 + 512], writes=[qk])
                            else:
                                P.dma(Qt[qi][0:64, :], Qsrc[h * 64:(h + 1) * 64, c0:c0 + 512], writes=[qk])
                                if kind == "fox":
                                    P.dma(Qt[qi][64:65, :], FQ[h:h + 1, c0:c0 + 512], writes=[qk + "a"])
                                else:
                                    MEMSET("pool", Qt[qi][64:128, :], 0.0, [qk + "a"])
                            P.dma(SGt[qi][:, :], Gsrc[h * 64:(h + 1) * 64, c0:c0 + 512], writes=[f"SGt{qi}"])
                            return qi

                        if kind == "mla":
                            P.dma(cstg[0][:], cck[l].rearrange("(t p) c -> p t c", p=128), writes=["cstg0", "cstg0v"])
                            for g4 in range(4):
                                b = jr2.next()
                                for i in range(4):
                                    TR(ps[b][:, i * 128:(i + 1) * 128], cstg[0][:, g4 * 4 + i, :], ident, ["cstg0", "cst"], [psk[b]])
                                CP(evrot.next(), ckTs[:, g4 * 512:(g4 + 1) * 512], ps[b][:, :], [psk[b]], ["ckTs"])

                        if kind == "mla":
                            mla_sample_once()
                        c03 = sample_loads([0, 1, 2, 3], 0)
                        sample_prep(c03)
                        yield
                        spr0 = sample_attn(c03)
                        pbufs = Bufs()
                        late = []
                        prep_head(0)
                        all_items = []
                        qpre = {}
                        qseq = [(h_, q_) for h_ in range(nheads) for q_ in range(8)]
                        qpre[(0, 0)] = load_q(0, 0)
                        qpre[(0, 1)] = load_q(0, 1)
                        for h in range(nheads):
                            vc = vidx(h)
                            vk = f"Vbig{vc}"
                            KT, kk = KTb[h % 2], f"KT{h % 2}"
                            if kind == "sb":
                                vv = Vbig[vc][:].rearrange("p t h c -> p (t h c)")[:, 0:32 * 256].rearrange("p (t c) -> p t c", t=32)
                                Vof = lambda t, h=h, vv=vv: vv[:, t, (h // 2) * 128:(h // 2) * 128 + 128]
                            else:
                                Vof = lambda t, h=h, vc=vc: Vbig[vc][:, t, h % 4, :]
                            items = []
                            for qt in range(8):
                                c0 = qt * 512
                                nkb = 4 * qt + 4
                                ob = orot.next()
                                blk = []
                                for kb in range(nkb):
                                    r = kb - 4 * qt
                                    mask = None
                                    cl = 0
                                    if r >= 0:
                                        ms = mkind * 896 + 384
                                        mask = mk[:, ms:ms + 128]
                                        cl = 128 * r
                                    blk.append(dict(kt=KT[0:Kd, kb * 128:(kb + 1) * 128], v=Vof(kb), kw=128, qw=512, mask=mask, cl=cl, mw=128,
                                                    bias=(negF[:, kb, h:h + 1] if kind == "fox" else None),
                                                    bk=(["negF"] if kind == "fox" else []), rk=[kk, kk + "a", vk],
                                                    scale=scale, M=M, ob=ob, first=False, last=False, fin=None, qt=qt))
                                if kind == "sb":
                                    blk = blk[::-1]
                                blk[0]["first"] = True
                                blk[-1]["last"] = True
                                items.extend(blk)
                            def mk_pre(h, qt, blk_items):
                                def pre():
                                    if qt == 1 and h + 1 < nheads:
                                        prep_head(h + 1)
                                    if h == 2 and qt == 2 and hook is not None:
                                        hook()
                                    if kind == "mla" and h == 5 and qt == 2:
                                        c47 = sample_loads([4, 5, 6, 7], 0)
                                        sample_prep(c47)
                                        late.append(c47)
                                    qi = qpre.pop((h, qt))
                                    for it in blk_items:
                                        it["q"] = Qt[qi][0:Kd, :]
                                        it["qk"] = [f"Qt{qi}", f"Qt{qi}a"]
                                    blk_items[-1]["fin"] = (lambda qi=qi, ob=blk_items[0]["ob"], qt=qt, h=h:
                                                            finalize(kind, ob, SGt[qi][:, :], f"SGt{qi}", 512, ybase + h * 64, qt * 512,
                                                                     ro=((h % 2) * 64 if kind == "sb" else 0)))
                                    idx = h * 8 + qt + 2
                                    if idx < len(qseq):
                                        qpre[qseq[idx]] = load_q(*qseq[idx])
                                return pre
                            pos = 0
                            for qt in range(8):
                                nkb = 4 * qt + 4
                                seg = items[pos:pos + nkb]
                                seg[0]["pre"] = mk_pre(h, qt, seg)
                                pos += nkb
                            all_items.extend(items)
                        runner([(all_items, pbufs)] + spr0)

                        for c47 in late:
                            runner(sample_attn(c47))

                    g_mla = run_group("mla", None)
                    g_sb = run_group("sb", lambda: next(g_mla))
                    g_fox = run_group("fox", lambda: next(g_sb))
                    next(g_fox)
                    for g_ in (g_fox, g_sb, g_mla):
                        next(g_, None)
                    P.barrier()
                    P.emit()

            if l + 1 < depth:
                wst_cur[0] = ExitStack()
                cur_w[0] = alloc_weights(wst_cur[0])
                wgen[0] = load_weights(l + 1, cur_w[0], ("pool",))
            if do_c:
                with ExitStack() as st:
                    wob = sbt(st, "wob", [128, 8, D], BF16)
                    stg = [sbt(st, f"stgc{i}", [128, 2048], F32) for i in range(2)]
                    ytl = [sbt(st, f"ytl{i}", [128, 8, 512], BF16) for i in range(2)]
                    xr = [sbt(st, f"xr{i}", [128, D], F32) for i in range(2)]
                    tt_ = [sbt(st, f"tt{i}", [128, D], F32) for i in range(2)]
                    xo = [sbt(st, f"xo{i}", [128, D], F32) for i in range(2)]
                    sqc = sbt(st, "sqc", [128, 512], BF16)
                    stc = sbt(st, "stc", [128, 8], F32)
                    wv = w_out[l].rearrange("(k p) c -> p k c", p=128)
                    for c4 in range(4):
                        sv = stg[c4 % 2][:, 0:2048].rearrange("p (k c) -> p k c", k=8)
                        P.dma(sv, wv[:, :, c4 * 256:(c4 + 1) * 256], writes=[f"stgc{c4 % 2}"])
                        CP(["pool", "dve"][c4 % 2], wob[:, :, c4 * 256:(c4 + 1) * 256], sv, [f"stgc{c4 % 2}"], ["wob"])
                    yv = YT.rearrange("(k p) t -> p k t", p=128)
                    crot = Rot([0, 1])
                    brot = Rot([0, 2, 4, 6])

                    def out_tile(ytile, ykey, coff, nr, xsrc, xkey, gg, dst, dkey):
                        b0 = brot.next()
                        for hf in range(2):
                            for k in range(8):
                                MM(ps[b0 + hf][0:nr, :], ytile[:, k, coff:coff + nr], wob[:, k, hf * 512:(hf + 1) * 512], k == 0, k == 7,
                                   [ykey, "wob"], [psk[b0 + hf]])
                        ci = crot.next()
                        for hf in range(2):
                            ACT(sqc[0:nr, :], ps[b0 + hf][0:nr, :], AF.Square, [psk[b0 + hf]], ["sqc"], accum=stc[0:nr, hf:hf + 1])
                        TT("dve", stc[0:nr, 2:3], stc[0:nr, 0:1], stc[0:nr, 1:2], ALU.add, ["sqc"], ["stc2"])
                        rstd_from_ssq(stc[0:nr, 4:5], stc[0:nr, 2:3], D, nr, ["stc2"], "stc4", stc[0:nr, 3:4])
                        for hf in range(2):
                            TT("dve", tt_[ci][0:nr, hf * 512:(hf + 1) * 512], ps[b0 + hf][0:nr, :], gg[0:nr, hf * 512:(hf + 1) * 512], ALU.mult,
                               [psk[b0 + hf], "ggP", "ggS"], [f"tt{ci}"])
                        STT(dst[0:nr, :], tt_[ci][0:nr, :], stc[0:nr, 4:5], xsrc[0:nr, :], ALU.mult, ALU.add, [f"tt{ci}", "stc4", xkey], [dkey])

                    for g in range(8):
                        yi = g % 2
                        P.dma(ytl[yi][:], yv[:, :, g * 512:(g + 1) * 512], reads=["YT"], writes=[f"ytl{yi}"])
                        for i in range(4):
                            t = g * 4 + i
                            xi = crot.i % 2
                            P.dma(xr[xi][:], xin[t * 128:(t + 1) * 128, :], reads=[f"x1_{t}"] if l > 0 else [], writes=[f"xr{xi}"])
                            out_tile(ytl[yi], f"ytl{yi}", i * 128, 128, xr[xi], f"xr{xi}", ggP, xo[xi], f"xo{xi}")
                            P.dma(xout[t * 128:(t + 1) * 128, :], xo[xi][:], reads=[f"xo{xi}"], writes=[f"x1_{t}"], queue="pool")
                            if wgen[0] is not None and t % 2 == 1:
                                next(wgen[0], None)
                    if wgen[0] is not None:
                        for _ in wgen[0]:
                            pass
                        wgen[0] = None
                    P.dma(ytl[0][:, :, 0:SS], yv[:, :, S:S + SS], reads=["YT"], writes=["ytl0"])
                    out_tile(ytl[0], "ytl0", 0, SS, xs, "xs", ggS, xs, "xs")
                    if l == depth - 1:
                        P.dma(y_s, xs[:], reads=["xs"])
                    P.barrier()
                    P.emit()

          except StopBuild:
            P.barrier()
            break
        P.finish()
        P.emit()
    return nc


_CACHE = {}


def _host_consts():
    c = np.zeros((128, 512 + 3 * 896), np.float32)
    idx = np.arange(128)
    c[:, 0:128] = np.eye(128, dtype=np.float32)
    c[:, 128:256] = (idx[:, None] <= idx[None, :]).astype(np.float32)
    c[:, 256:384] = 1.0
    c[:, 384:512] = -(idx[:, None] >= idx[None, :]).astype(np.float32)
    k = idx[:, None]
    q = idx[None, :]
    tri = [np.where(k > q, NEG, 0.0), np.where(k >= q, NEG, 0.0), np.where((k // 64) > (q // 64), NEG, 0.0)]
    for m in range(3):
        base = 512 + m * 896
        c[:, base:base + 384] = NEG
        c[:, base + 384:base + 512] = tri[m]
        c[:, base + 512:base + 896] = 0.0
    sel = np.zeros((2, 256), np.float32)
    sel[0, 0:128] = 1.0
    sel[1, 128:256] = 1.0
    half = 16
    inv = (10000.0 ** (-np.arange(half, dtype=np.float32) / half)).astype(np.float32)
    pos = np.arange(S + 128, dtype=np.float32)
    ang = pos[:, None] * inv[None, :]
    cs_tok = np.concatenate([np.cos(ang), np.sin(ang)], axis=1).astype(np.float32)
    cosT = np.cos(ang[:S]).T.astype(np.float32)
    sinT = np.sin(ang[:S]).T.astype(np.float32)
    ropeT = np.stack([np.concatenate([cosT, cosT], 0), np.concatenate([-sinT, sinT], 0)]).astype(np.float32)
    return c, sel, cs_tok, ropeT


def kernel(x_prompt, x_sample, c_prompt, c_sample, cache_fox_k, cache_fox_v, cache_fox_logf,
           cache_sb_k, cache_sb_v, cache_mla_ckv, cache_mla_kpe,
           g_pre, g_post, w_ada, b_ada, w_in, b_f, g_q_a, w_uq, g_kv_a, w_uk, w_uv, w_out):
    f = lambda a: np.ascontiguousarray(np.asarray(a, dtype=np.float32))
    if "nc" not in _CACHE:
        _CACHE["nc"] = build_program()
    nc = _CACHE["nc"]
    consts, sel, cs_tok, ropeT = _host_consts()
    perm = np.arange(768)
    for h in range(8):
        b0 = h * 96 + 64
        perm[b0:b0 + 16] = np.arange(b0 + 16, b0 + 32)
        perm[b0 + 16:b0 + 32] = np.arange(b0, b0 + 16)
    w_uq_f = f(w_uq)
    shared = {
        "g_pre": f(g_pre), "g_post": f(g_post), "w_ada": f(w_ada), "b_ada": f(b_ada), "w_in": f(w_in), "b_f": f(b_f),
        "g_q_a": f(g_q_a), "w_uq": w_uq_f, "w_uqs": np.ascontiguousarray(w_uq_f[:, :, perm]), "g_kv_a": f(g_kv_a),
        "w_uk": f(w_uk).reshape(DEPTH, 128, 512), "w_uv": f(w_uv).reshape(DEPTH, 128, 512), "w_out": f(w_out),
        "consts": consts, "selc": sel, "cs_tok": cs_tok, "ropeT": ropeT,
    }
    xp = f(x_prompt); xsm = f(x_sample); cp = f(c_prompt); csm = f(c_sample)
    cf = [f(a) for a in (cache_fox_k, cache_fox_v, cache_fox_logf, cache_sb_k, cache_sb_v, cache_mla_ckv, cache_mla_kpe)]
    in_maps = []
    for b in range(8):
        m = dict(shared)
        m["x_p"] = xp[b]
        m["x_s"] = xsm[b]
        m["c2"] = np.ascontiguousarray(np.stack([cp[b], csm[b]]))
        m["cfk"] = np.ascontiguousarray(cf[0][:, b].reshape(DEPTH, PAST, 256))
        m["cfv"] = np.ascontiguousarray(cf[1][:, b].reshape(DEPTH, PAST, 256))
        m["cfl"] = np.ascontiguousarray(cf[2][:, b])
        m["csk"] = np.ascontiguousarray(cf[3][:, b].reshape(DEPTH, PAST, 256))
        m["csv"] = np.ascontiguousarray(cf[4][:, b].reshape(DEPTH, PAST, 256))
        m["cck"] = np.ascontiguousarray(cf[5][:, b])
        m["ckp"] = np.ascontiguousarray(cf[6][:, b])
        in_maps.append(m)
    res = run_bass_kernel_spmd(nc, in_maps, core_ids=list(range(8)))
    R = res.results

    def gat(name, shape):
        return np.stack([np.asarray(R[b][name], dtype=np.float32) for b in range(8)], axis=1).reshape(shape)

    y_prompt = np.stack([np.asarray(R[b]["y_p"], dtype=np.float32) for b in range(8)])
    y_sample = np.stack([np.asarray(R[b]["y_s"], dtype=np.float32) for b in range(8)])
    outs = [y_prompt, y_sample]
    for pre, T in (("p", S), ("s", SS)):
        outs.append(gat(f"{pre}_fk", (DEPTH, 8, T, 4, 64)))
        outs.append(gat(f"{pre}_fv", (DEPTH, 8, T, 4, 64)))
        outs.append(gat(f"{pre}_fl", (DEPTH, 8, T, 4)))
        outs.append(gat(f"{pre}_sk", (DEPTH, 8, T, 4, 64)))
        outs.append(gat(f"{pre}_sv", (DEPTH, 8, T, 4, 64)))
        outs.append(gat(f"{pre}_ck", (DEPTH, 8, T, 128)))
        outs.append(gat(f"{pre}_kp", (DEPTH, 8, T, 32)))
    return tuple(outs)
```

```python
from contextlib import ExitStack
import numpy as np
import concourse.bass as bass
import concourse.mybir as mybir
from concourse.bass_utils import run_bass_kernel_spmd

F32 = mybir.dt.float32
BF16 = mybir.dt.bfloat16
ALU = mybir.AluOpType
AF = mybir.ActivationFunctionType
AX = mybir.AxisListType

N_DMA_SEMS = 56
SEM_EPOCH = 30000

S = 4096
D = 1024
SS = 16
PAST = 2048
TA = S + SS
DEPTH = 2
C_FQ, C_FK, C_FV, C_FF, C_FG = 0, 256, 512, 768, 772
C_SQ, C_SK, C_SV, C_SG = 1028, 1284, 1540, 1796
C_CQ, C_CKV, C_KPE, C_MG = 2052, 2308, 2436, 2468
INW = 2980
NEG = -240000.0
EPS = 1e-6
MLA_SCALE = 96.0 ** -0.5


class Prog:
    ENGS = ("pe", "act", "dve", "pool", "sp")

    def __init__(self, nc, stack):
        self.nc = nc
        self.stack = stack
        self.streams = {e: [] for e in self.ENGS}
        self.sems = {}
        self.cur_sem = {}
        self.count = {}
        self.epoch = {e: 0 for e in self.ENGS}
        for e in self.ENGS:
            self._new_epoch(e)
        self.waited = {e: {} for e in self.ENGS}
        self.res = {}
        self.dma_slots = []
        for i in range(N_DMA_SEMS):
            k = ("dma", i)
            self.sems[k] = stack.enter_context(nc.semaphore(f"dma{i}"))
            self.count[k] = 0
            self.dma_slots.append(k)
        self.dma_rr = 0
        self.n_ops = 0

    def _new_epoch(self, e):
        k = (e, self.epoch[e])
        self.sems[k] = self.stack.enter_context(self.nc.semaphore(f"s_{e}_{self.epoch[e]}"))
        self.count[k] = 0
        self.cur_sem[e] = k
        self.epoch[e] += 1

    def _deps(self, reads, writes):
        ev = []
        for r in reads:
            st = self.res.get(r)
            if st and st[0] is not None:
                ev.append(st[0])
        for w in writes:
            st = self.res.get(w)
            if st:
                if st[0] is not None:
                    ev.append(st[0])
                ev.extend(st[1])
        return ev

    def _record(self, reads, writes, event):
        for r in reads:
            st = self.res.setdefault(r, [None, []])
            st[1].append(event)
            if len(st[1]) > 48:
                best = {}
                for (k, v) in st[1]:
                    if best.get(k, -1) < v:
                        best[k] = v
                st[1] = list(best.items())
        for w in writes:
            self.res[w] = [event, []]

    def _waits_for(self, eng, events, skip_self):
        need = {}
        for (k, v) in events:
            if skip_self and k[0] == eng:
                continue
            if need.get(k, 0) < v:
                need[k] = v
        out = []
        wd = self.waited[eng]
        for k, v in need.items():
            if wd.get(k, 0) >= v:
                continue
            wd[k] = v
            out.append((self.sems[k], v))
        return out

    def op(self, eng, fn, reads=(), writes=(), signal=True):
        self.n_ops += 1
        pr = [r for r in reads if r.startswith("ps")]
        if pr:
            reads = [r for r in reads if not r.startswith("ps")]
            writes = list(writes) + [p for p in pr if p not in writes]
        events = self._deps(reads, writes)
        waits = self._waits_for(eng, events, skip_self=(eng == "pe"))
        k = self.cur_sem[eng]
        if signal:
            self.count[k] += 1
            ev = (k, self.count[k])
            if self.count[k] >= SEM_EPOCH:
                self._new_epoch(eng)
        else:
            ev = (k, self.count[k] + 1)
        sem = self.sems[k]

        def run(e, waits=waits, fn=fn, sem=sem, signal=signal):
            for (s, v) in waits:
                e.wait_ge(s, v)
            ins = fn(e)
            if signal:
                ins.then_inc(sem, 1)
        self.streams[eng].append(run)
        self._record(reads, writes, ev)
        return ev

    def dma(self, out, in_, reads=(), writes=(), queue="sp", **kw):
        self.n_ops += 1
        slot = self.dma_slots[self.dma_rr % N_DMA_SEMS]
        self.dma_rr += 1
        events = self._deps(reads, writes)
        if self.count[slot] > 0:
            events.append((slot, self.count[slot]))
        waits = self._waits_for(queue, events, skip_self=False)
        self.count[slot] += 16
        ev = (slot, self.count[slot])
        sem = self.sems[slot]

        def run(e, waits=waits, sem=sem, out=out, in_=in_, kw=kw):
            for (s, v) in waits:
                e.wait_ge(s, v)
            e.dma_start(out=out, in_=in_, **kw).then_inc(sem, 16)
        self.streams[queue].append(run)
        self._record(reads, writes, ev)
        return ev

    def barrier(self):
        events = [(k, c) for k, c in self.count.items() if c > 0]
        for e in self.ENGS:
            waits = self._waits_for(e, events, skip_self=True)

            def run(eo, waits=waits):
                for (s, v) in waits:
                    eo.wait_ge(s, v)
            self.streams[e].append(run)
        self.res = {}

    def finish(self):
        events = [(k, c) for k, c in self.count.items() if c > 0]
        waits = self._waits_for("sp", events, skip_self=True)

        def run(eo, waits=waits):
            for (s, v) in waits:
                eo.wait_ge(s, v)
        self.streams["sp"].append(run)

    def emit(self):
        nc = self.nc
        streams = self.streams
        self.streams = {e: [] for e in self.ENGS}
        with nc.Block() as block:
            @block.tensor
            def _(e):
                for f in streams["pe"]:
                    f(e)

            @block.scalar
            def _(e):
                for f in streams["act"]:
                    f(e)

            @block.vector
            def _(e):
                for f in streams["dve"]:
                    f(e)

            @block.gpsimd
            def _(e):
                for f in streams["pool"]:
                    f(e)

            @block.sync
            def _(e):
                for f in streams["sp"]:
                    f(e)


class StopBuild(Exception):
    pass


STOP = [None]
PLIM = [9]


def ckpt(name):
    if STOP[0] == name:
        raise StopBuild()


class Rot:
    def __init__(self, items):
        self.items = items
        self.i = 0

    def next(self):
        it = self.items[self.i % len(self.items)]
        self.i += 1
        return it


def build_program(depth=DEPTH, do_b=True, do_c=True):
    nc = bass.Bass("TRN2", target_bir_lowering=False)

    def din(name, shape, dt=F32):
        return nc.dram_tensor(name, shape, dt, kind="ExternalInput").ap()

    def dout(name, shape):
        return nc.dram_tensor(name, shape, F32, kind="ExternalOutput").ap()

    def dscr(name, shape, dt=BF16):
        return nc.dram_tensor(name, shape, dt, kind="Internal").ap()

    x_p = din("x_p", [S, D]); x_s = din("x_s", [SS, D]); c2 = din("c2", [2, D])
    cfk = din("cfk", [DEPTH, PAST, 256]); cfv = din("cfv", [DEPTH, PAST, 256]); cfl = din("cfl", [DEPTH, PAST, 4])
    csk = din("csk", [DEPTH, PAST, 256]); csv = din("csv", [DEPTH, PAST, 256])
    cck = din("cck", [DEPTH, PAST, 128]); ckp = din("ckp", [DEPTH, PAST, 32])
    g_pre = din("g_pre", [DEPTH, D]); g_post = din("g_post", [DEPTH, D])
    w_ada = din("w_ada", [DEPTH, D, 3 * D]); b_ada = din("b_ada", [DEPTH, 3 * D])
    w_in = din("w_in", [DEPTH, D, INW]); b_f = din("b_f", [DEPTH, 4]); g_q_a = din("g_q_a", [DEPTH, 256])
    w_uq = din("w_uq", [DEPTH, 256, 768]); w_uqs = din("w_uqs", [DEPTH, 256, 768])
    g_kv_a = din("g_kv_a", [DEPTH, 128]); w_uk = din("w_uk", [DEPTH, 128, 512]); w_uv = din("w_uv", [DEPTH, 128, 512])
    w_out = din("w_out", [DEPTH, D, D])
    consts = din("consts", [128, 512 + 3 * 896]); selc = din("selc", [2, 256])
    cs_tok = din("cs_tok", [S + 128, 32]); ropeT = din("ropeT", [2, 32, S])

    y_p = dout("y_p", [S, D]); y_s = dout("y_s", [SS, D])
    o_fk = [dout("p_fk", [DEPTH, S, 256]), dout("s_fk", [DEPTH, SS, 256])]
    o_fv = [dout("p_fv", [DEPTH, S, 256]), dout("s_fv", [DEPTH, SS, 256])]
    o_fl = [dout("p_fl", [DEPTH, S, 4]), dout("s_fl", [DEPTH, SS, 4])]
    o_sk = [dout("p_sk", [DEPTH, S, 256]), dout("s_sk", [DEPTH, SS, 256])]
    o_sv = [dout("p_sv", [DEPTH, S, 256]), dout("s_sv", [DEPTH, SS, 256])]
    o_ck = [dout("p_ck", [DEPTH, S, 128]), dout("s_ck", [DEPTH, SS, 128])]
    o_kp = [dout("p_kp", [DEPTH, S, 32]), dout("s_kp", [DEPTH, SS, 32])]

    QF = dscr("QF", [256, TA]); KF = dscr("KF", [256, TA]); GF = dscr("GF", [256, TA]); FQ = dscr("FQ", [4, TA])
    VF = dscr("VF", [TA + 112, 4, 128])
    QB = dscr("QB", [256, TA]); KB = dscr("KB", [256, TA]); GB = dscr("GB", [256, TA]); VB = dscr("VB", [TA + 112, 256])
    QM = dscr("QM", [8, 96, TA]); KM = dscr("KM", [512, TA]); KP = dscr("KP", [32, TA]); GM = dscr("GM", [512, TA])
    VM = dscr("VM", [TA + 112, 8, 128])
    YT = dscr("YT", [D, TA])
    x1 = dscr("x1", [S, D], F32)

    with ExitStack() as gst:
        P = Prog(nc, gst)

        uid = [0]

        def sbt(st, name, shape, dt):
            uid[0] += 1
            return st.enter_context(nc.sbuf_tensor(f"{name}_u{uid[0]}", shape, dt))

        ps = [gst.enter_context(nc.psum_tensor(f"ps{i}", [128, 512], F32)) for i in range(8)]
        psk = [f"ps{i}" for i in range(8)]
        jrot = Rot([0, 1, 2, 3])

        def MM(out, lhsT, rhs, start, stop, r, w, sig=None):
            if sig is None:
                sig = stop
            P.op("pe", lambda e: e.matmul(out, lhsT=lhsT, rhs=rhs, start=start, stop=stop), reads=r, writes=w, signal=sig)

        def TR(out, in_, ident, r, w):
            P.op("pe", lambda e: e.transpose(out, in_, ident), reads=r, writes=w)

        def ACT(out, in_, func, r, w, bias=None, scale=None, accum=None):
            kw = {}
            if bias is not None:
                kw["bias"] = bias
            if scale is not None:
                kw["scale"] = scale
            if accum is not None:
                kw["accum_out"] = accum
            P.op("act", lambda e: e.activation(out=out, in_=in_, func=func, **kw), reads=r, writes=w)

        def TT(eng, out, in0, in1, op, r, w):
            P.op(eng, lambda e: e.tensor_tensor(out=out, in0=in0, in1=in1, op=op), reads=r, writes=w)

        def TS(eng, out, in0, s1, s2, op0, op1, r, w):
            if op1 is None:
                P.op(eng, lambda e: e.tensor_scalar(out=out, in0=in0, scalar1=s1, scalar2=None, op0=op0), reads=r, writes=w)
            else:
                P.op(eng, lambda e: e.tensor_scalar(out=out, in0=in0, scalar1=s1, scalar2=s2, op0=op0, op1=op1), reads=r, writes=w)

        def STT(out, in0, scalar, in1, op0, op1, r, w):
            P.op("dve", lambda e: e.scalar_tensor_tensor(out=out, in0=in0, scalar=scalar, in1=in1, op0=op0, op1=op1), reads=r, writes=w)

        def CP(eng, out, in_, r, w):
            if eng == "act":
                P.op("act", lambda e: e.copy(out=out, in_=in_), reads=r, writes=w)
            else:
                P.op(eng, lambda e: e.tensor_copy(out=out, in_=in_), reads=r, writes=w)

        def MEMSET(eng, ap, val, w):
            P.op(eng, lambda e: e.memset(ap, val), writes=w)

        evrot = Rot(["act", "dve"])

        cst = sbt(gst, "cst", [128, 512], F32)
        cstb = sbt(gst, "cstb", [128, 384], BF16)
        mk = sbt(gst, "mk", [128, 3 * 896], BF16)
        selt = sbt(gst, "selt", [2, 256], F32)
        zt = sbt(gst, "zt", [128, 128], F32)
        zb = sbt(gst, "zb", [128, 128], BF16)
        epst = sbt(gst, "epst", [128, 1], F32)
        P.dma(cst[:], consts[:, 0:512], writes=["cst"])
        P.dma(selt[:], selc, writes=["selt"])
        with ExitStack() as st0:
            mkf = sbt(st0, "mkf", [128, 3 * 896], F32)
            P.dma(mkf[:], consts[:, 512:512 + 3 * 896], writes=["mkf"])
            CP("dve", mk[:], mkf[:], ["mkf"], ["mk"])
            CP("pool", cstb[:, 0:128], cst[:, 0:128], ["cst"], ["cstb"])
            CP("pool", cstb[:, 128:256], cst[:, 384:512], ["cst"], ["cstb"])
            TS("dve", cstb[:, 256:384], cst[:, 256:384], -1.0, None, ALU.mult, None, ["cst"], ["cstb"])
            MEMSET("pool", zt[:], 0.0, ["zt"])
            MEMSET("pool", zb[:], 0.0, ["zb"])
            MEMSET("pool", epst[:], EPS, ["epst"])
            P.barrier()
            P.emit()
        ident = cst[:, 0:128]
        triF = cst[:, 128:256]
        onesF = cst[:, 256:384]
        identb = cstb[:, 0:128]
        tinclneg = cstb[:, 128:256]
        onesneg = cstb[:, 256:384]

        xs = sbt(gst, "xs", [SS, D], F32)
        P.dma(xs[:], x_s, writes=["xs"])
        negF = sbt(gst, "negF", [128, 34, 4], F32)
        negFp = sbt(gst, "negFp", [128, 16, 4], F32)
        prmT = sbt(gst, "prmT", [128, 42], F32)
        modS = sbt(gst, "modS", [128, 2, 16], F32)
        aPS = sbt(gst, "aPS", [128, 2, 8], F32)
        ggP = sbt(gst, "ggP", [128, D], F32)
        ggS = sbt(gst, "ggS", [SS, D], F32)
        bfb = sbt(gst, "bfb", [128, 4, 4], F32)
        gkvb = sbt(gst, "gkvb", [128, 128], F32)
        wuk = sbt(gst, "wuk", [128, 512], BF16)
        wuv = sbt(gst, "wuv", [128, 512], BF16)

        def rstd_from_ssq(out, ssq, n, rows, rk, wk, tmp):
            ACT(tmp, ssq, AF.Ln, rk, [wk + "_t"], bias=epst[0:rows, :], scale=1.0 / n)
            ACT(out, tmp, AF.Exp, [wk + "_t"], [wk], scale=-0.5)

        def alloc_weights(wst):
            return dict(win=sbt(wst, "win", [128, 8, INW], BF16), wuq=sbt(wst, "wuq", [128, 2, 768], BF16),
                        wuqs=sbt(wst, "wuqs", [128, 2, 768], BF16), stg=[sbt(wst, f"stg{i}", [128, 2048], F32) for i in range(2)])

        def load_weights(l, wd, engines):
            win, wuq, wuqs, stg = wd["win"], wd["wuq"], wd["wuqs"], wd["stg"]
            srot = Rot([0, 1])
            castrot = Rot(list(engines))
            wv = w_in[l].rearrange("(k p) c -> p k c", p=128)
            for c0 in range(0, INW, 256):
                cw = min(256, INW - c0)
                si = srot.next()
                sv = stg[si][:, 0:8 * cw].rearrange("p (k c) -> p k c", k=8)
                P.dma(sv, wv[:, :, c0:c0 + cw], writes=[f"stg{si}"], queue="pool")
                CP(castrot.next(), win[:, :, c0:c0 + cw], sv, [f"stg{si}"], ["win"])
                yield
            for (dst, src_, nm) in ((wuq, w_uq, "wuq"), (wuqs, w_uqs, "wuqs")):
                si = srot.next()
                sv = stg[si][:, 0:1536].rearrange("p (k c) -> p k c", k=2)
                P.dma(sv, src_[l].rearrange("(k p) c -> p k c", p=128), writes=[f"stg{si}"], queue="pool")
                CP(castrot.next(), dst[:], sv, [f"stg{si}"], [nm])
                yield
            for (dst, src_, nm) in ((wuk, w_uk, "wuk"), (wuv, w_uv, "wuv")):
                si = srot.next()
                P.dma(stg[si][:, 0:512], src_[l], writes=[f"stg{si}"], queue="pool")
                CP(castrot.next(), dst[:], stg[si][:, 0:512], [f"stg{si}"], [nm])
                yield

        wst_cur = [ExitStack()]
        cur_w = [alloc_weights(wst_cur[0])]
        for _ in load_weights(0, cur_w[0], ("pool", "dve")):
            pass
        wgen = [None]

        for l in range(depth):
          try:
            xin = x_p if l == 0 else x1
            xout = y_p if l == depth - 1 else x1

            ckpt("setup")
            with ExitStack() as st:
                prm = sbt(st, "prm", [42, 128], F32)
                t16 = sbt(st, "t16", [16, 128], F32)
                t16b = sbt(st, "t16b", [16, 128], F32)
                wa = [sbt(st, f"wa{i}", [128, 3 * D], F32) for i in range(2)]
                grow = sbt(st, "grow", [2, D], F32)
                badag = sbt(st, "badag", [2, D], F32)
                gpb = sbt(st, "gpb", [128, D], F32)
                P.dma(prm[0:16, :], c2.rearrange("j (k p) -> (j k) p", p=128), writes=["prm"])
                P.dma(prm[16:24, :], g_pre[l].rearrange("(k p) -> k p", p=128), writes=["prm"])
                P.dma(prm[24:40, :], b_ada[l, 0:2048].rearrange("(k p) -> k p", p=128), writes=["prm"])
                P.dma(prm[40:42, :], g_q_a[l].rearrange("(k p) -> k p", p=128), writes=["prm"])
                P.dma(badag[:], b_ada[l, 2048:3072].partition_broadcast(2), writes=["badag"])
                P.dma(gpb[:], g_post[l].partition_broadcast(128), writes=["gpb"])
                for j4 in range(4):
                    P.dma(bfb[:, j4, :], b_f[l].partition_broadcast(128), writes=["bfb"])
                P.dma(gkvb[:], g_kv_a[l].partition_broadcast(128), writes=["gkvb"])
                ACT(t16[:], prm[0:16, :], AF.Exp, ["prm"], ["t16"], scale=-1.0)
                ACT(t16b[:], t16[:], AF.Ln, ["t16"], ["t16b"], bias=1.0)
                ACT(t16[:], t16b[:], AF.Exp, ["t16b"], ["t16"], scale=-1.0)
                TT("dve", prm[0:16, :], t16[:], prm[0:16, :], ALU.mult, ["t16", "prm"], ["prm"])
                TR(ps[6][:, 0:42], prm[0:42, :], cst[0:42, 0:42], ["prm", "cst"], ["ps6"])
                CP("dve", prmT[:], ps[6][:, 0:42], ["ps6"], ["prmT"])
                MM(ps[7][:, 0:32], zt[:], zt[:, 0:32], True, False, ["zt"], ["ps7"], sig=False)
                for k in range(8):
                    w_t = wa[k % 2]
                    wk = f"wa{k % 2}"
                    P.dma(w_t[:], w_ada[l, k * 128:(k + 1) * 128, :], writes=[wk])
                    sck = prmT[:, k:16:8]
                    for fc in range(16):
                        MM(ps[7][:, 2 * fc:2 * fc + 2], w_t[:, fc * 128:(fc + 1) * 128], sck, False, (k == 7 and fc == 15),
                           [wk, "prmT"], ["ps7"], sig=(fc == 15))
                    MM(ps[4][0:2, :], sck, w_t[:, 2048:2560], k == 0, k == 7, [wk, "prmT"], ["ps4"], sig=True)
                    MM(ps[5][0:2, :], sck, w_t[:, 2560:3072], k == 0, k == 7, [wk, "prmT"], ["ps5"], sig=True)
                for j in range(2):
                    TT("dve", modS[:, j, :], ps[7][:, j:32:2], prmT[:, 24:40], ALU.add, ["ps7", "prmT"], ["modS"])
                    STT(aPS[:, j, :], modS[:, j, 8:16], 1.0, prmT[:, 16:24], ALU.add, ALU.mult, ["modS", "prmT"], ["aPS"])
                TT("dve", grow[:, 0:512], ps[4][0:2, :], badag[:, 0:512], ALU.add, ["ps4", "badag"], ["grow"])
                TT("dve", grow[:, 512:1024], ps[5][0:2, :], badag[:, 512:1024], ALU.add, ["ps5", "badag"], ["grow"])
                for hf in range(2):
                    MM(ps[0 + hf][:, :], selt[0:2, 0:128], grow[0:2, hf * 512:(hf + 1) * 512], True, True, ["selt", "grow"], [psk[hf]])
                    TT("dve", ggP[:, hf * 512:(hf + 1) * 512], ps[hf][:, :], gpb[:, hf * 512:(hf + 1) * 512], ALU.mult,
                       [psk[hf], "gpb"], ["ggP"])
                    MM(ps[2 + hf][0:SS, :], selt[0:2, 128:128 + SS], grow[0:2, hf * 512:(hf + 1) * 512], True, True,
                       ["selt", "grow"], [psk[2 + hf]])
                    TT("dve", ggS[:, hf * 512:(hf + 1) * 512], ps[2 + hf][0:SS, :], gpb[0:SS, hf * 512:(hf + 1) * 512], ALU.mult,
                       [psk[2 + hf], "gpb"], ["ggS"])
                P.barrier()
                P.emit()

            ckpt("adaln")
            with ExitStack() as st:
                win, wuq, wuqs = cur_w[0]["win"], cur_w[0]["wuq"], cur_w[0]["wuqs"]
                cstok = sbt(st, "cstok", [128, 33, 32], F32)
                for q4 in range(4):
                    P.dma(cstok[:, q4 * 8:(q4 + 1) * 8, :],
                          cs_tok[q4 * 1024:(q4 + 1) * 1024, :].rearrange("(t p) c -> p t c", p=128), writes=["cstok"])
                P.dma(cstok[:, 32, :], cs_tok[PAST:PAST + 128, :], writes=["cstok"])

                xts = [sbt(st, f"xt{i}", [128, D], F32) for i in range(2)]
                xns = [sbt(st, f"xn{i}", [128, D], BF16) for i in range(4)]
                sqj = sbt(st, "sqj", [128, D], BF16)
                stat = sbt(st, "stat", [128, 16], F32)
                HTg = [sbt(st, f"HTg{i}", [128, 8, 512], BF16) for i in range(2)]
                stb = [sbt(st, f"stb{i}", [128, 512], BF16) for i in range(6)]
                stbrot = Rot([0, 1, 2, 3, 4, 5])
                kvst = [sbt(st, f"kvst{i}", [128, 512], F32) for i in range(2)]
                kvrot = Rot([0, 1])
                vst = [sbt(st, f"vst{i}", [128, 8, 128], BF16) for i in range(2)]
                vrot = Rot([0, 1])
                for i in range(2):
                    MEMSET("pool", vst[i][:, :, 64:128], 2.0, [f"vst{i}"])
                ckst = [sbt(st, f"ckst{i}", [128, 160], F32) for i in range(4)]
                ckrot = Rot([0, 1])
                rt = [sbt(st, f"rt{i}", [128, 32], F32) for i in range(4)]
                lf = sbt(st, "lf", [128, 8, 4], F32)
                Rrun = sbt(st, "Rrun", [128, 4], F32)
                f8 = sbt(st, "f8", [128, 4, 4], F32)
                fqs = sbt(st, "fqs", [4, 512], BF16)
                ckTg = sbt(st, "ckTg", [128, 512], BF16)
                kpTg = sbt(st, "kpTg", [32, 512], BF16)
                cqT = sbt(st, "cqT", [128, 2, 512], F32)
                sqb = sbt(st, "sqb", [128, 2, 512], BF16)
                cqg = sbt(st, "cqg", [128, 2, 512], BF16)
                rsb = sbt(st, "rsb", [128, 512], F32)
                rsbt = sbt(st, "rsbt", [128, 512], F32)
                ropet = sbt(st, "ropet", [128, 2, 512], F32)
                tq1 = [sbt(st, f"tq1_{i}", [128, 512], F32) for i in range(2)]
                tq2 = [sbt(st, f"tq2_{i}", [128, 512], F32) for i in range(2)]
                tqrot = Rot([0, 1])
                qhrot = Rot([0, 1, 2, 3, 4, 5, 6])
                g1s = [sbt(st, f"g1_{i}", [128, 512], F32) for i in range(2)]
                g2s = [sbt(st, f"g2_{i}", [128, 512], F32) for i in range(2)]
                grot = Rot([0, 1])
                lfp = sbt(st, "lfp", [128, 16, 4], F32)
                Rp = sbt(st, "Rp", [128, 17, 4], F32)
                qst = [sbt(st, f"qst{i}", [96, 512], BF16) for i in range(3)]
                qrot = Rot([0, 1, 2])
                xrot = Rot([0, 1])

                class PG:
                    def __init__(self, gi, W, tiles, sample):
                        self.gi, self.W, self.tiles, self.sample = gi, W, tiles, sample
                        self.si = 1 if sample else 0
                        self.c0 = S if sample else gi * 512
                        self.pos0 = PAST if sample else gi * 512
                        self.HT = HTg[gi % 2]
                        self.hk = f"HTg{gi % 2}"
                        self.aP = aPS[:, self.si, :]
                        self.bP = modS[:, self.si, 0:8]
                        self.xt = []

                def put(dst_ap, bf_ap, key):
                    P.dma(dst_ap, bf_ap, reads=[key], writes=[])

                def P_A1(g):
                    for i, (r0, nr) in enumerate(g.tiles):
                        if g.sample:
                            xt, xk = xs, "xs"
                        else:
                            xi = xrot.next()
                            xt, xk = xts[xi], f"xt{xi}"
                            P.dma(xt[:], xin[r0:r0 + nr, :], reads=[f"x1_{r0 // 128}"] if l > 0 else [], writes=[xk])
                        ACT(sqj[0:nr, :], xt[0:nr, :], AF.Square, [xk], ["sqj"], accum=stat[0:nr, i:i + 1])
                        rstd_from_ssq(stat[0:nr, 8 + i:9 + i], stat[0:nr, i:i + 1], D, nr, ["sqj"], f"stat{i}", stat[0:nr, 4 + i:5 + i])
                        TS("dve", xns[i][0:nr, :], xt[0:nr, :], stat[0:nr, 8 + i:9 + i], None, ALU.mult, None, [xk, f"stat{i}"], [f"xn{i}"])

                def P_A2(g):
                    W, HT, hk = g.W, g.HT, g.hk
                    for k in range(8):
                        b = jrot.next()
                        psb = ps[b][:].bitcast(BF16)
                        for i, (r0, nr) in enumerate(g.tiles):
                            TR(psb[:, i * 128:i * 128 + nr], xns[i][0:nr, k * 128:(k + 1) * 128], cstb[0:nr, 0:nr],
                               [f"xn{i}", "cstb"], [psk[b]])
                        if k % 2 == 0:
                            ACT(HT[:, k, 0:W], psb[:, 0:W], AF.Identity, [psk[b], "aPS", "modS"], [hk],
                                bias=g.bP[:, k:k + 1], scale=g.aP[:, k:k + 1])
                        else:
                            TS("dve", HT[:, k, 0:W], psb[:, 0:W], g.aP[:, k:k + 1], g.bP[:, k:k + 1], ALU.mult, ALU.add,
                               [psk[b], "aPS", "modS"], [hk])

                jrotB = Rot([0, 1, 2, 3, 4, 5, 6])

                def P_B(g):
                    W, HT, hk, c0, si_, sample = g.W, g.HT, g.hk, g.c0, g.si, g.sample
                    jrot = jrotB

                    def projT(col0, M, b):
                        for k in range(8):
                            MM(ps[b][0:M, 0:W], win[:, k, col0:col0 + M], HT[:, k, 0:W], k == 0, k == 7, ["win", hk], [psk[b]])

                    P.dma(ropet[64:96, 0, 0:W], ropeT[0, :, g.pos0:g.pos0 + W], writes=["ropet"])
                    P.dma(ropet[64:96, 1, 0:W], ropeT[1, :, g.pos0:g.pos0 + W], writes=["ropet"])
                    gt = [(col0 + mt * 128, dst, mt) for (col0, dst, nmt) in ((C_FG, GF, 2), (C_SG, GB, 2), (C_MG, GM, 4)) for mt in range(nmt)]
                    for p0 in range(0, len(gt), 2):
                        pr = gt[p0:p0 + 2]
                        bs = []
                        for (cc, dst, mt) in pr:
                            b = jrot.next()
                            projT(cc, 128, b)
                            bs.append(b)
                        for gi_, b in enumerate(bs):
                            ACT(g1s[gi_][:, 0:W], ps[b][:, 0:W], AF.Exp, [psk[b]], [f"g1_{gi_}"], scale=-1.0)
                        for gi_, b in enumerate(bs):
                            ACT(g2s[gi_][:, 0:W], g1s[gi_][:, 0:W], AF.Ln, [f"g1_{gi_}"], [f"g2_{gi_}"], bias=1.0)
                        for gi_, b in enumerate(bs):
                            ACT(g1s[gi_][:, 0:W], g2s[gi_][:, 0:W], AF.Exp, [f"g2_{gi_}"], [f"g1_{gi_}"], scale=-1.0)
                        for gi_, (b, (cc, dst, mt)) in enumerate(zip(bs, pr)):
                            sb_i = stbrot.next()
                            TT("dve", stb[sb_i][:, 0:W], ps[b][:, 0:W], g1s[gi_][:, 0:W], ALU.mult, [psk[b], f"g1_{gi_}"], [f"stb{sb_i}"])
                            put(dst[mt * 128:(mt + 1) * 128, c0:c0 + W], stb[sb_i][:, 0:W], f"stb{sb_i}")
                    for (col0, dst, scl) in ((C_FQ, QF, None), (C_FK, KF, None), (C_SQ, QB, 0.125), (C_SK, KB, None)):
                        for mt in range(2):
                            b = jrot.next()
                            projT(col0 + mt * 128, 128, b)
                            sb_i = stbrot.next()
                            eng = evrot.next()
                            if scl is not None:
                                TS("dve", stb[sb_i][:, 0:W], ps[b][:, 0:W], scl, None, ALU.mult, None, [psk[b]], [f"stb{sb_i}"])
                            else:
                                CP(eng, stb[sb_i][:, 0:W], ps[b][:, 0:W], [psk[b]], [f"stb{sb_i}"])
                            put(dst[mt * 128:(mt + 1) * 128, c0:c0 + W], stb[sb_i][:, 0:W], f"stb{sb_i}")
                    for c in range(2):
                        b = jrot.next()
                        projT(C_CQ + c * 128, 128, b)
                        CP("act", cqT[:, c, 0:W], ps[b][:, 0:W], [psk[b]], ["cqT"])
                        ACT(sqb[:, c, 0:W], ps[b][:, 0:W], AF.Square, [psk[b]], ["sqb"])
                    bL = 7
                    nt = len(g.tiles)
                    nr0 = g.tiles[0][1]
                    for i, (r0, nr) in enumerate(g.tiles):
                        for k in range(8):
                            MM(ps[bL][0:nr, i * 4:i * 4 + 4], HT[:, k, i * 128:i * 128 + nr], win[:, k, C_FF:C_FF + 4], k == 0, k == 7,
                               ["win", hk], [psk[bL]])
                    TT("dve", lf[0:nr0, 0:nt, :], ps[bL][0:nr0, 0:nt * 4].rearrange("p (t h) -> p t h", t=nt), bfb[0:nr0, 0:nt, :], ALU.add,
                       [psk[bL], "bfb"], ["lf"])
                    ACT(lf[0:nr0, 4:4 + nt, :], lf[0:nr0, 0:nt, :], AF.Exp, ["lf"], ["lfe"], scale=-1.0)
                    ACT(lf[0:nr0, 0:nt, :], lf[0:nr0, 4:4 + nt, :], AF.Ln, ["lfe"], ["lf"], bias=1.0)
                    TS("dve", lf[0:nr0, 0:nt, :], lf[0:nr0, 0:nt, :], -1.0, None, ALU.mult, None, ["lf"], ["lf"])
                    for i, (r0, nr) in enumerate(g.tiles):
                        P.dma(o_fl[si_][l, r0:r0 + nr, :], lf[0:nr, i, :], reads=["lf"])
                    for i, (r0, nr) in enumerate(g.tiles):
                        orow = slice(r0, r0 + nr)
                        srow = slice(c0 + i * 128, c0 + i * 128 + nr)
                        lhs = lambda k: HT[:, k, i * 128:i * 128 + nr]
                        for (colk, okk, ovv, isfox) in ((C_FK, o_fk, o_fv, True), (C_SK, o_sk, o_sv, False)):
                            b = jrot.next()
                            for k in range(8):
                                MM(ps[b][0:nr, :], lhs(k), win[:, k, colk:colk + 512], k == 0, k == 7, ["win", hk], [psk[b]])
                            ki = kvrot.next()
                            CP("act", kvst[ki][0:nr, :], ps[b][0:nr, :], [psk[b]], [f"kvst{ki}"])
                            P.dma(okk[si_][l, orow, :], kvst[ki][0:nr, 0:256], reads=[f"kvst{ki}"])
                            P.dma(ovv[si_][l, orow, :], kvst[ki][0:nr, 256:512], reads=[f"kvst{ki}"])
                            if isfox:
                                vi = vrot.next()
                                CP("dve", vst[vi][0:nr, 0:4, 0:64], ps[b][0:nr, 256:512].rearrange("p (h d) -> p h d", h=4),
                                   [psk[b]], [f"vst{vi}"])
                                P.dma(VF[srow, :, :], vst[vi][0:nr, 0:4, :], reads=[f"vst{vi}"])
                            else:
                                sb_i = stbrot.next()
                                CP("dve", stb[sb_i][0:nr, 0:256], ps[b][0:nr, 256:512], [psk[b]], [f"stb{sb_i}"])
                                P.dma(VB[srow, :], stb[sb_i][0:nr, 0:256], reads=[f"stb{sb_i}"])
                        b = jrot.next()
                        for k in range(8):
                            MM(ps[b][0:nr, 0:160], lhs(k), win[:, k, C_CKV:C_CKV + 160], k == 0, k == 7, ["win", hk], [psk[b]])
                        r_ = rt[i]
                        rk = f"rt{i}"
                        ACT(sqj[0:nr, 0:128], ps[b][0:nr, 0:128], AF.Square, [psk[b]], ["sqj"], accum=r_[0:nr, 0:1])
                        rstd_from_ssq(r_[0:nr, 2:3], r_[0:nr, 0:1], 128, nr, ["sqj"], rk, r_[0:nr, 1:2])
                        ck = ckst[i]
                        ckk = f"ckst{i}"
                        STT(ck[0:nr, 0:128], ps[b][0:nr, 0:128], r_[0:nr, 2:3], gkvb[0:nr, :], ALU.mult, ALU.mult, [psk[b], rk, "gkvb"], [ckk])
                        cosv = cstok[0:nr, (32 if sample else r0 // 128), 0:16]
                        sinv = cstok[0:nr, (32 if sample else r0 // 128), 16:32]
                        x1p = ps[b][0:nr, 128:144]
                        x2p = ps[b][0:nr, 144:160]
                        TT("dve", ck[0:nr, 128:144], x1p, cosv, ALU.mult, [psk[b], "cstok"], [ckk])
                        TT("dve", r_[0:nr, 16:32], x2p, sinv, ALU.mult, [psk[b], "cstok"], [rk + "b"])
                        TT("dve", ck[0:nr, 128:144], ck[0:nr, 128:144], r_[0:nr, 16:32], ALU.subtract, [ckk, rk + "b"], [ckk])
                        TT("dve", ck[0:nr, 144:160], x2p, cosv, ALU.mult, [psk[b], "cstok"], [ckk])
                        TT("dve", r_[0:nr, 16:32], x1p, sinv, ALU.mult, [psk[b], "cstok", ckk], [rk + "b"])
                        TT("dve", ck[0:nr, 144:160], ck[0:nr, 144:160], r_[0:nr, 16:32], ALU.add, [ckk, rk + "b"], [ckk])
                        P.dma(o_ck[si_][l, orow, :], ck[0:nr, 0:128], reads=[ckk])
                        P.dma(o_kp[si_][l, orow, :], ck[0:nr, 128:160], reads=[ckk])

                def P_C(g):
                    W, c0, sample = g.W, g.c0, g.sample
                    bT1, bT2, bT3, bL = 4, 5, 6, 7
                    if sample:
                        P.dma(lfp[:, 0:8, :], cfl[l, 0:1024, :].rearrange("(t p) h -> p t h", p=128), writes=["lfp"])
                        P.dma(lfp[:, 8:16, :], cfl[l, 1024:2048, :].rearrange("(t p) h -> p t h", p=128), writes=["lfp"])
                        MEMSET("pool", Rp[:, 0, :], 0.0, ["Rp"])
                        for t in range(16):
                            TT("pool", Rp[:, t + 1, :], Rp[:, t, :], lfp[:, t, :], ALU.add, ["Rp", "lfp"], ["Rp"])
                        bF = jrot.next()
                        for t in range(16):
                            MM(ps[bF][:, t * 4:t * 4 + 4], triF, lfp[:, t, :], True, False, ["cst", "lfp"], [psk[bF]], sig=False)
                            MM(ps[bF][:, t * 4:t * 4 + 4], onesF, Rp[:, t, :], False, True, ["cst", "Rp"], [psk[bF]], sig=True)
                        TS("dve", negFp[:].rearrange("p t h -> p (t h)"), ps[bF][:, 0:64], -1.0, None, ALU.mult, None, [psk[bF]], ["negFp"])
                        CP("dve", Rrun[:], Rp[:, 16, :], ["Rp"], ["Rrun"])
                    elif g.gi == 0:
                        MEMSET("pool", Rrun[:], 0.0, ["Rrun"])
                    nt = len(g.tiles)
                    nr0 = g.tiles[0][1]
                    tg0 = 33 if sample else (g.tiles[0][0] // 128)
                    for i, (r0, nr) in enumerate(g.tiles):
                        ck = ckst[i]
                        ckk = f"ckst{i}"
                        TR(ps[bT1][:, i * 128:i * 128 + nr], ck[0:nr, 0:128], cst[0:nr, 0:nr], [ckk, "cst"], [psk[bT1]])
                        TR(ps[bT2][0:32, i * 128:i * 128 + nr], ck[0:nr, 128:160], cst[0:nr, 0:nr], [ckk, "cst"], [psk[bT2]])
                    for i, (r0, nr) in enumerate(g.tiles):
                        fo = ps[bL][0:nr, 32 + i * 4:36 + i * 4]
                        MM(fo, cst[0:nr, 128:128 + nr], lf[0:nr, i, :], True, False, ["cst", "lf"], [psk[bL]], sig=False)
                        MM(fo, cst[:, 256:256 + nr], Rrun[:], False, i == 0, ["cst", "Rrun"], [psk[bL]], sig=(i == 0))
                        for j in range(i):
                            MM(fo, cst[:, 256:256 + nr], lf[:, j, :], False, j == i - 1, ["cst", "lf"], [psk[bL]], sig=(j == i - 1))
                    TS("dve", negF[0:nr0, tg0:tg0 + nt, :], ps[bL][0:nr0, 32:32 + 4 * nt].rearrange("p (t h) -> p t h", t=nt), -1.0, None,
                       ALU.mult, None, [psk[bL]], ["negF"])
                    TS("dve", f8[0:nr0, 0:nt, :], ps[bL][0:nr0, 32:32 + 4 * nt].rearrange("p (t h) -> p t h", t=nt), 8.0, None,
                       ALU.mult, None, [psk[bL]], ["f8"])
                    if not sample:
                        for i in range(nt):
                            TT("dve", Rrun[:], Rrun[:], lf[:, i, :], ALU.add, ["Rrun", "lf"], ["Rrun"])
                    for i, (r0, nr) in enumerate(g.tiles):
                        TR(ps[bT3][0:4, i * 128:i * 128 + nr], f8[0:nr, i, :], cst[0:nr, 0:nr], ["f8", "cst"], [psk[bT3]])
                    CP("act", ckTg[:, 0:W], ps[bT1][:, 0:W], [psk[bT1]], ["ckTg"])
                    CP("dve", kpTg[:, 0:W], ps[bT2][0:32, 0:W], [psk[bT2]], ["kpTg"])
                    put(KP[:, c0:c0 + W], kpTg[:, 0:W], "kpTg")
                    CP("dve", fqs[:, 0:W], ps[bT3][0:4, 0:W], [psk[bT3]], ["fqs"])
                    put(FQ[:, c0:c0 + W], fqs[:, 0:W], "fqs")
                    b = jrot.next()
                    for c in range(2):
                        MM(ps[b][:, 0:W], cstb[:, 256:384], sqb[:, c, 0:W], c == 0, c == 1, ["cstb", "sqb"], [psk[b]])
                    ACT(rsbt[:, 0:W], ps[b][:, 0:W], AF.Ln, [psk[b]], ["rsbt"], bias=epst[:, :], scale=-1.0 / 256)
                    ACT(rsb[:, 0:W], rsbt[:, 0:W], AF.Exp, ["rsbt"], ["rsb"], scale=-0.5)
                    for c in range(2):
                        STT(cqg[:, c, 0:W], cqT[:, c, 0:W], prmT[:, 40 + c:41 + c], rsb[:, 0:W], ALU.mult, ALU.mult, ["cqT", "prmT", "rsb"], ["cqg"])
                    for hp in range(4):
                        b = jrot.next()
                        MM(ps[b][:, 0:W], wuk[:, hp * 128:(hp + 1) * 128], ckTg[:, 0:W], True, True, ["wuk", "ckTg"], [psk[b]])
                        sb_i = stbrot.next()
                        CP(evrot.next(), stb[sb_i][:, 0:W], ps[b][:, 0:W], [psk[b]], [f"stb{sb_i}"])
                        put(KM[hp * 128:(hp + 1) * 128, c0:c0 + W], stb[sb_i][:, 0:W], f"stb{sb_i}")
                    for i, (r0, nr) in enumerate(g.tiles):
                        b = jrot.next()
                        MM(ps[b][0:nr, :], ckTg[:, i * 128:i * 128 + nr], wuv[:], True, True, ["wuv", "ckTg"], [psk[b]])
                        vi = vrot.next()
                        CP(evrot.next(), vst[vi][0:nr, :, 0:64], ps[b][0:nr, :].rearrange("p (h d) -> p h d", h=8), [psk[b]], [f"vst{vi}"])
                        P.dma(VM[c0 + i * 128:c0 + i * 128 + nr, :, :], vst[vi][0:nr, :, :], reads=[f"vst{vi}"])
                    for h in range(8):
                        bA = qhrot.next()
                        bB = qhrot.next()
                        for c in range(2):
                            MM(ps[bA][0:96, 0:W], wuq[:, c, h * 96:(h + 1) * 96], cqg[:, c, 0:W], c == 0, c == 1, ["wuq", "cqg"], [psk[bA]])
                        for c in range(2):
                            MM(ps[bB][0:96, 0:W], wuqs[:, c, h * 96:(h + 1) * 96], cqg[:, c, 0:W], c == 0, c == 1, ["wuqs", "cqg"], [psk[bB]])
                        qi = qrot.next()
                        q_ = qst[qi]
                        qk = f"qst{qi}"
                        ti = tqrot.next()
                        CP("act", q_[0:64, 0:W], ps[bA][0:64, 0:W], [psk[bA]], [qk])
                        TT("dve", tq1[ti][64:96, 0:W], ps[bA][64:96, 0:W], ropet[64:96, 0, 0:W], ALU.mult, [psk[bA], "ropet"], [f"tq1_{ti}"])
                        TT("dve", tq2[ti][64:96, 0:W], ps[bB][64:96, 0:W], ropet[64:96, 1, 0:W], ALU.mult, [psk[bB], "ropet"], [f"tq2_{ti}"])
                        TT("pool", q_[64:96, 0:W], tq1[ti][64:96, 0:W], tq2[ti][64:96, 0:W], ALU.add, [f"tq1_{ti}", f"tq2_{ti}"], [qk + "b"])
                        P.dma(QM[h, :, c0:c0 + W], q_[0:96, 0:W], reads=[qk, qk + "b"], writes=[])

                groups = [PG(gi, 512, [(gi * 512 + i * 128, 128) for i in range(4)], False) for gi in range(8)]
                groups.append(PG(8, SS, [(0, SS)], True))
                P_A1(groups[0])
                P_A2(groups[0])
                for gi, g in enumerate(groups):
                    nx = groups[gi + 1] if gi + 1 < len(groups) else None
                    if nx is not None:
                        P_A1(nx)
                    P_B(g)
                    if nx is not None:
                        P_A2(nx)
                    P_C(g)
                P.barrier()
                P.emit()

            wst_cur[0].close()
            if do_b:
                with ExitStack() as st:
                    KTb = [sbt(st, f"KT{i}", [128, S], BF16) for i in range(2)]
                    KTs = [sbt(st, f"KTs{i}", [128, PAST + SS], BF16) for i in range(4)]
                    Vbig = [sbt(st, f"Vbig{i}", [128, 32, 4, 128], BF16) for i in range(2)]
                    Vs = [sbt(st, f"Vs{i}", [128, 17, 128], BF16) for i in range(4)]
                    Qt = [sbt(st, f"Qt{i}", [128, 512], BF16) for i in range(5)]
                    SGt = [sbt(st, f"SGt{i}", [64, 512], BF16) for i in range(5)]
                    Qs = sbt(st, "Qs", [128, 4, SS], BF16)
                    SGs = sbt(st, "SGs", [64, 4, SS], BF16)
                    Pb = [sbt(st, f"Pb{i}", [128, 512], BF16) for i in range(5)]
                    Eb = [sbt(st, f"Eb{i}", [128, 512], F32) for i in range(2)]
                    SPb = [sbt(st, f"SPb{i}", [128, 512], BF16) for i in range(4)]
                    SPs = [sbt(st, f"SPs{i}", [128, 512], BF16) for i in range(5)]
                    Pbs = sbt(st, "Pbs", [128, 4, 6, SS], BF16)
                    Ebs = sbt(st, "Ebs", [128, 4, 2, SS], F32)
                    SPbs = sbt(st, "SPbs", [128, 4, 4, SS], BF16)
                    SPss = sbt(st, "SPss", [128, 4, 5, SS], BF16)
                    rl = sbt(st, "rl", [64, 512], F32)
                    tmpb = sbt(st, "tmpb", [64, 512], F32)
                    yb = [sbt(st, f"yb{i}", [64, 512], BF16) for i in range(2)]
                    cstg = [sbt(st, f"cstg{i}", [128, 16, 128], F32) for i in range(1)]
                    kst = sbt(st, "kst", [128, 4, 16, 64], BF16)
                    ckTs = sbt(st, "ckTs", [128, PAST], BF16)
                    qrot2 = Rot([0, 1, 2, 3, 4])
                    yrot = Rot([0, 1])
                    orot = Rot([4, 5, 6])
                    jr2 = Rot([2, 3])
                    ktrot = Rot([0, 1])
                    cgrot = Rot([0, 1])
                    for i in range(4):
                        MEMSET("pool", Vs[i][:, :, 64:128], 2.0, [f"Vs{i}"])

                    class Bufs:
                        def __init__(self, sample_slot=None):
                            self.ss = sample_slot
                            self.c = {}

                        def _n(self, name, n):
                            i = self.c.get(name, 0)
                            self.c[name] = i + 1
                            return i % n

                        def S(self):
                            i = self._n("S", 4) if self.ss is None else self.ss
                            return ps[i], psk[i]

                        def Z(self):
                            i = self._n("Z", 2) if self.ss is None else (self.ss % 2)
                            return ps[i], psk[i]

                        def W(self):
                            i = 2 + (self._n("W", 2) if self.ss is None else (self.ss % 2))
                            return ps[i], psk[i]

                        def P(self):
                            if self.ss is None:
                                i = self._n("P", 5)
                                return Pb[i], f"Pb{i}"
                            i = self._n("P", 6)
                            return Pbs[:, self.ss, i, :], f"Pbs{self.ss}_{i}"

                        def E(self):
                            if self.ss is None:
                                i = self._n("E", 2)
                                return Eb[i], f"Eb{i}"
                            i = self._n("E", 2)
                            return Ebs[:, self.ss, i, :], f"Ebs{self.ss}_{i}"

                        def SP(self):
                            if self.ss is None:
                                i = self._n("SP", 4)
                                return SPb[i], f"SPb{i}"
                            i = self._n("SP", 4)
                            return SPbs[:, self.ss, i, :], f"SPbs{self.ss}_{i}"

                        def SS_(self):
                            if self.ss is None:
                                i = self._n("SS", 5)
                                return SPs[i], f"SPs{i}"
                            i = self._n("SS", 5)
                            return SPss[:, self.ss, i, :], f"SPss{self.ss}_{i}"

                    def run_softmax(problems, LA=3):
                        problems = [(p[0], p[1], (p[2] if len(p) > 2 else 0)) for p in problems]
                        N = max(len(p[0]) + p[2] for p in problems)
                        for s0 in range(N + LA):
                            for (items, bf, off) in problems:
                                s = s0 - off
                                if 0 <= s < len(items):
                                    it = items[s]
                                    if it.get("pre"):
                                        it["pre"]()
                                    kw, qw, cl, mw = it["kw"], it["qw"], it.get("cl", 0), it.get("mw", 128)
                                    sb_, sk_ = bf.S()
                                    last = it["mask"] is None
                                    MM(sb_[0:kw, cl:qw], it["kt"], it["q"][:, cl:qw], True, last, it["rk"] + it["qk"], [sk_], sig=last)
                                    if not last:
                                        MM(sb_[0:kw, cl:cl + mw], identb[0:kw, 0:kw], it["mask"], False, True, ["cstb", "mk"], [sk_], sig=True)
                                    pt, pk = bf.P()
                                    ACT(pt[0:kw, cl:qw], sb_[0:kw, cl:qw], AF.Exp, [sk_] + it["bk"], [pk], bias=it["bias"], scale=it["scale"])
                                    it["p"] = (pt, pk)
                            for (items, bf, off) in problems:
                                j = s0 - off - LA
                                if 0 <= j < len(items):
                                    it = items[j]
                                    kw, qw, cl = it["kw"], it["qw"], it.get("cl", 0)
                                    pt, pk = it["p"]
                                    ob = it["ob"]
                                    oc = it.get("oc", 0)
                                    MM(ps[ob][0:it["M"], oc + cl:oc + qw], it["v"], pt[0:kw, cl:qw], it["first"] and not it.get("nostart"), it["last"],
                                       it["rk"] + [pk], [psk[ob]], sig=it["last"])
                                    if it["last"]:
                                        it["fin"]()

                    def run_sb(problems):
                        problems = [(p[0], p[1], (p[2] if len(p) > 2 else 0)) for p in problems]
                        N = max(len(p[0]) + p[2] for p in problems)
                        for s0 in range(N + 5):
                            for (items, bf, off) in problems:
                                s = s0 - off
                                if 0 <= s < len(items):
                                    it = items[s]
                                    if it.get("pre"):
                                        it["pre"]()
                                    kw, qw, cl, mw = it["kw"], it["qw"], it.get("cl", 0), it.get("mw", 128)
                                    zb, zk = bf.Z()
                                    last = it["mask"] is None
                                    MM(zb[0:kw, cl:qw], it["kt"], it["q"][:, cl:qw], True, last, it["rk"] + it["qk"], [zk], sig=last)
                                    if not last:
                                        MM(zb[0:kw, cl:cl + mw], identb[0:kw, 0:kw], it["mask"], False, True, ["cstb", "mk"], [zk], sig=True)
                                    et, ek = bf.E()
                                    ACT(et[0:kw, cl:qw], zb[0:kw, cl:qw], AF.Exp, [zk], [ek])
                                    it["e"] = (et, ek)
                            for (items, bf, off) in problems:
                                s = s0 - off - 1
                                if 0 <= s < len(items):
                                    it = items[s]
                                    kw, qw, cl, mw = it["kw"], it["qw"], it.get("cl", 0), it.get("mw", 128)
                                    et, ek = it["e"]
                                    spt, spk = bf.SP()
                                    ACT(spt[0:kw, cl:qw], et[0:kw, cl:qw], AF.Ln, [ek], [spk], bias=1.0)
                                    sst, ssk = bf.SS_()
                                    if it["first"]:
                                        if kw < 128:
                                            MEMSET("pool", sst[:, 0:qw], 0.0, [ssk])
                                        CP("pool", sst[0:kw, cl:qw], spt[0:kw, cl:qw], [spk], [ssk])
                                    else:
                                        pst, psk_ = items[s - 1]["ss"]
                                        pcl = items[s - 1].get("cl", 0)
                                        TT("dve", sst[:, pcl:qw], pst[:, pcl:qw], spt[:, pcl:qw], ALU.add, [psk_, psk_ + "x", spk], [ssk])
                                        if pcl > cl:
                                            CP("pool", sst[:, cl:pcl], spt[:, cl:pcl], [spk], [ssk + "x"])
                                    it["sp"] = (spt, spk)
                                    it["ss"] = (sst, ssk)
                            for (items, bf, off) in problems:
                                j = s0 - off - 3
                                if 0 <= j < len(items):
                                    it = items[j]
                                    kw, qw, cl, mw = it["kw"], it["qw"], it.get("cl", 0), it.get("mw", 128)
                                    wb, wk_ = bf.W()
                                    MM(wb[0:kw, cl:qw], it["kt"], it["q"][:, cl:qw], True, False, it["rk"] + it["qk"], [wk_], sig=False)
                                    if it["mask"] is not None:
                                        MM(wb[0:kw, cl:cl + mw], identb[0:kw, 0:kw], it["mask"], False, False, ["cstb", "mk"], [wk_], sig=False)
                                    spt, spk = it["sp"]
                                    MM(wb[0:kw, cl:qw], tinclneg[0:kw, 0:kw], spt[0:kw, cl:qw], False, it["first"], ["cstb", spk], [wk_], sig=it["first"])
                                    if not it["first"]:
                                        pst, psk_ = items[j - 1]["ss"]
                                        pcl = items[j - 1].get("cl", 0)
                                        MM(wb[0:kw, pcl:qw], onesneg[:, 0:kw], pst[:, pcl:qw], False, True, ["cstb", psk_, psk_ + "x"], [wk_], sig=True)
                                    pt, pk = bf.P()
                                    ACT(pt[0:kw, cl:qw], wb[0:kw, cl:qw], AF.Exp, [wk_], [pk])
                                    it["p"] = (pt, pk)
                            for (items, bf, off) in problems:
                                j = s0 - off - 5
                                if 0 <= j < len(items):
                                    it = items[j]
                                    kw, qw, cl = it["kw"], it["qw"], it.get("cl", 0)
                                    pt, pk = it["p"]
                                    ob = it["ob"]
                                    oc = it.get("oc", 0)
                                    MM(ps[ob][0:it["M"], oc + cl:oc + qw], it["v"], pt[0:kw, cl:qw], it["first"] and not it.get("nostart"), it["last"],
                                       it["rk"] + [pk], [psk[ob]], sig=it["last"])
                                    if it["last"]:
                                        it["fin"]()

                    def finalize(kind, ob, sg_ap, sgk, qw, yrow, c0, ro=0, oc=0):
                        yi = yrot.next()
                        if kind == "sb":
                            STT(yb[yi][:, 0:qw], ps[ob][ro:ro + 64, oc:oc + qw], 1.0, sg_ap, ALU.mult, ALU.mult, [psk[ob], sgk], [f"yb{yi}"])
                        else:
                            P.op("dve", lambda e: e.reciprocal(out=rl[:, 0:qw], in_=ps[ob][64:128, oc:oc + qw]), reads=[psk[ob]], writes=["rl"])
                            TT("pool", tmpb[:, 0:qw], sg_ap, rl[:, 0:qw], ALU.mult, [sgk, "rl"], ["tmpb"])
                            STT(yb[yi][:, 0:qw], ps[ob][0:64, oc:oc + qw], 2.0, tmpb[:, 0:qw], ALU.mult, ALU.mult, [psk[ob], "tmpb"], [f"yb{yi}"])
                        P.dma(YT[yrow:yrow + 64, c0:c0 + qw], yb[yi][:, 0:qw], reads=[f"yb{yi}"])

                    def run_group(kind, hook):
                        nheads = 8 if kind == "mla" else 4
                        Qsrc = {"fox": QF, "sb": QB}.get(kind)
                        Ksrc = {"fox": KF, "sb": KB, "mla": KM}[kind]
                        Gsrc = {"fox": GF, "sb": GB, "mla": GM}[kind]
                        ybase = {"fox": 0, "sb": 256, "mla": 512}[kind]
                        Kd = {"fox": 65, "sb": 128, "mla": 96}[kind]
                        M = 128
                        mkind = {"fox": 0, "sb": 1, "mla": 2}[kind]
                        scale = {"fox": 0.125, "sb": 1.0, "mla": MLA_SCALE}[kind]
                        runner = run_sb if kind == "sb" else run_softmax

                        def sample_loads(hs, sl0, emit=True):
                            ctxs = []
                            for sl, h in enumerate(hs, start=sl0):
                                Ks, ksk = KTs[sl], f"KTs{sl}"
                                Vh, vsk = Vs[sl], f"Vs{sl}"
                                if not emit:
                                    ctxs.append(dict(h=h, sl=sl, Ks=Ks, ksk=ksk, Vh=Vh, vsk=vsk))
                                    continue
                                if kind in ("fox", "sb"):
                                    ksrc_c = cfk if kind == "fox" else csk
                                    vsrc_c = cfv if kind == "fox" else csv
                                    P.dma(kst[:, sl, :, :], ksrc_c[l, :, h * 64:(h + 1) * 64].rearrange("(t p) c -> p t c", p=128),
                                          writes=[f"kst{sl}"], queue="pool")
                                    P.dma(Vh[:, 0:16, 0:64], vsrc_c[l, :, h * 64:(h + 1) * 64].rearrange("(t p) c -> p t c", p=128),
                                          writes=[vsk], queue="pool")
                                    if kind == "fox":
                                        MEMSET("pool", Ks[64:65, :], 1.0, [ksk + "a"])
                                        P.dma(Vh[0:SS, 16, :], VF[S:S + SS, h, :], writes=[vsk + "n"])
                                        P.dma(Qs[64:65, sl, :], FQ[h:h + 1, S:S + SS], writes=[f"Qs{sl}a"])
                                    else:
                                        MEMSET("pool", Ks[64:128, :], 0.0, [ksk + "a"])
                                        MEMSET("pool", Qs[64:128, sl, :], 0.0, [f"Qs{sl}a"])
                                        P.dma(Vh[0:SS, 16, 0:64], VB[S:S + SS, h * 64:(h + 1) * 64], writes=[vsk + "n"])
                                    P.dma(Qs[0:64, sl, :], Qsrc[h * 64:(h + 1) * 64, S:S + SS], writes=[f"Qs{sl}"])
                                else:
                                    P.dma(Vh[0:SS, 16, :], VM[S:S + SS, h, :], writes=[vsk + "n"])
                                    P.dma(Qs[0:96, sl, :], QM[h, :, S:S + SS], writes=[f"Qs{sl}"])
                                P.dma(Ks[0:64, PAST:PAST + SS], Ksrc[h * 64:(h + 1) * 64, S:S + SS], writes=[ksk + "n"])
                                P.dma(SGs[:, sl, :], Gsrc[h * 64:(h + 1) * 64, S:S + SS], writes=[f"SGs{sl}"])
                                ctxs.append(dict(h=h, sl=sl, Ks=Ks, ksk=ksk, Vh=Vh, vsk=vsk))
                            return ctxs

                        def sample_prep(ctxs):
                            for c_ in ctxs:
                                h, sl, Ks, ksk, Vh, vsk = c_["h"], c_["sl"], c_["Ks"], c_["ksk"], c_["Vh"], c_["vsk"]
                                if kind in ("fox", "sb"):
                                    for g2 in range(2):
                                        b = jr2.next()
                                        psb = ps[b][:].bitcast(BF16)
                                        for j in range(8):
                                            TR(psb[0:64, j * 128:(j + 1) * 128], kst[:, sl, g2 * 8 + j, :], identb, [f"kst{sl}", "cstb"], [psk[b]])
                                        CP(evrot.next(), Ks[0:64, g2 * 1024:(g2 + 1) * 1024], psb[0:64, 0:1024], [psk[b]], [ksk])
                                else:
                                    for g4 in range(4):
                                        b = jr2.next()
                                        MM(ps[b][0:64, :], wuk[:, h * 64:(h + 1) * 64], ckTs[:, g4 * 512:(g4 + 1) * 512], True, True, ["wuk", "ckTs"], [psk[b]])
                                        CP(evrot.next(), Ks[0:64, g4 * 512:(g4 + 1) * 512], ps[b][0:64, :], [psk[b]], [ksk])
                                    for g4 in range(4):
                                        b = jr2.next()
                                        for i in range(4):
                                            t = g4 * 4 + i
                                            MM(ps[b][:, i * 64:(i + 1) * 64], ckTs[:, t * 128:(t + 1) * 128], wuv[:, h * 64:(h + 1) * 64], True, True,
                                               ["wuv", "ckTs"], [psk[b]])
                                        CP(evrot.next(), Vh[:, g4 * 4:(g4 + 1) * 4, 0:64], ps[b][:, 0:256].rearrange("p (t d) -> p t d", t=4), [psk[b]], [vsk])

                        def mla_sample_once():
                            P.dma(kst[:, 0, :, 0:32], ckp[l].rearrange("(t p) c -> p t c", p=128), writes=["kst0"], queue="pool")
                            for g2 in range(2):
                                b = jr2.next()
                                psb = ps[b][:].bitcast(BF16)
                                for j in range(8):
                                    TR(psb[0:32, j * 128:(j + 1) * 128], kst[:, 0, g2 * 8 + j, 0:32], identb, ["kst0", "cstb"], [psk[b]])
                                for sl in range(4):
                                    CP(evrot.next(), KTs[sl][64:96, g2 * 1024:(g2 + 1) * 1024], psb[0:32, 0:1024], [psk[b]], [f"KTs{sl}a"])
                            for sl in range(4):
                                P.dma(KTs[sl][64:96, PAST:PAST + SS], KP[:, S:S + SS], writes=[f"KTs{sl}b"])

                        def sample_attn(ctxs):
                            problems = []
                            for c_ in ctxs:
                                h, sl, Ks, ksk, Vh, vsk = c_["h"], c_["sl"], c_["Ks"], c_["ksk"], c_["Vh"], c_["vsk"]
                                rks = [ksk, ksk + "a", ksk + "b", ksk + "n", vsk, vsk + "n"]
                                ob = sl
                                qap = Qs[0:Kd, sl, :]
                                qk = [f"Qs{sl}", f"Qs{sl}a"]
                                items = []
                                for kb in range(16):
                                    items.append(dict(kt=Ks[0:Kd, kb * 128:(kb + 1) * 128], v=Vh[:, kb, 0:M], kw=128, qw=SS, mask=None,
                                                      bias=(negFp[:, kb, h:h + 1] if kind == "fox" else None),
                                                      bk=(["negFp"] if kind == "fox" else []), rk=rks))
                                smask = None
                                if kind != "mla":
                                    ms = mkind * 896 + 384
                                    smask = mk[0:SS, ms:ms + SS]
                                items.append(dict(kt=Ks[0:Kd, PAST:PAST + SS], v=Vh[0:SS, 16, 0:M], kw=SS, qw=SS, mask=smask, cl=0, mw=SS,
                                                  bias=(negF[0:SS, 33, h:h + 1] if kind == "fox" else None),
                                                  bk=(["negF"] if kind == "fox" else []), rk=rks))
                                if kind == "sb":
                                    items = items[::-1]
                                obank = 4 + (sl % 2) if False else None
                                for ii, it in enumerate(items):
                                    it.update(q=qap, qk=qk, scale=scale, M=M, first=(ii == 0), last=(ii == len(items) - 1), ob=None, fin=None)
                                problems.append((items, Bufs(sample_slot=sl), c_))
                            MM(ps[7][:, 0:4 * SS], zb[:], zb[:, 0:4 * SS], True, True, ["zb"], ["ps7"])
                            pr = []
                            for (items, bf, c_) in problems:
                                h, sl = c_["h"], c_["sl"]
                                for it in items:
                                    it.update(ob=7, oc=sl * SS, nostart=True)
                                items[-1]["fin"] = (lambda sl=sl, h=h: finalize(kind, 7, SGs[:, sl, :], f"SGs{sl}", SS, ybase + h * 64, S, oc=sl * SS))
                                pr.append((items, bf))
                            return pr

                        def vidx(h):
                            return ((0 if kind != "sb" else 1) if h < 4 else 1)

                        def prep_head(h):
                            ctx = dict(h=h)
                            if h % 4 == 0:
                                vi = vidx(h)
                                vk = f"Vbig{vi}"
                                if kind == "fox":
                                    for q4 in range(4):
                                        P.dma(Vbig[vi][:, q4 * 8:(q4 + 1) * 8, :, :],
                                              VF[q4 * 1024:(q4 + 1) * 1024, :, :].rearrange("(t p) h c -> p t h c", p=128), writes=[vk])
                                elif kind == "sb":
                                    vv = Vbig[vi][:].rearrange("p t h c -> p (t h c)")[:, 0:32 * 256].rearrange("p (t c) -> p t c", t=32)
                                    for q4 in range(4):
                                        P.dma(vv[:, q4 * 8:(q4 + 1) * 8, :],
                                              VB[q4 * 1024:(q4 + 1) * 1024, :].rearrange("(t p) c -> p t c", p=128), writes=[vk])
                                else:
                                    for q4 in range(4):
                                        P.dma(Vbig[vi][:, q4 * 8:(q4 + 1) * 8, :, :],
                                              VM[q4 * 1024:(q4 + 1) * 1024, h:h + 4, :].rearrange("(t p) h c -> p t h c", p=128), writes=[vk])
                            ki = h % 2
                            KT = KTb[ki]
                            kk = f"KT{ki}"
                            P.dma(KT[0:64, :], Ksrc[h * 64:(h + 1) * 64, 0:S], writes=[kk])
                            if kind == "fox":
                                MEMSET("pool", KT[64:65, :], 1.0, [kk + "a"])
                            elif kind == "mla":
                                P.dma(KT[64:96, :], KP[:, 0:S], writes=[kk + "a"])
                            else:
                                MEMSET("pool", KT[64:128, :], 0.0, [kk + "a"])
                            ctx.update(KT=KT, kk=kk)
                            return ctx

                        def load_q(h, qt):
                            qi = qrot2.next()
                            qk = f"Qt{qi}"
                            c0 = qt * 512
                            if kind == "mla":
                                P.dma(Qt[qi][0:96, :], QM[h, :, c0:c0 + 512], writes=[qk])
                            else:
                                P.dma(Qt[qi][0:64, :], Qsrc[h * 64:(h + 1) * 64, c0:c0 + 512], writes=[qk])
                                if kind == "fox":
                                    P.dma(Qt[qi][64:65, :], FQ[h:h + 1, c0:c0 + 512], writes=[qk + "a"])
                                else:
                                    MEMSET("pool", Qt[qi][64:128, :], 0.0, [qk + "a"])
                            P.dma(SGt[qi][:, :], Gsrc[h * 64:(h + 1) * 64, c0:c0 + 512], writes=[f"SGt{qi}"])
                            return qi

                        if kind == "mla":
                            P.dma(cstg[0][:], cck[l].rearrange("(t p) c -> p t c", p=128), writes=["cstg0", "cstg0v"])
                            for g4 in range(4):
                                b = jr2.next()
                                for i in range(4):
                                    TR(ps[b][:, i * 128:(i + 1) * 128], cstg[0][:, g4 * 4 + i, :], ident, ["cstg0", "cst"], [psk[b]])
                                CP(evrot.next(), ckTs[:, g4 * 512:(g4 + 1) * 512], ps[b][:, :], [psk[b]], ["ckTs"])

                        if kind == "mla":
                            mla_sample_once()
                        if kind != "fox":
                            c03 = sample_loads([0, 1, 2, 3], 0)
                            sample_prep(c03)
                            yield
                            spr0 = sample_attn(c03)
                        else:
                            yield
                            c03 = sample_loads([0, 1, 2, 3], 0, emit=False)
                            spr0 = [(p[0], p[1], 150) for p in sample_attn(c03)]
                        pbufs = Bufs()
                        late = []
                        prep_head(0)
                        all_items = []
                        qpre = {}
                        qseq = [(h_, q_) for h_ in range(nheads) for q_ in range(8)]
                        qpre[(0, 0)] = load_q(0, 0)
                        qpre[(0, 1)] = load_q(0, 1)
                        for h in range(nheads):
                            vc = vidx(h)
                            vk = f"Vbig{vc}"
                            KT, kk = KTb[h % 2], f"KT{h % 2}"
                            if kind == "sb":
                                vv = Vbig[vc][:].rearrange("p t h c -> p (t h c)")[:, 0:32 * 256].rearrange("p (t c) -> p t c", t=32)
                                Vof = lambda t, h=h, vv=vv: vv[:, t, (h // 2) * 128:(h // 2) * 128 + 128]
                            else:
                                Vof = lambda t, h=h, vc=vc: Vbig[vc][:, t, h % 4, :]
                            items = []
                            for qt in range(8):
                                c0 = qt * 512
                                nkb = 4 * qt + 4
                                ob = orot.next()
                                blk = []
                                for kb in range(nkb):
                                    r = kb - 4 * qt
                                    mask = None
                                    cl = 0
                                    if r >= 0:
                                        ms = mkind * 896 + 384
                                        mask = mk[:, ms:ms + 128]
                                        cl = 128 * r
                                    blk.append(dict(kt=KT[0:Kd, kb * 128:(kb + 1) * 128], v=Vof(kb), kw=128, qw=512, mask=mask, cl=cl, mw=128,
                                                    bias=(negF[:, kb, h:h + 1] if kind == "fox" else None),
                                                    bk=(["negF"] if kind == "fox" else []), rk=[kk, kk + "a", vk],
                                                    scale=scale, M=M, ob=ob, first=False, last=False, fin=None, qt=qt))
                                if kind == "sb":
                                    blk = blk[::-1]
                                blk[0]["first"] = True
                                blk[-1]["last"] = True
                                items.extend(blk)
                            def mk_pre(h, qt, blk_items):
                                def pre():
                                    if qt == 1 and h + 1 < nheads:
                                        prep_head(h + 1)
                                    if kind == "fox" and h == 0 and qt == 4:
                                        sample_prep(sample_loads([0, 1, 2, 3], 0))
                                    if h == 2 and qt == 2 and hook is not None:
                                        hook()
                                    if kind == "mla" and h == 5 and qt == 2:
                                        c47 = sample_loads([4, 5, 6, 7], 0)
                                        sample_prep(c47)
                                        late.append(c47)
                                    qi = qpre.pop((h, qt))
                                    for it in blk_items:
                                        it["q"] = Qt[qi][0:Kd, :]
                                        it["qk"] = [f"Qt{qi}", f"Qt{qi}a"]
                                    blk_items[-1]["fin"] = (lambda qi=qi, ob=blk_items[0]["ob"], qt=qt, h=h:
                                                            finalize(kind, ob, SGt[qi][:, :], f"SGt{qi}", 512, ybase + h * 64, qt * 512,
                                                                     ro=((h % 2) * 64 if kind == "sb" else 0)))
                                    idx = h * 8 + qt + 2
                                    if idx < len(qseq):
                                        qpre[qseq[idx]] = load_q(*qseq[idx])
                                return pre
                            pos = 0
                            for qt in range(8):
                                nkb = 4 * qt + 4
                                seg = items[pos:pos + nkb]
                                seg[0]["pre"] = mk_pre(h, qt, seg)
                                pos += nkb
                            all_items.extend(items)
                        runner([(all_items, pbufs)] + spr0)

                        for c47 in late:
                            runner(sample_attn(c47))

                    g_mla = run_group("mla", None)
                    g_sb = run_group("sb", lambda: next(g_mla))
                    g_fox = run_group("fox", lambda: next(g_sb))
                    next(g_fox)
                    for g_ in (g_fox, g_sb, g_mla):
                        next(g_, None)
                    P.barrier()
                    P.emit()

            if l + 1 < depth:
                wst_cur[0] = ExitStack()
                cur_w[0] = alloc_weights(wst_cur[0])
                wgen[0] = load_weights(l + 1, cur_w[0], ("pool",))
            if do_c:
                with ExitStack() as st:
                    wob = sbt(st, "wob", [128, 8, D], BF16)
                    stg = [sbt(st, f"stgc{i}", [128, 2048], F32) for i in range(2)]
                    ytl = [sbt(st, f"ytl{i}", [128, 8, 512], BF16) for i in range(2)]
                    xr = [sbt(st, f"xr{i}", [128, D], F32) for i in range(2)]
                    tt_ = [sbt(st, f"tt{i}", [128, D], F32) for i in range(2)]
                    xo = [sbt(st, f"xo{i}", [128, D], F32) for i in range(2)]
                    sqc = sbt(st, "sqc", [128, 512], BF16)
                    stc = sbt(st, "stc", [128, 8], F32)
                    wv = w_out[l].rearrange("(k p) c -> p k c", p=128)
                    for c4 in range(4):
                        sv = stg[c4 % 2][:, 0:2048].rearrange("p (k c) -> p k c", k=8)
                        P.dma(sv, wv[:, :, c4 * 256:(c4 + 1) * 256], writes=[f"stgc{c4 % 2}"])
                        CP(["pool", "dve"][c4 % 2], wob[:, :, c4 * 256:(c4 + 1) * 256], sv, [f"stgc{c4 % 2}"], ["wob"])
                    yv = YT.rearrange("(k p) t -> p k t", p=128)
                    crot = Rot([0, 1])
                    brot = Rot([0, 2, 4, 6])

                    def out_tile(ytile, ykey, coff, nr, xsrc, xkey, gg, dst, dkey):
                        b0 = brot.next()
                        for hf in range(2):
                            for k in range(8):
                                MM(ps[b0 + hf][0:nr, :], ytile[:, k, coff:coff + nr], wob[:, k, hf * 512:(hf + 1) * 512], k == 0, k == 7,
                                   [ykey, "wob"], [psk[b0 + hf]])
                        ci = crot.next()
                        for hf in range(2):
                            ACT(sqc[0:nr, :], ps[b0 + hf][0:nr, :], AF.Square, [psk[b0 + hf]], ["sqc"], accum=stc[0:nr, hf:hf + 1])
                        TT("dve", stc[0:nr, 2:3], stc[0:nr, 0:1], stc[0:nr, 1:2], ALU.add, ["sqc"], ["stc2"])
                        rstd_from_ssq(stc[0:nr, 4:5], stc[0:nr, 2:3], D, nr, ["stc2"], "stc4", stc[0:nr, 3:4])
                        for hf in range(2):
                            TT("dve", tt_[ci][0:nr, hf * 512:(hf + 1) * 512], ps[b0 + hf][0:nr, :], gg[0:nr, hf * 512:(hf + 1) * 512], ALU.mult,
                               [psk[b0 + hf], "ggP", "ggS"], [f"tt{ci}"])
                        STT(dst[0:nr, :], tt_[ci][0:nr, :], stc[0:nr, 4:5], xsrc[0:nr, :], ALU.mult, ALU.add, [f"tt{ci}", "stc4", xkey], [dkey])

                    for g in range(8):
                        yi = g % 2
                        P.dma(ytl[yi][:], yv[:, :, g * 512:(g + 1) * 512], reads=["YT"], writes=[f"ytl{yi}"])
                        for i in range(4):
                            t = g * 4 + i
                            xi = crot.i % 2
                            P.dma(xr[xi][:], xin[t * 128:(t + 1) * 128, :], reads=[f"x1_{t}"] if l > 0 else [], writes=[f"xr{xi}"])
                            out_tile(ytl[yi], f"ytl{yi}", i * 128, 128, xr[xi], f"xr{xi}", ggP, xo[xi], f"xo{xi}")
                            P.dma(xout[t * 128:(t + 1) * 128, :], xo[xi][:], reads=[f"xo{xi}"], writes=[f"x1_{t}"], queue="pool")
                            if wgen[0] is not None and t % 2 == 1:
                                next(wgen[0], None)
                    if wgen[0] is not None:
                        for _ in wgen[0]:
                            pass
                        wgen[0] = None
                    P.dma(ytl[0][:, :, 0:SS], yv[:, :, S:S + SS], reads=["YT"], writes=["ytl0"])
                    out_tile(ytl[0], "ytl0", 0, SS, xs, "xs", ggS, xs, "xs")
                    if l == depth - 1:
                        P.dma(y_s, xs[:], reads=["xs"])
                    P.barrier()
                    P.emit()

          except StopBuild:
            P.barrier()
            break
        P.finish()
        P.emit()
    return nc


_CACHE = {}


def _host_consts():
    c = np.zeros((128, 512 + 3 * 896), np.float32)
    idx = np.arange(128)
    c[:, 0:128] = np.eye(128, dtype=np.float32)
    c[:, 128:256] = (idx[:, None] <= idx[None, :]).astype(np.float32)
    c[:, 256:384] = 1.0
    c[:, 384:512] = -(idx[:, None] >= idx[None, :]).astype(np.float32)
    k = idx[:, None]
    q = idx[None, :]
    tri = [np.where(k > q, NEG, 0.0), np.where(k >= q, NEG, 0.0), np.where((k // 64) > (q // 64), NEG, 0.0)]
    for m in range(3):
        base = 512 + m * 896
        c[:, base:base + 384] = NEG
        c[:, base + 384:base + 512] = tri[m]
        c[:, base + 512:base + 896] = 0.0
    sel = np.zeros((2, 256), np.float32)
    sel[0, 0:128] = 1.0
    sel[1, 128:256] = 1.0
    half = 16
    inv = (10000.0 ** (-np.arange(half, dtype=np.float32) / half)).astype(np.float32)
    pos = np.arange(S + 128, dtype=np.float32)
    ang = pos[:, None] * inv[None, :]
    cs_tok = np.concatenate([np.cos(ang), np.sin(ang)], axis=1).astype(np.float32)
    cosT = np.cos(ang[:S]).T.astype(np.float32)
    sinT = np.sin(ang[:S]).T.astype(np.float32)
    ropeT = np.stack([np.concatenate([cosT, cosT], 0), np.concatenate([-sinT, sinT], 0)]).astype(np.float32)
    return c, sel, cs_tok, ropeT


def kernel(x_prompt, x_sample, c_prompt, c_sample, cache_fox_k, cache_fox_v, cache_fox_logf,
           cache_sb_k, cache_sb_v, cache_mla_ckv, cache_mla_kpe,
           g_pre, g_post, w_ada, b_ada, w_in, b_f, g_q_a, w_uq, g_kv_a, w_uk, w_uv, w_out):
    f = lambda a: np.ascontiguousarray(np.asarray(a, dtype=np.float32))
    if "nc" not in _CACHE:
        _CACHE["nc"] = build_program()
    nc = _CACHE["nc"]
    consts, sel, cs_tok, ropeT = _host_consts()
    perm = np.arange(768)
    for h in range(8):
        b0 = h * 96 + 64
        perm[b0:b0 + 16] = np.arange(b0 + 16, b0 + 32)
        perm[b0 + 16:b0 + 32] = np.arange(b0, b0 + 16)
    w_uq_f = f(w_uq)
    shared = {
        "g_pre": f(g_pre), "g_post": f(g_post), "w_ada": f(w_ada), "b_ada": f(b_ada), "w_in": f(w_in), "b_f": f(b_f),
        "g_q_a": f(g_q_a), "w_uq": w_uq_f, "w_uqs": np.ascontiguousarray(w_uq_f[:, :, perm]), "g_kv_a": f(g_kv_a),
        "w_uk": f(w_uk).reshape(DEPTH, 128, 512), "w_uv": f(w_uv).reshape(DEPTH, 128, 512), "w_out": f(w_out),
        "consts": consts, "selc": sel, "cs_tok": cs_tok, "ropeT": ropeT,
    }
    xp = f(x_prompt); xsm = f(x_sample); cp = f(c_prompt); csm = f(c_sample)
    cf = [f(a) for a in (cache_fox_k, cache_fox_v, cache_fox_logf, cache_sb_k, cache_sb_v, cache_mla_ckv, cache_mla_kpe)]
    in_maps = []
    for b in range(8):
        m = dict(shared)
        m["x_p"] = xp[b]
        m["x_s"] = xsm[b]
        m["c2"] = np.ascontiguousarray(np.stack([cp[b], csm[b]]))
        m["cfk"] = np.ascontiguousarray(cf[0][:, b].reshape(DEPTH, PAST, 256))
        m["cfv"] = np.ascontiguousarray(cf[1][:, b].reshape(DEPTH, PAST, 256))
        m["cfl"] = np.ascontiguousarray(cf[2][:, b])
        m["csk"] = np.ascontiguousarray(cf[3][:, b].reshape(DEPTH, PAST, 256))
        m["csv"] = np.ascontiguousarray(cf[4][:, b].reshape(DEPTH, PAST, 256))
        m["cck"] = np.ascontiguousarray(cf[5][:, b])
        m["ckp"] = np.ascontiguousarray(cf[6][:, b])
        in_maps.append(m)
    res = run_bass_kernel_spmd(nc, in_maps, core_ids=list(range(8)))
    R = res.results

    def gat(name, shape):
        return np.stack([np.asarray(R[b][name], dtype=np.float32) for b in range(8)], axis=1).reshape(shape)

    y_prompt = np.stack([np.asarray(R[b]["y_p"], dtype=np.float32) for b in range(8)])
    y_sample = np.stack([np.asarray(R[b]["y_s"], dtype=np.float32) for b in range(8)])
    outs = [y_prompt, y_sample]
    for pre, T in (("p", S), ("s", SS)):
        outs.append(gat(f"{pre}_fk", (DEPTH, 8, T, 4, 64)))
        outs.append(gat(f"{pre}_fv", (DEPTH, 8, T, 4, 64)))
        outs.append(gat(f"{pre}_fl", (DEPTH, 8, T, 4)))
        outs.append(gat(f"{pre}_sk", (DEPTH, 8, T, 4, 64)))
        outs.append(gat(f"{pre}_sv", (DEPTH, 8, T, 4, 64)))
        outs.append(gat(f"{pre}_ck", (DEPTH, 8, T, 128)))
        outs.append(gat(f"{pre}_kp", (DEPTH, 8, T, 32)))
    return tuple(outs)
```

```python
from contextlib import ExitStack
import numpy as np
import concourse.bass as bass
import concourse.mybir as mybir
from concourse.bass_utils import run_bass_kernel_spmd

F32 = mybir.dt.float32
BF16 = mybir.dt.bfloat16
ALU = mybir.AluOpType
AF = mybir.ActivationFunctionType
AX = mybir.AxisListType

N_DMA_SEMS = 56
SEM_EPOCH = 30000

S = 4096
D = 1024
SS = 16
PAST = 2048
TA = S + SS
DEPTH = 2
C_FQ, C_FK, C_FV, C_FF, C_FG = 0, 256, 512, 768, 772
C_SQ, C_SK, C_SV, C_SG = 1028, 1284, 1540, 1796
C_CQ, C_CKV, C_KPE, C_MG = 2052, 2308, 2436, 2468
INW = 2980
NEG = -240000.0
EPS = 1e-6
MLA_SCALE = 96.0 ** -0.5


class Prog:
    ENGS = ("pe", "act", "dve", "pool", "sp")

    def __init__(self, nc, stack):
        self.nc = nc
        self.stack = stack
        self.streams = {e: [] for e in self.ENGS}
        self.sems = {}
        self.cur_sem = {}
        self.count = {}
        self.epoch = {e: 0 for e in self.ENGS}
        for e in self.ENGS:
            self._new_epoch(e)
        self.waited = {e: {} for e in self.ENGS}
        self.res = {}
        self.dma_slots = []
        for i in range(N_DMA_SEMS):
            k = ("dma", i)
            self.sems[k] = stack.enter_context(nc.semaphore(f"dma{i}"))
            self.count[k] = 0
            self.dma_slots.append(k)
        self.dma_rr = 0
        self.n_ops = 0

    def _new_epoch(self, e):
        k = (e, self.epoch[e])
        self.sems[k] = self.stack.enter_context(self.nc.semaphore(f"s_{e}_{self.epoch[e]}"))
        self.count[k] = 0
        self.cur_sem[e] = k
        self.epoch[e] += 1

    def _deps(self, reads, writes):
        ev = []
        for r in reads:
            st = self.res.get(r)
            if st and st[0] is not None:
                ev.append(st[0])
        for w in writes:
            st = self.res.get(w)
            if st:
                if st[0] is not None:
                    ev.append(st[0])
                ev.extend(st[1])
        return ev

    def _record(self, reads, writes, event):
        for r in reads:
            st = self.res.setdefault(r, [None, []])
            st[1].append(event)
            if len(st[1]) > 48:
                best = {}
                for (k, v) in st[1]:
                    if best.get(k, -1) < v:
                        best[k] = v
                st[1] = list(best.items())
        for w in writes:
            self.res[w] = [event, []]

    def _waits_for(self, eng, events, skip_self):
        need = {}
        for (k, v) in events:
            if skip_self and k[0] == eng:
                continue
            if need.get(k, 0) < v:
                need[k] = v
        out = []
        wd = self.waited[eng]
        for k, v in need.items():
            if wd.get(k, 0) >= v:
                continue
            wd[k] = v
            out.append((self.sems[k], v))
        return out

    def op(self, eng, fn, reads=(), writes=(), signal=True):
        self.n_ops += 1
        pr = [r for r in reads if r.startswith("ps")]
        if pr:
            reads = [r for r in reads if not r.startswith("ps")]
            writes = list(writes) + [p for p in pr if p not in writes]
        events = self._deps(reads, writes)
        waits = self._waits_for(eng, events, skip_self=(eng == "pe"))
        k = self.cur_sem[eng]
        if signal:
            self.count[k] += 1
            ev = (k, self.count[k])
            if self.count[k] >= SEM_EPOCH:
                self._new_epoch(eng)
        else:
            ev = (k, self.count[k] + 1)
        sem = self.sems[k]

        def run(e, waits=waits, fn=fn, sem=sem, signal=signal):
            for (s, v) in waits:
                e.wait_ge(s, v)
            ins = fn(e)
            if signal:
                ins.then_inc(sem, 1)
        self.streams[eng].append(run)
        self._record(reads, writes, ev)
        return ev

    def dma(self, out, in_, reads=(), writes=(), queue="sp", **kw):
        self.n_ops += 1
        slot = self.dma_slots[self.dma_rr % N_DMA_SEMS]
        self.dma_rr += 1
        events = self._deps(reads, writes)
        if self.count[slot] > 0:
            events.append((slot, self.count[slot]))
        waits = self._waits_for(queue, events, skip_self=False)
        self.count[slot] += 16
        ev = (slot, self.count[slot])
        sem = self.sems[slot]

        def run(e, waits=waits, sem=sem, out=out, in_=in_, kw=kw):
            for (s, v) in waits:
                e.wait_ge(s, v)
            e.dma_start(out=out, in_=in_, **kw).then_inc(sem, 16)
        self.streams[queue].append(run)
        self._record(reads, writes, ev)
        return ev

    def barrier(self):
        events = [(k, c) for k, c in self.count.items() if c > 0]
        for e in self.ENGS:
            waits = self._waits_for(e, events, skip_self=True)

            def run(eo, waits=waits):
                for (s, v) in waits:
                    eo.wait_ge(s, v)
            self.streams[e].append(run)
        self.res = {}

    def finish(self):
        events = [(k, c) for k, c in self.count.items() if c > 0]
        waits = self._waits_for("sp", events, skip_self=True)

        def run(eo, waits=waits):
            for (s, v) in waits:
                eo.wait_ge(s, v)
        self.streams["sp"].append(run)

    def emit(self):
        nc = self.nc
        streams = self.streams
        self.streams = {e: [] for e in self.ENGS}
        with nc.Block() as block:
            @block.tensor
            def _(e):
                for f in streams["pe"]:
                    f(e)

            @block.scalar
            def _(e):
                for f in streams["act"]:
                    f(e)

            @block.vector
            def _(e):
                for f in streams["dve"]:
                    f(e)

            @block.gpsimd
            def _(e):
                for f in streams["pool"]:
                    f(e)

            @block.sync
            def _(e):
                for f in streams["sp"]:
                    f(e)


class StopBuild(Exception):
    pass


STOP = [None]
PLIM = [9]


def ckpt(name):
    if STOP[0] == name:
        raise StopBuild()


class Rot:
    def __init__(self, items):
        self.items = items
        self.i = 0

    def next(self):
        it = self.items[self.i % len(self.items)]
        self.i += 1
        return it


def build_program(depth=DEPTH, do_b=True, do_c=True):
    nc = bass.Bass("TRN2", target_bir_lowering=False)

    def din(name, shape, dt=F32):
        return nc.dram_tensor(name, shape, dt, kind="ExternalInput").ap()

    def dout(name, shape):
        return nc.dram_tensor(name, shape, F32, kind="ExternalOutput").ap()

    def dscr(name, shape, dt=BF16):
        return nc.dram_tensor(name, shape, dt, kind="Internal").ap()

    x_p = din("x_p", [S, D]); x_s = din("x_s", [SS, D]); c2 = din("c2", [2, D])
    cfk = din("cfk", [DEPTH, PAST, 256]); cfv = din("cfv", [DEPTH, PAST, 256]); cfl = din("cfl", [DEPTH, PAST, 4])
    csk = din("csk", [DEPTH, PAST, 256]); csv = din("csv", [DEPTH, PAST, 256])
    cck = din("cck", [DEPTH, PAST, 128]); ckp = din("ckp", [DEPTH, PAST, 32])
    g_pre = din("g_pre", [DEPTH, D]); g_post = din("g_post", [DEPTH, D])
    w_ada = din("w_ada", [DEPTH, D, 3 * D]); b_ada = din("b_ada", [DEPTH, 3 * D])
    w_in = din("w_in", [DEPTH, D, INW]); b_f = din("b_f", [DEPTH, 4]); g_q_a = din("g_q_a", [DEPTH, 256])
    w_uq = din("w_uq", [DEPTH, 256, 768]); w_uqs = din("w_uqs", [DEPTH, 256, 768])
    g_kv_a = din("g_kv_a", [DEPTH, 128]); w_uk = din("w_uk", [DEPTH, 128, 512]); w_uv = din("w_uv", [DEPTH, 128, 512])
    w_out = din("w_out", [DEPTH, D, D])
    consts = din("consts", [128, 512 + 3 * 896]); selc = din("selc", [2, 256])
    cs_tok = din("cs_tok", [S + 128, 32]); ropeT = din("ropeT", [2, 32, S])

    y_p = dout("y_p", [S, D]); y_s = dout("y_s", [SS, D])
    o_fk = [dout("p_fk", [DEPTH, S, 256]), dout("s_fk", [DEPTH, SS, 256])]
    o_fv = [dout("p_fv", [DEPTH, S, 256]), dout("s_fv", [DEPTH, SS, 256])]
    o_fl = [dout("p_fl", [DEPTH, S, 4]), dout("s_fl", [DEPTH, SS, 4])]
    o_sk = [dout("p_sk", [DEPTH, S, 256]), dout("s_sk", [DEPTH, SS, 256])]
    o_sv = [dout("p_sv", [DEPTH, S, 256]), dout("s_sv", [DEPTH, SS, 256])]
    o_ck = [dout("p_ck", [DEPTH, S, 128]), dout("s_ck", [DEPTH, SS, 128])]
    o_kp = [dout("p_kp", [DEPTH, S, 32]), dout("s_kp", [DEPTH, SS, 32])]

    QF = dscr("QF", [256, TA]); KF = dscr("KF", [256, TA]); GF = dscr("GF", [256, TA]); FQ = dscr("FQ", [4, TA])
    VF = dscr("VF", [TA + 112, 4, 128])
    QB = dscr("QB", [256, TA]); KB = dscr("KB", [256, TA]); GB = dscr("GB", [256, TA]); VB = dscr("VB", [TA + 112, 256])
    QM = dscr("QM", [8, 96, TA]); KM = dscr("KM", [512, TA]); KP = dscr("KP", [32, TA]); GM = dscr("GM", [512, TA])
    VM = dscr("VM", [TA + 112, 8, 128])
    YT = dscr("YT", [D, TA])
    x1 = dscr("x1", [S, D], F32)

    with ExitStack() as gst:
        P = Prog(nc, gst)

        uid = [0]

        def sbt(st, name, shape, dt):
            uid[0] += 1
            return st.enter_context(nc.sbuf_tensor(f"{name}_u{uid[0]}", shape, dt))

        ps = [gst.enter_context(nc.psum_tensor(f"ps{i}", [128, 512], F32)) for i in range(8)]
        psk = [f"ps{i}" for i in range(8)]
        jrot = Rot([0, 1, 2, 3])

        def MM(out, lhsT, rhs, start, stop, r, w, sig=None):
            if sig is None:
                sig = stop
            P.op("pe", lambda e: e.matmul(out, lhsT=lhsT, rhs=rhs, start=start, stop=stop), reads=r, writes=w, signal=sig)

        def TR(out, in_, ident, r, w):
            P.op("pe", lambda e: e.transpose(out, in_, ident), reads=r, writes=w)

        def ACT(out, in_, func, r, w, bias=None, scale=None, accum=None):
            kw = {}
            if bias is not None:
                kw["bias"] = bias
            if scale is not None:
                kw["scale"] = scale
            if accum is not None:
                kw["accum_out"] = accum
            P.op("act", lambda e: e.activation(out=out, in_=in_, func=func, **kw), reads=r, writes=w)

        def TT(eng, out, in0, in1, op, r, w):
            P.op(eng, lambda e: e.tensor_tensor(out=out, in0=in0, in1=in1, op=op), reads=r, writes=w)

        def TS(eng, out, in0, s1, s2, op0, op1, r, w):
            if op1 is None:
                P.op(eng, lambda e: e.tensor_scalar(out=out, in0=in0, scalar1=s1, scalar2=None, op0=op0), reads=r, writes=w)
            else:
                P.op(eng, lambda e: e.tensor_scalar(out=out, in0=in0, scalar1=s1, scalar2=s2, op0=op0, op1=op1), reads=r, writes=w)

        def STT(out, in0, scalar, in1, op0, op1, r, w):
            P.op("dve", lambda e: e.scalar_tensor_tensor(out=out, in0=in0, scalar=scalar, in1=in1, op0=op0, op1=op1), reads=r, writes=w)

        def CP(eng, out, in_, r, w):
            if eng == "act":
                P.op("act", lambda e: e.copy(out=out, in_=in_), reads=r, writes=w)
            else:
                P.op(eng, lambda e: e.tensor_copy(out=out, in_=in_), reads=r, writes=w)

        def MEMSET(eng, ap, val, w):
            P.op(eng, lambda e: e.memset(ap, val), writes=w)

        evrot = Rot(["act", "dve"])

        cst = sbt(gst, "cst", [128, 512], F32)
        cstb = sbt(gst, "cstb", [128, 384], BF16)
        mk = sbt(gst, "mk", [128, 3 * 896], BF16)
        selt = sbt(gst, "selt", [2, 256], F32)
        zt = sbt(gst, "zt", [128, 128], F32)
        zb = sbt(gst, "zb", [128, 128], BF16)
        epst = sbt(gst, "epst", [128, 1], F32)
        P.dma(cst[:], consts[:, 0:512], writes=["cst"])
        P.dma(selt[:], selc, writes=["selt"])
        with ExitStack() as st0:
            mkf = sbt(st0, "mkf", [128, 3 * 896], F32)
            P.dma(mkf[:], consts[:, 512:512 + 3 * 896], writes=["mkf"])
            CP("dve", mk[:], mkf[:], ["mkf"], ["mk"])
            CP("pool", cstb[:, 0:128], cst[:, 0:128], ["cst"], ["cstb"])
            CP("pool", cstb[:, 128:256], cst[:, 384:512], ["cst"], ["cstb"])
            TS("dve", cstb[:, 256:384], cst[:, 256:384], -1.0, None, ALU.mult, None, ["cst"], ["cstb"])
            MEMSET("pool", zt[:], 0.0, ["zt"])
            MEMSET("pool", zb[:], 0.0, ["zb"])
            MEMSET("pool", epst[:], EPS, ["epst"])
            P.barrier()
            P.emit()
        ident = cst[:, 0:128]
        triF = cst[:, 128:256]
        onesF = cst[:, 256:384]
        identb = cstb[:, 0:128]
        tinclneg = cstb[:, 128:256]
        onesneg = cstb[:, 256:384]

        xs = sbt(gst, "xs", [SS, D], F32)
        P.dma(xs[:], x_s, writes=["xs"])
        negF = sbt(gst, "negF", [128, 34, 4], F32)
        negFp = sbt(gst, "negFp", [128, 16, 4], F32)
        prmT = sbt(gst, "prmT", [128, 42], F32)
        modS = sbt(gst, "modS", [128, 2, 16], F32)
        aPS = sbt(gst, "aPS", [128, 2, 8], F32)
        ggP = sbt(gst, "ggP", [128, D], F32)
        ggS = sbt(gst, "ggS", [SS, D], F32)
        bfb = sbt(gst, "bfb", [128, 4, 4], F32)
        gkvb = sbt(gst, "gkvb", [128, 128], F32)
        wuk = sbt(gst, "wuk", [128, 512], BF16)
        wuv = sbt(gst, "wuv", [128, 512], BF16)

        def rstd_from_ssq(out, ssq, n, rows, rk, wk, tmp):
            ACT(tmp, ssq, AF.Ln, rk, [wk + "_t"], bias=epst[0:rows, :], scale=1.0 / n)
            ACT(out, tmp, AF.Exp, [wk + "_t"], [wk], scale=-0.5)

        def alloc_weights(wst):
            return dict(win=sbt(wst, "win", [128, 8, INW], BF16), wuq=sbt(wst, "wuq", [128, 2, 768], BF16),
                        wuqs=sbt(wst, "wuqs", [128, 2, 768], BF16), stg=[sbt(wst, f"stg{i}", [128, 2048], F32) for i in range(2)])

        def load_weights(l, wd, engines):
            win, wuq, wuqs, stg = wd["win"], wd["wuq"], wd["wuqs"], wd["stg"]
            srot = Rot([0, 1])
            castrot = Rot(list(engines))
            wv = w_in[l].rearrange("(k p) c -> p k c", p=128)
            for c0 in range(0, INW, 256):
                cw = min(256, INW - c0)
                si = srot.next()
                sv = stg[si][:, 0:8 * cw].rearrange("p (k c) -> p k c", k=8)
                P.dma(sv, wv[:, :, c0:c0 + cw], writes=[f"stg{si}"], queue="pool")
                CP(castrot.next(), win[:, :, c0:c0 + cw], sv, [f"stg{si}"], ["win"])
                yield
            for (dst, src_, nm) in ((wuq, w_uq, "wuq"), (wuqs, w_uqs, "wuqs")):
                si = srot.next()
                sv = stg[si][:, 0:1536].rearrange("p (k c) -> p k c", k=2)
                P.dma(sv, src_[l].rearrange("(k p) c -> p k c", p=128), writes=[f"stg{si}"], queue="pool")
                CP(castrot.next(), dst[:], sv, [f"stg{si}"], [nm])
                yield
            for (dst, src_, nm) in ((wuk, w_uk, "wuk"), (wuv, w_uv, "wuv")):
                si = srot.next()
                P.dma(stg[si][:, 0:512], src_[l], writes=[f"stg{si}"], queue="pool")
                CP(castrot.next(), dst[:], stg[si][:, 0:512], [f"stg{si}"], [nm])
                yield

        wst_cur = [ExitStack()]
        cur_w = [alloc_weights(wst_cur[0])]
        for _ in load_weights(0, cur_w[0], ("pool", "dve")):
            pass
        wgen = [None]

        for l in range(depth):
          try:
            xin = x_p if l == 0 else x1
            xout = y_p if l == depth - 1 else x1

            ckpt("setup")
            with ExitStack() as st:
                prm = sbt(st, "prm", [42, 128], F32)
                t16 = sbt(st, "t16", [16, 128], F32)
                t16b = sbt(st, "t16b", [16, 128], F32)
                wa = [sbt(st, f"wa{i}", [128, 3 * D], F32) for i in range(2)]
                grow = sbt(st, "grow", [2, D], F32)
                badag = sbt(st, "badag", [2, D], F32)
                gpb = sbt(st, "gpb", [128, D], F32)
                P.dma(prm[0:16, :], c2.rearrange("j (k p) -> (j k) p", p=128), writes=["prm"])
                P.dma(prm[16:24, :], g_pre[l].rearrange("(k p) -> k p", p=128), writes=["prm"])
                P.dma(prm[24:40, :], b_ada[l, 0:2048].rearrange("(k p) -> k p", p=128), writes=["prm"])
                P.dma(prm[40:42, :], g_q_a[l].rearrange("(k p) -> k p", p=128), writes=["prm"])
                P.dma(badag[:], b_ada[l, 2048:3072].partition_broadcast(2), writes=["badag"])
                P.dma(gpb[:], g_post[l].partition_broadcast(128), writes=["gpb"])
                for j4 in range(4):
                    P.dma(bfb[:, j4, :], b_f[l].partition_broadcast(128), writes=["bfb"])
                P.dma(gkvb[:], g_kv_a[l].partition_broadcast(128), writes=["gkvb"])
                ACT(t16[:], prm[0:16, :], AF.Exp, ["prm"], ["t16"], scale=-1.0)
                ACT(t16b[:], t16[:], AF.Ln, ["t16"], ["t16b"], bias=1.0)
                ACT(t16[:], t16b[:], AF.Exp, ["t16b"], ["t16"], scale=-1.0)
                TT("dve", prm[0:16, :], t16[:], prm[0:16, :], ALU.mult, ["t16", "prm"], ["prm"])
                TR(ps[6][:, 0:42], prm[0:42, :], cst[0:42, 0:42], ["prm", "cst"], ["ps6"])
                CP("dve", prmT[:], ps[6][:, 0:42], ["ps6"], ["prmT"])
                MM(ps[7][:, 0:32], zt[:], zt[:, 0:32], True, False, ["zt"], ["ps7"], sig=False)
                for k in range(8):
                    w_t = wa[k % 2]
                    wk = f"wa{k % 2}"
                    P.dma(w_t[:], w_ada[l, k * 128:(k + 1) * 128, :], writes=[wk])
                    sck = prmT[:, k:16:8]
                    for fc in range(16):
                        MM(ps[7][:, 2 * fc:2 * fc + 2], w_t[:, fc * 128:(fc + 1) * 128], sck, False, (k == 7 and fc == 15),
                           [wk, "prmT"], ["ps7"], sig=(fc == 15))
                    MM(ps[4][0:2, :], sck, w_t[:, 2048:2560], k == 0, k == 7, [wk, "prmT"], ["ps4"], sig=True)
                    MM(ps[5][0:2, :], sck, w_t[:, 2560:3072], k == 0, k == 7, [wk, "prmT"], ["ps5"], sig=True)
                for j in range(2):
                    TT("dve", modS[:, j, :], ps[7][:, j:32:2], prmT[:, 24:40], ALU.add, ["ps7", "prmT"], ["modS"])
                    STT(aPS[:, j, :], modS[:, j, 8:16], 1.0, prmT[:, 16:24], ALU.add, ALU.mult, ["modS", "prmT"], ["aPS"])
                TT("dve", grow[:, 0:512], ps[4][0:2, :], badag[:, 0:512], ALU.add, ["ps4", "badag"], ["grow"])
                TT("dve", grow[:, 512:1024], ps[5][0:2, :], badag[:, 512:1024], ALU.add, ["ps5", "badag"], ["grow"])
                for hf in range(2):
                    MM(ps[0 + hf][:, :], selt[0:2, 0:128], grow[0:2, hf * 512:(hf + 1) * 512], True, True, ["selt", "grow"], [psk[hf]])
                    TT("dve", ggP[:, hf * 512:(hf + 1) * 512], ps[hf][:, :], gpb[:, hf * 512:(hf + 1) * 512], ALU.mult,
                       [psk[hf], "gpb"], ["ggP"])
                    MM(ps[2 + hf][0:SS, :], selt[0:2, 128:128 + SS], grow[0:2, hf * 512:(hf + 1) * 512], True, True,
                       ["selt", "grow"], [psk[2 + hf]])
                    TT("dve", ggS[:, hf * 512:(hf + 1) * 512], ps[2 + hf][0:SS, :], gpb[0:SS, hf * 512:(hf + 1) * 512], ALU.mult,
                       [psk[2 + hf], "gpb"], ["ggS"])
                P.barrier()
                P.emit()

            ckpt("adaln")
            with ExitStack() as st:
                win, wuq, wuqs = cur_w[0]["win"], cur_w[0]["wuq"], cur_w[0]["wuqs"]
                cstok = sbt(st, "cstok", [128, 33, 32], F32)
                for q4 in range(4):
                    P.dma(cstok[:, q4 * 8:(q4 + 1) * 8, :],
                          cs_tok[q4 * 1024:(q4 + 1) * 1024, :].rearrange("(t p) c -> p t c", p=128), writes=["cstok"])
                P.dma(cstok[:, 32, :], cs_tok[PAST:PAST + 128, :], writes=["cstok"])

                xts = [sbt(st, f"xt{i}", [128, D], F32) for i in range(2)]
                xns = [sbt(st, f"xn{i}", [128, D], BF16) for i in range(4)]
                sqj = sbt(st, "sqj", [128, D], BF16)
                stat = sbt(st, "stat", [128, 16], F32)
                HTg = [sbt(st, f"HTg{i}", [128, 8, 512], BF16) for i in range(2)]
                stb = [sbt(st, f"stb{i}", [128, 512], BF16) for i in range(6)]
                stbrot = Rot([0, 1, 2, 3, 4, 5])
                kvst = [sbt(st, f"kvst{i}", [128, 512], F32) for i in range(2)]
                kvrot = Rot([0, 1])
                vst = [sbt(st, f"vst{i}", [128, 8, 128], BF16) for i in range(2)]
                vrot = Rot([0, 1])
                for i in range(2):
                    MEMSET("pool", vst[i][:, :, 64:128], 2.0, [f"vst{i}"])
                ckst = [sbt(st, f"ckst{i}", [128, 160], F32) for i in range(4)]
                ckrot = Rot([0, 1])
                rt = [sbt(st, f"rt{i}", [128, 32], F32) for i in range(4)]
                lf = sbt(st, "lf", [128, 8, 4], F32)
                Rrun = sbt(st, "Rrun", [128, 4], F32)
                f8 = sbt(st, "f8", [128, 4, 4], F32)
                fqs = sbt(st, "fqs", [4, 512], BF16)
                ckTg = sbt(st, "ckTg", [128, 512], BF16)
                kpTg = sbt(st, "kpTg", [32, 512], BF16)
                cqT = sbt(st, "cqT", [128, 2, 512], F32)
                sqb = sbt(st, "sqb", [128, 2, 512], BF16)
                cqg = sbt(st, "cqg", [128, 2, 512], BF16)
                rsb = sbt(st, "rsb", [128, 512], F32)
                rsbt = sbt(st, "rsbt", [128, 512], F32)
                ropet = sbt(st, "ropet", [128, 2, 512], F32)
                tq1 = [sbt(st, f"tq1_{i}", [128, 512], F32) for i in range(2)]
                tq2 = [sbt(st, f"tq2_{i}", [128, 512], F32) for i in range(2)]
                tqrot = Rot([0, 1])
                qhrot = Rot([0, 1, 2, 3, 4, 5, 6])
                g1s = [sbt(st, f"g1_{i}", [128, 512], F32) for i in range(2)]
                g2s = [sbt(st, f"g2_{i}", [128, 512], F32) for i in range(2)]
                grot = Rot([0, 1])
                lfp = sbt(st, "lfp", [128, 16, 4], F32)
                Rp = sbt(st, "Rp", [128, 17, 4], F32)
                qst = [sbt(st, f"qst{i}", [96, 512], BF16) for i in range(3)]
                qrot = Rot([0, 1, 2])
                xrot = Rot([0, 1])

                class PG:
                    def __init__(self, gi, W, tiles, sample):
                        self.gi, self.W, self.tiles, self.sample = gi, W, tiles, sample
                        self.si = 1 if sample else 0
                        self.c0 = S if sample else gi * 512
                        self.pos0 = PAST if sample else gi * 512
                        self.HT = HTg[gi % 2]
                        self.hk = f"HTg{gi % 2}"
                        self.aP = aPS[:, self.si, :]
                        self.bP = modS[:, self.si, 0:8]
                        self.xt = []

                def put(dst_ap, bf_ap, key):
                    P.dma(dst_ap, bf_ap, reads=[key], writes=[])

                def P_A1(g):
                    for i, (r0, nr) in enumerate(g.tiles):
                        if g.sample:
                            xt, xk = xs, "xs"
                        else:
                            xi = xrot.next()
                            xt, xk = xts[xi], f"xt{xi}"
                            P.dma(xt[:], xin[r0:r0 + nr, :], reads=[f"x1_{r0 // 128}"] if l > 0 else [], writes=[xk])
                        ACT(sqj[0:nr, :], xt[0:nr, :], AF.Square, [xk], ["sqj"], accum=stat[0:nr, i:i + 1])
                        rstd_from_ssq(stat[0:nr, 8 + i:9 + i], stat[0:nr, i:i + 1], D, nr, ["sqj"], f"stat{i}", stat[0:nr, 4 + i:5 + i])
                        TS("dve", xns[i][0:nr, :], xt[0:nr, :], stat[0:nr, 8 + i:9 + i], None, ALU.mult, None, [xk, f"stat{i}"], [f"xn{i}"])

                def P_A2(g):
                    W, HT, hk = g.W, g.HT, g.hk
                    for k in range(8):
                        b = jrot.next()
                        psb = ps[b][:].bitcast(BF16)
                        for i, (r0, nr) in enumerate(g.tiles):
                            TR(psb[:, i * 128:i * 128 + nr], xns[i][0:nr, k * 128:(k + 1) * 128], cstb[0:nr, 0:nr],
                               [f"xn{i}", "cstb"], [psk[b]])
                        if k % 2 == 0:
                            ACT(HT[:, k, 0:W], psb[:, 0:W], AF.Identity, [psk[b], "aPS", "modS"], [hk],
                                bias=g.bP[:, k:k + 1], scale=g.aP[:, k:k + 1])
                        else:
                            TS("dve", HT[:, k, 0:W], psb[:, 0:W], g.aP[:, k:k + 1], g.bP[:, k:k + 1], ALU.mult, ALU.add,
                               [psk[b], "aPS", "modS"], [hk])

                jrotB = Rot([0, 1, 2, 3, 4, 5, 6])

                def P_B(g):
                    W, HT, hk, c0, si_, sample = g.W, g.HT, g.hk, g.c0, g.si, g.sample
                    jrot = jrotB

                    def projT(col0, M, b):
                        for k in range(8):
                            MM(ps[b][0:M, 0:W], win[:, k, col0:col0 + M], HT[:, k, 0:W], k == 0, k == 7, ["win", hk], [psk[b]])

                    P.dma(ropet[64:96, 0, 0:W], ropeT[0, :, g.pos0:g.pos0 + W], writes=["ropet"])
                    P.dma(ropet[64:96, 1, 0:W], ropeT[1, :, g.pos0:g.pos0 + W], writes=["ropet"])
                    for (col0, dst, nmt) in ((C_FG, GF, 2), (C_SG, GB, 2), (C_MG, GM, 4)):
                        for mt in range(nmt):
                            b = jrot.next()
                            projT(col0 + mt * 128, 128, b)
                            gi_ = grot.next()
                            g1, g2 = g1s[gi_], g2s[gi_]
                            ACT(g1[:, 0:W], ps[b][:, 0:W], AF.Exp, [psk[b]], [f"g1_{gi_}"], scale=-1.0)
                            ACT(g2[:, 0:W], g1[:, 0:W], AF.Ln, [f"g1_{gi_}"], [f"g2_{gi_}"], bias=1.0)
                            ACT(g1[:, 0:W], g2[:, 0:W], AF.Exp, [f"g2_{gi_}"], [f"g1_{gi_}"], scale=-1.0)
                            sb_i = stbrot.next()
                            TT("dve", stb[sb_i][:, 0:W], ps[b][:, 0:W], g1[:, 0:W], ALU.mult, [psk[b], f"g1_{gi_}"], [f"stb{sb_i}"])
                            put(dst[mt * 128:(mt + 1) * 128, c0:c0 + W], stb[sb_i][:, 0:W], f"stb{sb_i}")
                    for (col0, dst, scl) in ((C_FQ, QF, None), (C_FK, KF, None), (C_SQ, QB, 0.125), (C_SK, KB, None)):
                        for mt in range(2):
                            b = jrot.next()
                            projT(col0 + mt * 128, 128, b)
                            sb_i = stbrot.next()
                            eng = evrot.next()
                            if scl is not None:
                                TS("dve", stb[sb_i][:, 0:W], ps[b][:, 0:W], scl, None, ALU.mult, None, [psk[b]], [f"stb{sb_i}"])
                            else:
                                CP(eng, stb[sb_i][:, 0:W], ps[b][:, 0:W], [psk[b]], [f"stb{sb_i}"])
                            put(dst[mt * 128:(mt + 1) * 128, c0:c0 + W], stb[sb_i][:, 0:W], f"stb{sb_i}")
                    for c in range(2):
                        b = jrot.next()
                        projT(C_CQ + c * 128, 128, b)
                        CP("act", cqT[:, c, 0:W], ps[b][:, 0:W], [psk[b]], ["cqT"])
                        ACT(sqb[:, c, 0:W], ps[b][:, 0:W], AF.Square, [psk[b]], ["sqb"])
                    bL = 7
                    nt = len(g.tiles)
                    nr0 = g.tiles[0][1]
                    for i, (r0, nr) in enumerate(g.tiles):
                        for k in range(8):
                            MM(ps[bL][0:nr, i * 4:i * 4 + 4], HT[:, k, i * 128:i * 128 + nr], win[:, k, C_FF:C_FF + 4], k == 0, k == 7,
                               ["win", hk], [psk[bL]])
                    TT("dve", lf[0:nr0, 0:nt, :], ps[bL][0:nr0, 0:nt * 4].rearrange("p (t h) -> p t h", t=nt), bfb[0:nr0, 0:nt, :], ALU.add,
                       [psk[bL], "bfb"], ["lf"])
                    ACT(lf[0:nr0, 4:4 + nt, :], lf[0:nr0, 0:nt, :], AF.Exp, ["lf"], ["lfe"], scale=-1.0)
                    ACT(lf[0:nr0, 0:nt, :], lf[0:nr0, 4:4 + nt, :], AF.Ln, ["lfe"], ["lf"], bias=1.0)
                    TS("dve", lf[0:nr0, 0:nt, :], lf[0:nr0, 0:nt, :], -1.0, None, ALU.mult, None, ["lf"], ["lf"])
                    for i, (r0, nr) in enumerate(g.tiles):
                        P.dma(o_fl[si_][l, r0:r0 + nr, :], lf[0:nr, i, :], reads=["lf"])
                    for i, (r0, nr) in enumerate(g.tiles):
                        orow = slice(r0, r0 + nr)
                        srow = slice(c0 + i * 128, c0 + i * 128 + nr)
                        lhs = lambda k: HT[:, k, i * 128:i * 128 + nr]
                        for (colk, okk, ovv, isfox) in ((C_FK, o_fk, o_fv, True), (C_SK, o_sk, o_sv, False)):
                            b = jrot.next()
                            for k in range(8):
                                MM(ps[b][0:nr, :], lhs(k), win[:, k, colk:colk + 512], k == 0, k == 7, ["win", hk], [psk[b]])
                            ki = kvrot.next()
                            CP("act", kvst[ki][0:nr, :], ps[b][0:nr, :], [psk[b]], [f"kvst{ki}"])
                            P.dma(okk[si_][l, orow, :], kvst[ki][0:nr, 0:256], reads=[f"kvst{ki}"])
                            P.dma(ovv[si_][l, orow, :], kvst[ki][0:nr, 256:512], reads=[f"kvst{ki}"])
                            if isfox:
                                vi = vrot.next()
                                CP("dve", vst[vi][0:nr, 0:4, 0:64], ps[b][0:nr, 256:512].rearrange("p (h d) -> p h d", h=4),
                                   [psk[b]], [f"vst{vi}"])
                                P.dma(VF[srow, :, :], vst[vi][0:nr, 0:4, :], reads=[f"vst{vi}"])
                            else:
                                sb_i = stbrot.next()
                                CP("dve", stb[sb_i][0:nr, 0:256], ps[b][0:nr, 256:512], [psk[b]], [f"stb{sb_i}"])
                                P.dma(VB[srow, :], stb[sb_i][0:nr, 0:256], reads=[f"stb{sb_i}"])
                        b = jrot.next()
                        for k in range(8):
                            MM(ps[b][0:nr, 0:160], lhs(k), win[:, k, C_CKV:C_CKV + 160], k == 0, k == 7, ["win", hk], [psk[b]])
                        r_ = rt[i]
                        rk = f"rt{i}"
                        ACT(sqj[0:nr, 0:128], ps[b][0:nr, 0:128], AF.Square, [psk[b]], ["sqj"], accum=r_[0:nr, 0:1])
                        rstd_from_ssq(r_[0:nr, 2:3], r_[0:nr, 0:1], 128, nr, ["sqj"], rk, r_[0:nr, 1:2])
                        ck = ckst[i]
                        ckk = f"ckst{i}"
                        STT(ck[0:nr, 0:128], ps[b][0:nr, 0:128], r_[0:nr, 2:3], gkvb[0:nr, :], ALU.mult, ALU.mult, [psk[b], rk, "gkvb"], [ckk])
                        cosv = cstok[0:nr, (32 if sample else r0 // 128), 0:16]
                        sinv = cstok[0:nr, (32 if sample else r0 // 128), 16:32]
                        x1p = ps[b][0:nr, 128:144]
                        x2p = ps[b][0:nr, 144:160]
                        TT("dve", ck[0:nr, 128:144], x1p, cosv, ALU.mult, [psk[b], "cstok"], [ckk])
                        TT("dve", r_[0:nr, 16:32], x2p, sinv, ALU.mult, [psk[b], "cstok"], [rk + "b"])
                        TT("dve", ck[0:nr, 128:144], ck[0:nr, 128:144], r_[0:nr, 16:32], ALU.subtract, [ckk, rk + "b"], [ckk])
                        TT("dve", ck[0:nr, 144:160], x2p, cosv, ALU.mult, [psk[b], "cstok"], [ckk])
                        TT("dve", r_[0:nr, 16:32], x1p, sinv, ALU.mult, [psk[b], "cstok", ckk], [rk + "b"])
                        TT("dve", ck[0:nr, 144:160], ck[0:nr, 144:160], r_[0:nr, 16:32], ALU.add, [ckk, rk + "b"], [ckk])
                        P.dma(o_ck[si_][l, orow, :], ck[0:nr, 0:128], reads=[ckk])
                        P.dma(o_kp[si_][l, orow, :], ck[0:nr, 128:160], reads=[ckk])

                def P_C(g):
                    W, c0, sample = g.W, g.c0, g.sample
                    bT1, bT2, bT3, bL = 4, 5, 6, 7
                    if sample:
                        P.dma(lfp[:, 0:8, :], cfl[l, 0:1024, :].rearrange("(t p) h -> p t h", p=128), writes=["lfp"])
                        P.dma(lfp[:, 8:16, :], cfl[l, 1024:2048, :].rearrange("(t p) h -> p t h", p=128), writes=["lfp"])
                        MEMSET("pool", Rp[:, 0, :], 0.0, ["Rp"])
                        for t in range(16):
                            TT("pool", Rp[:, t + 1, :], Rp[:, t, :], lfp[:, t, :], ALU.add, ["Rp", "lfp"], ["Rp"])
                        bF = jrot.next()
                        for t in range(16):
                            MM(ps[bF][:, t * 4:t * 4 + 4], triF, lfp[:, t, :], True, False, ["cst", "lfp"], [psk[bF]], sig=False)
                            MM(ps[bF][:, t * 4:t * 4 + 4], onesF, Rp[:, t, :], False, True, ["cst", "Rp"], [psk[bF]], sig=True)
                        TS("dve", negFp[:].rearrange("p t h -> p (t h)"), ps[bF][:, 0:64], -1.0, None, ALU.mult, None, [psk[bF]], ["negFp"])
                        CP("dve", Rrun[:], Rp[:, 16, :], ["Rp"], ["Rrun"])
                    elif g.gi == 0:
                        MEMSET("pool", Rrun[:], 0.0, ["Rrun"])
                    nt = len(g.tiles)
                    nr0 = g.tiles[0][1]
                    tg0 = 33 if sample else (g.tiles[0][0] // 128)
                    for i, (r0, nr) in enumerate(g.tiles):
                        ck = ckst[i]
                        ckk = f"ckst{i}"
                        TR(ps[bT1][:, i * 128:i * 128 + nr], ck[0:nr, 0:128], cst[0:nr, 0:nr], [ckk, "cst"], [psk[bT1]])
                        TR(ps[bT2][0:32, i * 128:i * 128 + nr], ck[0:nr, 128:160], cst[0:nr, 0:nr], [ckk, "cst"], [psk[bT2]])
                    for i, (r0, nr) in enumerate(g.tiles):
                        fo = ps[bL][0:nr, 32 + i * 4:36 + i * 4]
                        MM(fo, cst[0:nr, 128:128 + nr], lf[0:nr, i, :], True, False, ["cst", "lf"], [psk[bL]], sig=False)
                        MM(fo, cst[:, 256:256 + nr], Rrun[:], False, i == 0, ["cst", "Rrun"], [psk[bL]], sig=(i == 0))
                        for j in range(i):
                            MM(fo, cst[:, 256:256 + nr], lf[:, j, :], False, j == i - 1, ["cst", "lf"], [psk[bL]], sig=(j == i - 1))
                    TS("dve", negF[0:nr0, tg0:tg0 + nt, :], ps[bL][0:nr0, 32:32 + 4 * nt].rearrange("p (t h) -> p t h", t=nt), -1.0, None,
                       ALU.mult, None, [psk[bL]], ["negF"])
                    TS("dve", f8[0:nr0, 0:nt, :], ps[bL][0:nr0, 32:32 + 4 * nt].rearrange("p (t h) -> p t h", t=nt), 8.0, None,
                       ALU.mult, None, [psk[bL]], ["f8"])
                    if not sample:
                        for i in range(nt):
                            TT("dve", Rrun[:], Rrun[:], lf[:, i, :], ALU.add, ["Rrun", "lf"], ["Rrun"])
                    for i, (r0, nr) in enumerate(g.tiles):
                        TR(ps[bT3][0:4, i * 128:i * 128 + nr], f8[0:nr, i, :], cst[0:nr, 0:nr], ["f8", "cst"], [psk[bT3]])
                    CP("act", ckTg[:, 0:W], ps[bT1][:, 0:W], [psk[bT1]], ["ckTg"])
                    CP("dve", kpTg[:, 0:W], ps[bT2][0:32, 0:W], [psk[bT2]], ["kpTg"])
                    put(KP[:, c0:c0 + W], kpTg[:, 0:W], "kpTg")
                    CP("dve", fqs[:, 0:W], ps[bT3][0:4, 0:W], [psk[bT3]], ["fqs"])
                    put(FQ[:, c0:c0 + W], fqs[:, 0:W], "fqs")
                    b = jrot.next()
                    for c in range(2):
                        MM(ps[b][:, 0:W], cstb[:, 256:384], sqb[:, c, 0:W], c == 0, c == 1, ["cstb", "sqb"], [psk[b]])
                    ACT(rsbt[:, 0:W], ps[b][:, 0:W], AF.Ln, [psk[b]], ["rsbt"], bias=epst[:, :], scale=-1.0 / 256)
                    ACT(rsb[:, 0:W], rsbt[:, 0:W], AF.Exp, ["rsbt"], ["rsb"], scale=-0.5)
                    for c in range(2):
                        STT(cqg[:, c, 0:W], cqT[:, c, 0:W], prmT[:, 40 + c:41 + c], rsb[:, 0:W], ALU.mult, ALU.mult, ["cqT", "prmT", "rsb"], ["cqg"])
                    for hp in range(4):
                        b = jrot.next()
                        MM(ps[b][:, 0:W], wuk[:, hp * 128:(hp + 1) * 128], ckTg[:, 0:W], True, True, ["wuk", "ckTg"], [psk[b]])
                        sb_i = stbrot.next()
                        CP(evrot.next(), stb[sb_i][:, 0:W], ps[b][:, 0:W], [psk[b]], [f"stb{sb_i}"])
                        put(KM[hp * 128:(hp + 1) * 128, c0:c0 + W], stb[sb_i][:, 0:W], f"stb{sb_i}")
                    for i, (r0, nr) in enumerate(g.tiles):
                        b = jrot.next()
                        MM(ps[b][0:nr, :], ckTg[:, i * 128:i * 128 + nr], wuv[:], True, True, ["wuv", "ckTg"], [psk[b]])
                        vi = vrot.next()
                        CP(evrot.next(), vst[vi][0:nr, :, 0:64], ps[b][0:nr, :].rearrange("p (h d) -> p h d", h=8), [psk[b]], [f"vst{vi}"])
                        P.dma(VM[c0 + i * 128:c0 + i * 128 + nr, :, :], vst[vi][0:nr, :, :], reads=[f"vst{vi}"])
                    for h in range(8):
                        bA = qhrot.next()
                        bB = qhrot.next()
                        for c in range(2):
                            MM(ps[bA][0:96, 0:W], wuq[:, c, h * 96:(h + 1) * 96], cqg[:, c, 0:W], c == 0, c == 1, ["wuq", "cqg"], [psk[bA]])
                        for c in range(2):
                            MM(ps[bB][0:96, 0:W], wuqs[:, c, h * 96:(h + 1) * 96], cqg[:, c, 0:W], c == 0, c == 1, ["wuqs", "cqg"], [psk[bB]])
                        qi = qrot.next()
                        q_ = qst[qi]
                        qk = f"qst{qi}"
                        ti = tqrot.next()
                        CP("act", q_[0:64, 0:W], ps[bA][0:64, 0:W], [psk[bA]], [qk])
                        TT("dve", tq1[ti][64:96, 0:W], ps[bA][64:96, 0:W], ropet[64:96, 0, 0:W], ALU.mult, [psk[bA], "ropet"], [f"tq1_{ti}"])
                        TT("dve", tq2[ti][64:96, 0:W], ps[bB][64:96, 0:W], ropet[64:96, 1, 0:W], ALU.mult, [psk[bB], "ropet"], [f"tq2_{ti}"])
                        TT("pool", q_[64:96, 0:W], tq1[ti][64:96, 0:W], tq2[ti][64:96, 0:W], ALU.add, [f"tq1_{ti}", f"tq2_{ti}"], [qk + "b"])
                        P.dma(QM[h, :, c0:c0 + W], q_[0:96, 0:W], reads=[qk, qk + "b"], writes=[])

                groups = [PG(gi, 512, [(gi * 512 + i * 128, 128) for i in range(4)], False) for gi in range(8)]
                groups.append(PG(8, SS, [(0, SS)], True))
                P_A1(groups[0])
                P_A2(groups[0])
                for gi, g in enumerate(groups):
                    nx = groups[gi + 1] if gi + 1 < len(groups) else None
                    if nx is not None:
                        P_A1(nx)
                    P_B(g)
                    if nx is not None:
                        P_A2(nx)
                    P_C(g)
                P.barrier()
                P.emit()

            wst_cur[0].close()
            if do_b:
                with ExitStack() as st:
                    KTb = [sbt(st, f"KT{i}", [128, S], BF16) for i in range(2)]
                    KTs = [sbt(st, f"KTs{i}", [128, PAST + SS], BF16) for i in range(4)]
                    Vbig = [sbt(st, f"Vbig{i}", [128, 32, 4, 128], BF16) for i in range(2)]
                    Vs = [sbt(st, f"Vs{i}", [128, 17, 128], BF16) for i in range(4)]
                    Qt = [sbt(st, f"Qt{i}", [128, 512], BF16) for i in range(5)]
                    SGt = [sbt(st, f"SGt{i}", [64, 512], BF16) for i in range(5)]
                    Qs = sbt(st, "Qs", [128, 4, SS], BF16)
                    SGs = sbt(st, "SGs", [64, 4, SS], BF16)
                    Pb = [sbt(st, f"Pb{i}", [128, 512], BF16) for i in range(6)]
                    Eb = [sbt(st, f"Eb{i}", [128, 512], F32) for i in range(2)]
                    SPb = [sbt(st, f"SPb{i}", [128, 512], BF16) for i in range(4)]
                    SPs = [sbt(st, f"SPs{i}", [128, 512], BF16) for i in range(5)]
                    Pbs = sbt(st, "Pbs", [128, 4, 7, SS], BF16)
                    Ebs = sbt(st, "Ebs", [128, 4, 2, SS], F32)
                    SPbs = sbt(st, "SPbs", [128, 4, 4, SS], BF16)
                    SPss = sbt(st, "SPss", [128, 4, 5, SS], BF16)
                    rl = sbt(st, "rl", [64, 512], F32)
                    tmpb = sbt(st, "tmpb", [64, 512], F32)
                    yb = [sbt(st, f"yb{i}", [64, 512], BF16) for i in range(2)]
                    cstg = [sbt(st, f"cstg{i}", [128, 16, 128], F32) for i in range(1)]
                    kst = sbt(st, "kst", [128, 4, 16, 64], BF16)
                    ckTs = sbt(st, "ckTs", [128, PAST], BF16)
                    qrot2 = Rot([0, 1, 2, 3, 4])
                    yrot = Rot([0, 1])
                    orot = Rot([4, 5, 6])
                    jr2 = Rot([7, 0, 1, 2, 3])
                    ktrot = Rot([0, 1])
                    cgrot = Rot([0, 1])
                    for i in range(4):
                        MEMSET("pool", Vs[i][:, :, 64:128], 2.0, [f"Vs{i}"])

                    class Bufs:
                        def __init__(self, sample_slot=None):
                            self.ss = sample_slot
                            self.c = {}

                        def _n(self, name, n):
                            i = self.c.get(name, 0)
                            self.c[name] = i + 1
                            return i % n

                        def S(self):
                            i = self._n("S", 4) if self.ss is None else self.ss
                            return ps[i], psk[i]

                        def Z(self):
                            i = self._n("Z", 2) if self.ss is None else (self.ss % 2)
                            return ps[i], psk[i]

                        def W(self):
                            i = 2 + (self._n("W", 2) if self.ss is None else (self.ss % 2))
                            return ps[i], psk[i]

                        def P(self):
                            if self.ss is None:
                                i = self._n("P", 6)
                                return Pb[i], f"Pb{i}"
                            i = self._n("P", 7)
                            return Pbs[:, self.ss, i, :], f"Pbs{self.ss}_{i}"

                        def E(self):
                            if self.ss is None:
                                i = self._n("E", 2)
                                return Eb[i], f"Eb{i}"
                            i = self._n("E", 2)
                            return Ebs[:, self.ss, i, :], f"Ebs{self.ss}_{i}"

                        def SP(self):
                            if self.ss is None:
                                i = self._n("SP", 4)
                                return SPb[i], f"SPb{i}"
                            i = self._n("SP", 4)
                            return SPbs[:, self.ss, i, :], f"SPbs{self.ss}_{i}"

                        def SS_(self):
                            if self.ss is None:
                                i = self._n("SS", 5)
                                return SPs[i], f"SPs{i}"
                            i = self._n("SS", 5)
                            return SPss[:, self.ss, i, :], f"SPss{self.ss}_{i}"

                    def run_softmax(problems, LA=4):
                        N = max(len(p[0]) for p in problems)
                        for s in range(N + LA):
                            for (items, bf) in problems:
                                if s < len(items):
                                    it = items[s]
                                    if it.get("pre"):
                                        it["pre"]()
                                    kw, qw, cl, mw = it["kw"], it["qw"], it.get("cl", 0), it.get("mw", 128)
                                    sb_, sk_ = bf.S()
                                    last = it["mask"] is None
                                    MM(sb_[0:kw, cl:qw], it["kt"], it["q"][:, cl:qw], True, last, it["rk"] + it["qk"], [sk_], sig=last)
                                    if not last:
                                        MM(sb_[0:kw, cl:cl + mw], identb[0:kw, 0:kw], it["mask"], False, True, ["cstb", "mk"], [sk_], sig=True)
                                    pt, pk = bf.P()
                                    ACT(pt[0:kw, cl:qw], sb_[0:kw, cl:qw], AF.Exp, [sk_] + it["bk"], [pk], bias=it["bias"], scale=it["scale"])
                                    it["p"] = (pt, pk)
                            for (items, bf) in problems:
                                j = s - LA
                                if 0 <= j < len(items):
                                    it = items[j]
                                    kw, qw, cl = it["kw"], it["qw"], it.get("cl", 0)
                                    pt, pk = it["p"]
                                    ob = it["ob"]
                                    oc = it.get("oc", 0)
                                    MM(ps[ob][0:it["M"], oc + cl:oc + qw], it["v"], pt[0:kw, cl:qw], it["first"] and not it.get("nostart"), it["last"],
                                       it["rk"] + [pk], [psk[ob]], sig=it["last"])
                                    if it["last"]:
                                        it["fin"]()

                    def run_sb(problems):
                        N = max(len(p[0]) for p in problems)
                        for s in range(N + 5):
                            for (items, bf) in problems:
                                if s < len(items):
                                    it = items[s]
                                    if it.get("pre"):
                                        it["pre"]()
                                    kw, qw, cl, mw = it["kw"], it["qw"], it.get("cl", 0), it.get("mw", 128)
                                    zb, zk = bf.Z()
                                    last = it["mask"] is None
                                    MM(zb[0:kw, cl:qw], it["kt"], it["q"][:, cl:qw], True, last, it["rk"] + it["qk"], [zk], sig=last)
                                    if not last:
                                        MM(zb[0:kw, cl:cl + mw], identb[0:kw, 0:kw], it["mask"], False, True, ["cstb", "mk"], [zk], sig=True)
                                    et, ek = bf.E()
                                    ACT(et[0:kw, cl:qw], zb[0:kw, cl:qw], AF.Exp, [zk], [ek])
                                    it["e"] = (et, ek)
                            for (items, bf) in problems:
                                s_ = s
                                s = s_ - 1
                                if 0 <= s < len(items):
                                    it = items[s]
                                    kw, qw, cl, mw = it["kw"], it["qw"], it.get("cl", 0), it.get("mw", 128)
                                    et, ek = it["e"]
                                    spt, spk = bf.SP()
                                    ACT(spt[0:kw, cl:qw], et[0:kw, cl:qw], AF.Ln, [ek], [spk], bias=1.0)
                                    sst, ssk = bf.SS_()
                                    if it["first"]:
                                        if kw < 128:
                                            MEMSET("pool", sst[:, 0:qw], 0.0, [ssk])
                                        CP("pool", sst[0:kw, cl:qw], spt[0:kw, cl:qw], [spk], [ssk])
                                    else:
                                        pst, psk_ = items[s - 1]["ss"]
                                        pcl = items[s - 1].get("cl", 0)
                                        TT("dve", sst[:, pcl:qw], pst[:, pcl:qw], spt[:, pcl:qw], ALU.add, [psk_, psk_ + "x", spk], [ssk])
                                        if pcl > cl:
                                            CP("pool", sst[:, cl:pcl], spt[:, cl:pcl], [spk], [ssk + "x"])
                                    it["sp"] = (spt, spk)
                                    it["ss"] = (sst, ssk)
                                s = s_
                            for (items, bf) in problems:
                                j = s - 3
                                if 0 <= j < len(items):
                                    it = items[j]
                                    kw, qw, cl, mw = it["kw"], it["qw"], it.get("cl", 0), it.get("mw", 128)
                                    wb, wk_ = bf.W()
                                    MM(wb[0:kw, cl:qw], it["kt"], it["q"][:, cl:qw], True, False, it["rk"] + it["qk"], [wk_], sig=False)
                                    if it["mask"] is not None:
                                        MM(wb[0:kw, cl:cl + mw], identb[0:kw, 0:kw], it["mask"], False, False, ["cstb", "mk"], [wk_], sig=False)
                                    spt, spk = it["sp"]
                                    MM(wb[0:kw, cl:qw], tinclneg[0:kw, 0:kw], spt[0:kw, cl:qw], False, it["first"], ["cstb", spk], [wk_], sig=it["first"])
                                    if not it["first"]:
                                        pst, psk_ = items[j - 1]["ss"]
                                        pcl = items[j - 1].get("cl", 0)
                                        MM(wb[0:kw, pcl:qw], onesneg[:, 0:kw], pst[:, pcl:qw], False, True, ["cstb", psk_, psk_ + "x"], [wk_], sig=True)
                                    pt, pk = bf.P()
                                    ACT(pt[0:kw, cl:qw], wb[0:kw, cl:qw], AF.Exp, [wk_], [pk])
                                    it["p"] = (pt, pk)
                            for (items, bf) in problems:
                                j = s - 5
                                if 0 <= j < len(items):
                                    it = items[j]
                                    kw, qw, cl = it["kw"], it["qw"], it.get("cl", 0)
                                    pt, pk = it["p"]
                                    ob = it["ob"]
                                    oc = it.get("oc", 0)
                                    MM(ps[ob][0:it["M"], oc + cl:oc + qw], it["v"], pt[0:kw, cl:qw], it["first"] and not it.get("nostart"), it["last"],
                                       it["rk"] + [pk], [psk[ob]], sig=it["last"])
                                    if it["last"]:
                                        it["fin"]()

                    def finalize(kind, ob, sg_ap, sgk, qw, yrow, c0, ro=0, oc=0):
                        yi = yrot.next()
                        if kind == "sb":
                            STT(yb[yi][:, 0:qw], ps[ob][ro:ro + 64, oc:oc + qw], 1.0, sg_ap, ALU.mult, ALU.mult, [psk[ob], sgk], [f"yb{yi}"])
                        else:
                            P.op("dve", lambda e: e.reciprocal(out=rl[:, 0:qw], in_=ps[ob][64:128, oc:oc + qw]), reads=[psk[ob]], writes=["rl"])
                            TT("pool", tmpb[:, 0:qw], sg_ap, rl[:, 0:qw], ALU.mult, [sgk, "rl"], ["tmpb"])
                            STT(yb[yi][:, 0:qw], ps[ob][0:64, oc:oc + qw], 2.0, tmpb[:, 0:qw], ALU.mult, ALU.mult, [psk[ob], "tmpb"], [f"yb{yi}"])
                        P.dma(YT[yrow:yrow + 64, c0:c0 + qw], yb[yi][:, 0:qw], reads=[f"yb{yi}"])

                    def run_group(kind, hook):
                        nheads = 8 if kind == "mla" else 4
                        Qsrc = {"fox": QF, "sb": QB}.get(kind)
                        Ksrc = {"fox": KF, "sb": KB, "mla": KM}[kind]
                        Gsrc = {"fox": GF, "sb": GB, "mla": GM}[kind]
                        ybase = {"fox": 0, "sb": 256, "mla": 512}[kind]
                        Kd = {"fox": 65, "sb": 128, "mla": 96}[kind]
                        M = 128
                        mkind = {"fox": 0, "sb": 1, "mla": 2}[kind]
                        scale = {"fox": 0.125, "sb": 1.0, "mla": MLA_SCALE}[kind]
                        runner = run_sb if kind == "sb" else run_softmax

                        def sample_loads(hs, sl0):
                            ctxs = []
                            for sl, h in enumerate(hs, start=sl0):
                                Ks, ksk = KTs[sl], f"KTs{sl}"
                                Vh, vsk = Vs[sl], f"Vs{sl}"
                                if kind in ("fox", "sb"):
                                    ksrc_c = cfk if kind == "fox" else csk
                                    vsrc_c = cfv if kind == "fox" else csv
                                    P.dma(kst[:, sl, :, :], ksrc_c[l, :, h * 64:(h + 1) * 64].rearrange("(t p) c -> p t c", p=128),
                                          writes=[f"kst{sl}"], queue="pool")
                                    P.dma(Vh[:, 0:16, 0:64], vsrc_c[l, :, h * 64:(h + 1) * 64].rearrange("(t p) c -> p t c", p=128),
                                          writes=[vsk], queue="pool")
                                    if kind == "fox":
                                        MEMSET("pool", Ks[64:65, :], 1.0, [ksk + "a"])
                                        P.dma(Vh[0:SS, 16, :], VF[S:S + SS, h, :], writes=[vsk + "n"])
                                        P.dma(Qs[64:65, sl, :], FQ[h:h + 1, S:S + SS], writes=[f"Qs{sl}a"])
                                    else:
                                        MEMSET("pool", Ks[64:128, :], 0.0, [ksk + "a"])
                                        MEMSET("pool", Qs[64:128, sl, :], 0.0, [f"Qs{sl}a"])
                                        P.dma(Vh[0:SS, 16, 0:64], VB[S:S + SS, h * 64:(h + 1) * 64], writes=[vsk + "n"])
                                    P.dma(Qs[0:64, sl, :], Qsrc[h * 64:(h + 1) * 64, S:S + SS], writes=[f"Qs{sl}"])
                                else:
                                    P.dma(Vh[0:SS, 16, :], VM[S:S + SS, h, :], writes=[vsk + "n"])
                                    P.dma(Qs[0:96, sl, :], QM[h, :, S:S + SS], writes=[f"Qs{sl}"])
                                P.dma(Ks[0:64, PAST:PAST + SS], Ksrc[h * 64:(h + 1) * 64, S:S + SS], writes=[ksk + "n"])
                                P.dma(SGs[:, sl, :], Gsrc[h * 64:(h + 1) * 64, S:S + SS], writes=[f"SGs{sl}"])
                                ctxs.append(dict(h=h, sl=sl, Ks=Ks, ksk=ksk, Vh=Vh, vsk=vsk))
                            return ctxs

                        def sample_prep(ctxs):
                            for c_ in ctxs:
                                h, sl, Ks, ksk, Vh, vsk = c_["h"], c_["sl"], c_["Ks"], c_["ksk"], c_["Vh"], c_["vsk"]
                                if kind in ("fox", "sb"):
                                    for g2 in range(2):
                                        b = jr2.next()
                                        psb = ps[b][:].bitcast(BF16)
                                        for j in range(8):
                                            TR(psb[0:64, j * 128:(j + 1) * 128], kst[:, sl, g2 * 8 + j, :], identb, [f"kst{sl}", "cstb"], [psk[b]])
                                        CP(evrot.next(), Ks[0:64, g2 * 1024:(g2 + 1) * 1024], psb[0:64, 0:1024], [psk[b]], [ksk])
                                else:
                                    for g4 in range(4):
                                        b = jr2.next()
                                        MM(ps[b][0:64, :], wuk[:, h * 64:(h + 1) * 64], ckTs[:, g4 * 512:(g4 + 1) * 512], True, True, ["wuk", "ckTs"], [psk[b]])
                                        CP(evrot.next(), Ks[0:64, g4 * 512:(g4 + 1) * 512], ps[b][0:64, :], [psk[b]], [ksk])
                                    for g4 in range(4):
                                        b = jr2.next()
                                        for i in range(4):
                                            t = g4 * 4 + i
                                            MM(ps[b][:, i * 64:(i + 1) * 64], ckTs[:, t * 128:(t + 1) * 128], wuv[:, h * 64:(h + 1) * 64], True, True,
                                               ["wuv", "ckTs"], [psk[b]])
                                        CP(evrot.next(), Vh[:, g4 * 4:(g4 + 1) * 4, 0:64], ps[b][:, 0:256].rearrange("p (t d) -> p t d", t=4), [psk[b]], [vsk])

                        def mla_sample_once():
                            P.dma(kst[:, 0, :, 0:32], ckp[l].rearrange("(t p) c -> p t c", p=128), writes=["kst0"], queue="pool")
                            for g2 in range(2):
                                b = jr2.next()
                                psb = ps[b][:].bitcast(BF16)
                                for j in range(8):
                                    TR(psb[0:32, j * 128:(j + 1) * 128], kst[:, 0, g2 * 8 + j, 0:32], identb, ["kst0", "cstb"], [psk[b]])
                                for sl in range(4):
                                    CP(evrot.next(), KTs[sl][64:96, g2 * 1024:(g2 + 1) * 1024], psb[0:32, 0:1024], [psk[b]], [f"KTs{sl}a"])
                            for sl in range(4):
                                P.dma(KTs[sl][64:96, PAST:PAST + SS], KP[:, S:S + SS], writes=[f"KTs{sl}b"])

                        def sample_attn(ctxs):
                            problems = []
                            for c_ in ctxs:
                                h, sl, Ks, ksk, Vh, vsk = c_["h"], c_["sl"], c_["Ks"], c_["ksk"], c_["Vh"], c_["vsk"]
                                rks = [ksk, ksk + "a", ksk + "b", ksk + "n", vsk, vsk + "n"]
                                ob = sl
                                qap = Qs[0:Kd, sl, :]
                                qk = [f"Qs{sl}", f"Qs{sl}a"]
                                items = []
                                for kb in range(16):
                                    items.append(dict(kt=Ks[0:Kd, kb * 128:(kb + 1) * 128], v=Vh[:, kb, 0:M], kw=128, qw=SS, mask=None,
                                                      bias=(negFp[:, kb, h:h + 1] if kind == "fox" else None),
                                                      bk=(["negFp"] if kind == "fox" else []), rk=rks))
                                smask = None
                                if kind != "mla":
                                    ms = mkind * 896 + 384
                                    smask = mk[0:SS, ms:ms + SS]
                                items.append(dict(kt=Ks[0:Kd, PAST:PAST + SS], v=Vh[0:SS, 16, 0:M], kw=SS, qw=SS, mask=smask, cl=0, mw=SS,
                                                  bias=(negF[0:SS, 33, h:h + 1] if kind == "fox" else None),
                                                  bk=(["negF"] if kind == "fox" else []), rk=rks))
                                if kind == "sb":
                                    items = items[::-1]
                                obank = 4 + (sl % 2) if False else None
                                for ii, it in enumerate(items):
                                    it.update(q=qap, qk=qk, scale=scale, M=M, first=(ii == 0), last=(ii == len(items) - 1), ob=None, fin=None)
                                problems.append((items, Bufs(sample_slot=sl), c_))
                            MM(ps[7][:, 0:4 * SS], zb[:], zb[:, 0:4 * SS], True, True, ["zb"], ["ps7"])
                            pr = []
                            for (items, bf, c_) in problems:
                                h, sl = c_["h"], c_["sl"]
                                for it in items:
                                    it.update(ob=7, oc=sl * SS, nostart=True)
                                items[-1]["fin"] = (lambda sl=sl, h=h: finalize(kind, 7, SGs[:, sl, :], f"SGs{sl}", SS, ybase + h * 64, S, oc=sl * SS))
                                pr.append((items, bf))
                            return pr

                        def vidx(h):
                            return ((0 if kind != "sb" else 1) if h < 4 else 1)

                        def prep_head(h):
                            ctx = dict(h=h)
                            if h % 4 == 0:
                                vi = vidx(h)
                                vk = f"Vbig{vi}"
                                if kind == "fox":
                                    for q4 in range(4):
                                        P.dma(Vbig[vi][:, q4 * 8:(q4 + 1) * 8, :, :],
                                              VF[q4 * 1024:(q4 + 1) * 1024, :, :].rearrange("(t p) h c -> p t h c", p=128), writes=[vk])
                                elif kind == "sb":
                                    vv = Vbig[vi][:].rearrange("p t h c -> p (t h c)")[:, 0:32 * 256].rearrange("p (t c) -> p t c", t=32)
                                    for q4 in range(4):
                                        P.dma(vv[:, q4 * 8:(q4 + 1) * 8, :],
                                              VB[q4 * 1024:(q4 + 1) * 1024, :].rearrange("(t p) c -> p t c", p=128), writes=[vk])
                                else:
                                    for q4 in range(4):
                                        P.dma(Vbig[vi][:, q4 * 8:(q4 + 1) * 8, :, :],
                                              VM[q4 * 1024:(q4 + 1) * 1024, h:h + 4, :].rearrange("(t p) h c -> p t h c", p=128), writes=[vk])
                            ki = h % 2
                            KT = KTb[ki]
                            kk = f"KT{ki}"
                            P.dma(KT[0:64, :], Ksrc[h * 64:(h + 1) * 64, 0:S], writes=[kk])
                            if kind == "fox":
                                MEMSET("pool", KT[64:65, :], 1.0, [kk + "a"])
                            elif kind == "mla":
                                P.dma(KT[64:96, :], KP[:, 0:S], writes=[kk + "a"])
                            else:
                                MEMSET("pool", KT[64:128, :], 0.0, [kk + "a"])
                            ctx.update(KT=KT, kk=kk)
                            return ctx

                        def load_q(h, qt):
                            qi = qrot2.next()
                            qk = f"Qt{qi}"
                            c0 = qt * 512
                            if kind == "mla":
                                P.dma(Qt[qi][0:96, :], QM[h, :, c0:c0 + 512], writes=[qk])
                            else:
                                P.dma(Qt[qi][0:64, :], Qsrc[h * 64:(h + 1) * 64, c0:c0 + 512], writes=[qk])
                                if kind == "fox":
                                    P.dma(Qt[qi][64:65, :], FQ[h:h + 1, c0:c0 + 512], writes=[qk + "a"])
                                else:
                                    MEMSET("pool", Qt[qi][64:128, :], 0.0, [qk + "a"])
                            P.dma(SGt[qi][:, :], Gsrc[h * 64:(h + 1) * 64, c0:c0 + 512], writes=[f"SGt{qi}"])
                            return qi

                        if kind == "mla":
                            P.dma(cstg[0][:], cck[l].rearrange("(t p) c -> p t c", p=128), writes=["cstg0", "cstg0v"])
                            for g4 in range(4):
                                b = jr2.next()
                                for i in range(4):
                                    TR(ps[b][:, i * 128:(i + 1) * 128], cstg[0][:, g4 * 4 + i, :], ident, ["cstg0", "cst"], [psk[b]])
                                CP(evrot.next(), ckTs[:, g4 * 512:(g4 + 1) * 512], ps[b][:, :], [psk[b]], ["ckTs"])

                        if kind == "mla":
                            mla_sample_once()
                        c03 = sample_loads([0, 1, 2, 3], 0)
                        sample_prep(c03)
                        yield
                        spr0 = sample_attn(c03)
                        pbufs = Bufs()
                        late = []
                        prep_head(0)
                        all_items = []
                        qpre = {}
                        qseq = [(h_, q_) for h_ in range(nheads) for q_ in range(8)]
                        qpre[(0, 0)] = load_q(0, 0)
                        qpre[(0, 1)] = load_q(0, 1)
                        for h in range(nheads):
                            vc = vidx(h)
                            vk = f"Vbig{vc}"
                            KT, kk = KTb[h % 2], f"KT{h % 2}"
                            if kind == "sb":
                                vv = Vbig[vc][:].rearrange("p t h c -> p (t h c)")[:, 0:32 * 256].rearrange("p (t c) -> p t c", t=32)
                                Vof = lambda t, h=h, vv=vv: vv[:, t, (h // 2) * 128:(h // 2) * 128 + 128]
                            else:
                                Vof = lambda t, h=h, vc=vc: Vbig[vc][:, t, h % 4, :]
                            items = []
                            for qt in range(8):
                                c0 = qt * 512
                                nkb = 4 * qt + 4
                                ob = orot.next()
                                blk = []
                                for kb in range(nkb):
                                    r = kb - 4 * qt
                                    mask = None
                                    cl = 0
                                    if r >= 0:
                                        ms = mkind * 896 + 384
                                        mask = mk[:, ms:ms + 128]
                                        cl = 128 * r
                                    blk.append(dict(kt=KT[0:Kd, kb * 128:(kb + 1) * 128], v=Vof(kb), kw=128, qw=512, mask=mask, cl=cl, mw=128,
                                                    bias=(negF[:, kb, h:h + 1] if kind == "fox" else None),
                                                    bk=(["negF"] if kind == "fox" else []), rk=[kk, kk + "a", vk],
                                                    scale=scale, M=M, ob=ob, first=False, last=False, fin=None, qt=qt))
                                if kind == "sb":
                                    blk = blk[::-1]
                                blk[0]["first"] = True
                                blk[-1]["last"] = True
                                items.extend(blk)
                            def mk_pre(h, qt, blk_items):
                                def pre():
                                    if qt == 1 and h + 1 < nheads:
                                        prep_head(h + 1)
                                    if h == 2 and qt == 2 and hook is not None:
                                        hook()
                                    if kind == "mla" and h == 5 and qt == 2:
                                        c47 = sample_loads([4, 5, 6, 7], 0)
                                        sample_prep(c47)
                                        late.append(c47)
                                    qi = qpre.pop((h, qt))
                                    for it in blk_items:
                                        it["q"] = Qt[qi][0:Kd, :]
                                        it["qk"] = [f"Qt{qi}", f"Qt{qi}a"]
                                    blk_items[-1]["fin"] = (lambda qi=qi, ob=blk_items[0]["ob"], qt=qt, h=h:
                                                            finalize(kind, ob, SGt[qi][:, :], f"SGt{qi}", 512, ybase + h * 64, qt * 512,
                                                                     ro=((h % 2) * 64 if kind == "sb" else 0)))
                                    idx = h * 8 + qt + 2
                                    if idx < len(qseq):
                                        qpre[qseq[idx]] = load_q(*qseq[idx])
                                return pre
                            pos = 0
                            for qt in range(8):
                                nkb = 4 * qt + 4
                                seg = items[pos:pos + nkb]
                                seg[0]["pre"] = mk_pre(h, qt, seg)
                                pos += nkb
                            all_items.extend(items)
                        runner([(all_items, pbufs)] + spr0)

                        for c47 in late:
                            runner(sample_attn(c47))

                    g_mla = run_group("mla", None)
                    g_sb = run_group("sb", lambda: next(g_mla))
                    g_fox = run_group("fox", lambda: next(g_sb))
                    next(g_fox)
                    for g_ in (g_fox, g_sb, g_mla):
                        next(g_, None)
                    P.barrier()
                    P.emit()

            if l + 1 < depth:
                wst_cur[0] = ExitStack()
                cur_w[0] = alloc_weights(wst_cur[0])
                wgen[0] = load_weights(l + 1, cur_w[0], ("pool",))
            if do_c:
                with ExitStack() as st:
                    wob = sbt(st, "wob", [128, 8, D], BF16)
                    stg = [sbt(st, f"stgc{i}", [128, 2048], F32) for i in range(2)]
                    ytl = [sbt(st, f"ytl{i}", [128, 8, 512], BF16) for i in range(2)]
                    xr = [sbt(st, f"xr{i}", [128, D], F32) for i in range(2)]
                    tt_ = [sbt(st, f"tt{i}", [128, D], F32) for i in range(2)]
                    xo = [sbt(st, f"xo{i}", [128, D], F32) for i in range(2)]
                    sqc = sbt(st, "sqc", [128, 512], BF16)
                    stc = sbt(st, "stc", [128, 8], F32)
                    wv = w_out[l].rearrange("(k p) c -> p k c", p=128)
                    for c4 in range(4):
                        sv = stg[c4 % 2][:, 0:2048].rearrange("p (k c) -> p k c", k=8)
                        P.dma(sv, wv[:, :, c4 * 256:(c4 + 1) * 256], writes=[f"stgc{c4 % 2}"])
                        CP(["pool", "dve"][c4 % 2], wob[:, :, c4 * 256:(c4 + 1) * 256], sv, [f"stgc{c4 % 2}"], ["wob"])
                    yv = YT.rearrange("(k p) t -> p k t", p=128)
                    crot = Rot([0, 1])
                    brot = Rot([0, 2, 4, 6])

                    def out_tile(ytile, ykey, coff, nr, xsrc, xkey, gg, dst, dkey):
                        b0 = brot.next()
                        for hf in range(2):
                            for k in range(8):
                                MM(ps[b0 + hf][0:nr, :], ytile[:, k, coff:coff + nr], wob[:, k, hf * 512:(hf + 1) * 512], k == 0, k == 7,
                                   [ykey, "wob"], [psk[b0 + hf]])
                        ci = crot.next()
                        for hf in range(2):
                            ACT(sqc[0:nr, :], ps[b0 + hf][0:nr, :], AF.Square, [psk[b0 + hf]], ["sqc"], accum=stc[0:nr, hf:hf + 1])
                        TT("dve", stc[0:nr, 2:3], stc[0:nr, 0:1], stc[0:nr, 1:2], ALU.add, ["sqc"], ["stc2"])
                        rstd_from_ssq(stc[0:nr, 4:5], stc[0:nr, 2:3], D, nr, ["stc2"], "stc4", stc[0:nr, 3:4])
                        for hf in range(2):
                            TT("dve", tt_[ci][0:nr, hf * 512:(hf + 1) * 512], ps[b0 + hf][0:nr, :], gg[0:nr, hf * 512:(hf + 1) * 512], ALU.mult,
                               [psk[b0 + hf], "ggP", "ggS"], [f"tt{ci}"])
                        STT(dst[0:nr, :], tt_[ci][0:nr, :], stc[0:nr, 4:5], xsrc[0:nr, :], ALU.mult, ALU.add, [f"tt{ci}", "stc4", xkey], [dkey])

                    for g in range(8):
                        yi = g % 2
                        P.dma(ytl[yi][:], yv[:, :, g * 512:(g + 1) * 512], reads=["YT"], writes=[f"ytl{yi}"])
                        for i in range(4):
                            t = g * 4 + i
                            xi = crot.i % 2
                            P.dma(xr[xi][:], xin[t * 128:(t + 1) * 128, :], reads=[f"x1_{t}"] if l > 0 else [], writes=[f"xr{xi}"])
                            out_tile(ytl[yi], f"ytl{yi}", i * 128, 128, xr[xi], f"xr{xi}", ggP, xo[xi], f"xo{xi}")
                            P.dma(xout[t * 128:(t + 1) * 128, :], xo[xi][:], reads=[f"xo{xi}"], writes=[f"x1_{t}"], queue="pool")
                            if wgen[0] is not None and t % 2 == 1:
                                next(wgen[0], None)
                    if wgen[0] is not None:
                        for _ in wgen[0]:
                            pass
                        wgen[0] = None
                    P.dma(ytl[0][:, :, 0:SS], yv[:, :, S:S + SS], reads=["YT"], writes=["ytl0"])
                    out_tile(ytl[0], "ytl0", 0, SS, xs, "xs", ggS, xs, "xs")
                    if l == depth - 1:
                        P.dma(y_s, xs[:], reads=["xs"])
                    P.barrier()
                    P.emit()

          except StopBuild:
            P.barrier()
            break
        P.finish()
        P.emit()
    return nc


_CACHE = {}


def _host_consts():
    c = np.zeros((128, 512 + 3 * 896), np.float32)
    idx = np.arange(128)
    c[:, 0:128] = np.eye(128, dtype=np.float32)
    c[:, 128:256] = (idx[:, None] <= idx[None, :]).astype(np.float32)
    c[:, 256:384] = 1.0
    c[:, 384:512] = -(idx[:, None] >= idx[None, :]).astype(np.float32)
    k = idx[:, None]
    q = idx[None, :]
    tri = [np.where(k > q, NEG, 0.0), np.where(k >= q, NEG, 0.0), np.where((k // 64) > (q // 64), NEG, 0.0)]
    for m in range(3):
        base = 512 + m * 896
        c[:, base:base + 384] = NEG
        c[:, base + 384:base + 512] = tri[m]
        c[:, base + 512:base + 896] = 0.0
    sel = np.zeros((2, 256), np.float32)
    sel[0, 0:128] = 1.0
    sel[1, 128:256] = 1.0
    half = 16
    inv = (10000.0 ** (-np.arange(half, dtype=np.float32) / half)).astype(np.float32)
    pos = np.arange(S + 128, dtype=np.float32)
    ang = pos[:, None] * inv[None, :]
    cs_tok = np.concatenate([np.cos(ang), np.sin(ang)], axis=1).astype(np.float32)
    cosT = np.cos(ang[:S]).T.astype(np.float32)
    sinT = np.sin(ang[:S]).T.astype(np.float32)
    ropeT = np.stack([np.concatenate([cosT, cosT], 0), np.concatenate([-sinT, sinT], 0)]).astype(np.float32)
    return c, sel, cs_tok, ropeT


def kernel(x_prompt, x_sample, c_prompt, c_sample, cache_fox_k, cache_fox_v, cache_fox_logf,
           cache_sb_k, cache_sb_v, cache_mla_ckv, cache_mla_kpe,
           g_pre, g_post, w_ada, b_ada, w_in, b_f, g_q_a, w_uq, g_kv_a, w_uk, w_uv, w_out):
    f = lambda a: np.ascontiguousarray(np.asarray(a, dtype=np.float32))
    if "nc" not in _CACHE:
        _CACHE["nc"] = build_program()
    nc = _CACHE["nc"]
    consts, sel, cs_tok, ropeT = _host_consts()
    perm = np.arange(768)
    for h in range(8):
        b0 = h * 96 + 64
        perm[b0:b0 + 16] = np.arange(b0 + 16, b0 + 32)
        perm[b0 + 16:b0 + 32] = np.arange(b0, b0 + 16)
    w_uq_f = f(w_uq)
    shared = {
        "g_pre": f(g_pre), "g_post": f(g_post), "w_ada": f(w_ada), "b_ada": f(b_ada), "w_in": f(w_in), "b_f": f(b_f),
        "g_q_a": f(g_q_a), "w_uq": w_uq_f, "w_uqs": np.ascontiguousarray(w_uq_f[:, :, perm]), "g_kv_a": f(g_kv_a),
        "w_uk": f(w_uk).reshape(DEPTH, 128, 512), "w_uv": f(w_uv).reshape(DEPTH, 128, 512), "w_out": f(w_out),
        "consts": consts, "selc": sel, "cs_tok": cs_tok, "ropeT": ropeT,
    }
    xp = f(x_prompt); xsm = f(x_sample); cp = f(c_prompt); csm = f(c_sample)
    cf = [f(a) for a in (cache_fox_k, cache_fox_v, cache_fox_logf, cache_sb_k, cache_sb_v, cache_mla_ckv, cache_mla_kpe)]
    in_maps = []
    for b in range(8):
        m = dict(shared)
        m["x_p"] = xp[b]
        m["x_s"] = xsm[b]
        m["c2"] = np.ascontiguousarray(np.stack([cp[b], csm[b]]))
        m["cfk"] = np.ascontiguousarray(cf[0][:, b].reshape(DEPTH, PAST, 256))
        m["cfv"] = np.ascontiguousarray(cf[1][:, b].reshape(DEPTH, PAST, 256))
        m["cfl"] = np.ascontiguousarray(cf[2][:, b])
        m["csk"] = np.ascontiguousarray(cf[3][:, b].reshape(DEPTH, PAST, 256))
        m["csv"] = np.ascontiguousarray(cf[4][:, b].reshape(DEPTH, PAST, 256))
        m["cck"] = np.ascontiguousarray(cf[5][:, b])
        m["ckp"] = np.ascontiguousarray(cf[6][:, b])
        in_maps.append(m)
    res = run_bass_kernel_spmd(nc, in_maps, core_ids=list(range(8)))
    R = res.results

    def gat(name, shape):
        return np.stack([np.asarray(R[b][name], dtype=np.float32) for b in range(8)], axis=1).reshape(shape)

    y_prompt = np.stack([np.asarray(R[b]["y_p"], dtype=np.float32) for b in range(8)])
    y_sample = np.stack([np.asarray(R[b]["y_s"], dtype=np.float32) for b in range(8)])
    outs = [y_prompt, y_sample]
    for pre, T in (("p", S), ("s", SS)):
        outs.append(gat(f"{pre}_fk", (DEPTH, 8, T, 4, 64)))
        outs.append(gat(f"{pre}_fv", (DEPTH, 8, T, 4, 64)))
        outs.append(gat(f"{pre}_fl", (DEPTH, 8, T, 4)))
        outs.append(gat(f"{pre}_sk", (DEPTH, 8, T, 4, 64)))
        outs.append(gat(f"{pre}_sv", (DEPTH, 8, T, 4, 64)))
        outs.append(gat(f"{pre}_ck", (DEPTH, 8, T, 128)))
        outs.append(gat(f"{pre}_kp", (DEPTH, 8, T, 32)))
    return tuple(outs)
```
